# Optimizing a Trainium2 kernel written in Bass

```python
import jax, jax.numpy as jnp
from jax import lax
import numpy as np

D_MODEL = 1024
BATCH = 4
SEQ = 4096
DEPTH = 4

HEAD_DIM = 64
N_MEM = 256
MEM_HEADS = 4
CROSS_WIDTH = MEM_HEADS * HEAD_DIM
MIX_WIDTH = 12 * HEAD_DIM
ATTN_WIDTH = MIX_WIDTH + CROSS_WIDTH
EPS = 1e-6
NEG = -1e30
MLA_HEADS = 12
MLA_Q_RANK = 384
MLA_KV_RANK = 256
MLA_NOPE = 64
MLA_ROPE = 32
MLA_V = 64
MLA_QK = MLA_NOPE + MLA_ROPE
ROPE_THETA = 10000.0
Q_BLOCK = 128
MLA_IN = MLA_Q_RANK + MLA_KV_RANK + MLA_ROPE + CROSS_WIDTH
SWA_Q_HEADS = 12
SWA_KV_HEADS = 4
SWA_GROUP = SWA_Q_HEADS // SWA_KV_HEADS
WINDOW = 128
SWA_IN = (SWA_Q_HEADS + 2 * SWA_KV_HEADS) * HEAD_DIM + CROSS_WIDTH
D_FF = 4 * D_MODEL
N_MLA_LAYERS = (DEPTH + 1) // 2
N_SWA_LAYERS = DEPTH // 2

kernel_name = "hybrid_mla_swa_sink_memx_sqrelu"


def rmsnorm(x, g):
    xf = x.astype(jnp.float32)
    y = xf * lax.rsqrt(jnp.mean(xf * xf, axis=-1, keepdims=True) + EPS)
    return (y * g.astype(jnp.float32)).astype(x.dtype)


def rope(x, positions):
    r = x.shape[-1]
    half = r // 2
    inv = ROPE_THETA ** (-(jnp.arange(half, dtype=jnp.float32) * 2.0) / r)
    ang = positions.astype(jnp.float32)[..., None] * inv
    cos = jnp.cos(ang)[:, :, None, :]
    sin = jnp.sin(ang)[:, :, None, :]
    xf = x.astype(jnp.float32)
    x1, x2 = xf[..., :half], xf[..., half:]
    out = jnp.concatenate([x1 * cos - x2 * sin, x1 * sin + x2 * cos], axis=-1)
    return out.astype(x.dtype)


def alibi_slopes(n_heads):
    return 2.0 ** (-8.0 * (jnp.arange(n_heads, dtype=jnp.float32) + 1.0) / n_heads)


def causal_dense_attention(q, k, v):
    b, s, h, dk = q.shape
    dv = v.shape[-1]
    nb = s // Q_BLOCK
    scale = dk ** -0.5
    qb = q.reshape(b, nb, Q_BLOCK, h, dk).transpose(1, 0, 2, 3, 4)
    k_idx = jnp.arange(s)

    def one_block(args):
        q_blk, n = args
        t_idx = n * Q_BLOCK + jnp.arange(Q_BLOCK)
        sc = jnp.einsum('bqhd,bkhd->bhqk', q_blk, k,
                        preferred_element_type=jnp.float32) * scale
        mask = k_idx[None, :] <= t_idx[:, None]
        sc = jnp.where(mask[None, None], sc, NEG)
        p = jax.nn.softmax(sc, axis=-1).astype(v.dtype)
        return jnp.einsum('bhqk,bkhd->bqhd', p, v)

    out = lax.map(one_block, (qb, jnp.arange(nb)))
    return out.transpose(1, 0, 2, 3, 4).reshape(b, s, h, dv)


def mla_mixer(hn, positions, w_in, q_norm_g, kv_norm_g, w_uq, w_ukv):
    b, s, _ = hn.shape
    proj = hn @ w_in
    c_q = proj[..., :MLA_Q_RANK]
    c_kv = proj[..., MLA_Q_RANK:MLA_Q_RANK + MLA_KV_RANK]
    k_r = proj[..., MLA_Q_RANK + MLA_KV_RANK:MLA_Q_RANK + MLA_KV_RANK + MLA_ROPE]
    q_cross = proj[..., MLA_Q_RANK + MLA_KV_RANK + MLA_ROPE:]
    q = (rmsnorm(c_q, q_norm_g) @ w_uq).reshape(b, s, MLA_HEADS, MLA_QK)
    q = jnp.concatenate([q[..., :MLA_NOPE], rope(q[..., MLA_NOPE:], positions)], axis=-1)
    kv = (rmsnorm(c_kv, kv_norm_g) @ w_ukv).reshape(b, s, MLA_HEADS, MLA_NOPE + MLA_V)
    k_nope, v = kv[..., :MLA_NOPE], kv[..., MLA_NOPE:]
    k_rope = jnp.broadcast_to(rope(k_r[:, :, None, :], positions),
                              (b, s, MLA_HEADS, MLA_ROPE))
    k = jnp.concatenate([k_nope, k_rope], axis=-1)
    o = causal_dense_attention(q, k, v)
    return o.reshape(b, s, MLA_HEADS * MLA_V), q_cross


def _with_prev_block(a):
    pad = [(0, 0), (1, 0)] + [(0, 0)] * (a.ndim - 2)
    prev = jnp.pad(a[:, :-1], pad)
    return jnp.concatenate([prev, a], axis=2)


def swa_mixer(hn, positions, w_in, sinks):
    b, s, _ = hn.shape
    nb = s // WINDOW
    proj = hn @ w_in
    nq = SWA_Q_HEADS * HEAD_DIM
    nk = SWA_KV_HEADS * HEAD_DIM
    q = proj[..., :nq].reshape(b, nb, WINDOW, SWA_KV_HEADS, SWA_GROUP, HEAD_DIM)
    k = proj[..., nq:nq + nk].reshape(b, nb, WINDOW, SWA_KV_HEADS, HEAD_DIM)
    v = proj[..., nq + nk:nq + 2 * nk].reshape(b, nb, WINDOW, SWA_KV_HEADS, HEAD_DIM)
    q_cross = proj[..., nq + 2 * nk:]
    kk = _with_prev_block(k)
    vv = _with_prev_block(v)
    pos_q = positions.reshape(b, nb, WINDOW)
    pos_k = _with_prev_block(pos_q)
    sc = jnp.einsum('bnqhgd,bnkhd->bnhgqk', q, kk,
                    preferred_element_type=jnp.float32) * (HEAD_DIM ** -0.5)
    dist = (pos_q[..., :, None] - pos_k[..., None, :]).astype(jnp.float32)
    slopes = alibi_slopes(SWA_Q_HEADS).reshape(SWA_KV_HEADS, SWA_GROUP)
    sc = sc - slopes[None, None, :, :, None, None] * dist[:, :, None, None]
    qi = jnp.arange(WINDOW)[:, None]
    kj = jnp.arange(2 * WINDOW)[None, :]
    rel = WINDOW + qi - kj
    band = (rel >= 0) & (rel < WINDOW)
    valid = band[None] & ((jnp.arange(nb)[:, None, None] > 0) | (kj >= WINDOW)[None])
    sc = jnp.where(valid[None, :, None, None], sc, NEG)
    sink = sinks.astype(jnp.float32).reshape(SWA_KV_HEADS, SWA_GROUP)[None, None, :, :, None, None]
    sink = jnp.broadcast_to(sink, sc.shape[:-1] + (1,))
    p = jax.nn.softmax(jnp.concatenate([sc, sink], axis=-1), axis=-1)[..., :-1]
    o = jnp.einsum('bnhgqk,bnkhd->bnqhgd', p.astype(vv.dtype), vv)
    return o.reshape(b, s, SWA_Q_HEADS * HEAD_DIM), q_cross


def memory_cross_attention(q_cross, mem_n, w_mem_kv):
    b, s, _ = q_cross.shape
    kv = (mem_n @ w_mem_kv).reshape(b, N_MEM, 2, MEM_HEADS, HEAD_DIM)
    k, v = kv[:, :, 0], kv[:, :, 1]
    q = q_cross.reshape(b, s, MEM_HEADS, HEAD_DIM)
    sc = jnp.einsum('bshd,bmhd->bhsm', q, k,
                    preferred_element_type=jnp.float32) * (HEAD_DIM ** -0.5)
    p = jax.nn.softmax(sc, axis=-1).astype(v.dtype)
    return jnp.einsum('bhsm,bmhd->bshd', p, v).reshape(b, s, CROSS_WIDTH)


def squared_relu_mlp(h, w_up, w_down):
    a = jax.nn.relu(h @ w_up)
    return (a * a) @ w_down


def setup_inputs(seed: int = 0) -> dict:
    key = jax.random.key(seed)
    ks = jax.random.split(key, 20)

    def w(k, shape, fan_in):
        return jax.random.normal(k, shape, jnp.float32) * (fan_in ** -0.5)

    def gain(k, shape):
        return 1.0 + 0.02 * jax.random.normal(k, shape, jnp.float32)

    x = jax.random.normal(ks[0], (BATCH, SEQ, D_MODEL), jnp.float32)
    mem = jax.random.normal(ks[1], (BATCH, N_MEM, D_MODEL), jnp.float32)
    offsets = jax.random.randint(ks[2], (BATCH, 1), 0, 1024, dtype=jnp.int32)
    positions = (offsets + jnp.arange(SEQ, dtype=jnp.int32)[None, :]).astype(jnp.int32)
    return {
        "x": x,
        "mem": mem,
        "positions": positions,
        "attn_norm_g": gain(ks[3], (DEPTH, D_MODEL)),
        "mlp_norm_g": gain(ks[4], (DEPTH, D_MODEL)),
        "mem_norm_g": gain(ks[5], (D_MODEL,)),
        "final_norm_g": gain(ks[6], (D_MODEL,)),
        "mla_w_in": w(ks[7], (N_MLA_LAYERS, D_MODEL, MLA_IN), D_MODEL),
        "mla_q_norm_g": gain(ks[8], (N_MLA_LAYERS, MLA_Q_RANK)),
        "mla_kv_norm_g": gain(ks[9], (N_MLA_LAYERS, MLA_KV_RANK)),
        "mla_w_uq": w(ks[10], (N_MLA_LAYERS, MLA_Q_RANK, MLA_HEADS * MLA_QK), MLA_Q_RANK),
        "mla_w_ukv": w(ks[11], (N_MLA_LAYERS, MLA_KV_RANK, MLA_HEADS * (MLA_NOPE + MLA_V)), MLA_KV_RANK),
        "swa_w_in": w(ks[12], (N_SWA_LAYERS, D_MODEL, SWA_IN), D_MODEL),
        "swa_sinks": 0.5 * jax.random.normal(ks[13], (N_SWA_LAYERS, SWA_Q_HEADS), jnp.float32),
        "w_mem_kv": w(ks[14], (DEPTH, D_MODEL, 2 * CROSS_WIDTH), D_MODEL),
        "w_o": w(ks[15], (DEPTH, ATTN_WIDTH, D_MODEL), ATTN_WIDTH),
        "mlp_w_up": w(ks[16], (DEPTH, D_MODEL, D_FF), D_MODEL),
        "mlp_w_down": w(ks[17], (DEPTH, D_FF, D_MODEL), D_FF),
    }


def reference(x, mem, positions, attn_norm_g, mlp_norm_g, mem_norm_g, final_norm_g,
              mla_w_in, mla_q_norm_g, mla_kv_norm_g, mla_w_uq, mla_w_ukv,
              swa_w_in, swa_sinks, w_mem_kv, w_o, mlp_w_up, mlp_w_down):
    mem_n = rmsnorm(mem, mem_norm_g)
    for i in range(DEPTH):
        j = i // 2
        hn = rmsnorm(x, attn_norm_g[i])
        if i % 2 == 0:
            mix, q_cross = mla_mixer(hn, positions, mla_w_in[j], mla_q_norm_g[j],
                                     mla_kv_norm_g[j], mla_w_uq[j], mla_w_ukv[j])
        else:
            mix, q_cross = swa_mixer(hn, positions, swa_w_in[j], swa_sinks[j])
        cross = memory_cross_attention(q_cross, mem_n, w_mem_kv[i])
        x = x + jnp.concatenate([mix, cross], axis=-1) @ w_o[i]
        x = x + squared_relu_mlp(rmsnorm(x, mlp_norm_g[i]), mlp_w_up[i], mlp_w_down[i])
    return rmsnorm(x, final_norm_g)
```

```python
import numpy as np
from contextlib import ExitStack
import concourse.bass as bass
import concourse.mybir as mybir
from concourse.bass_utils import run_bass_kernel_spmd

F32, BF16, I32 = mybir.dt.float32, mybir.dt.bfloat16, mybir.dt.int32
ALU = mybir.AluOpType
AF = mybir.ActivationFunctionType

D = 1024
NT = 2048
TB = 512
NB = NT // TB
DFF = 4096
EPS = 1e-6
NEGB = -30000.0
N_MEM = 256


class KB:
    def __init__(self):
        self.nc = bass.Bass("TRN2", target_bir_lowering=False)
        nc = self.nc
        self.E = dict(pe=nc.tensor, act=nc.scalar, dve=nc.vector, pool=nc.gpsimd, sp=nc.sync)
        self.csem = {e: nc.alloc_semaphore("s_" + e) for e in ("pe", "act", "dve", "pool")}
        self.cnt = {e: 0 for e in self.csem}
        self.waited = {e: {} for e in self.E}
        self.trk = {}
        self.W = {}
        self.dsem = {}
        self.dcnt = {}
        self.semh = {}
        for e, h in self.csem.items():
            self.semh["c_" + e] = h
        self.es = ExitStack()
        self.n_ins = 0

    def sb(self, name, shape, dtype):
        t = self.es.enter_context(self.nc.sbuf_tensor(name, list(shape), dtype))
        self.W[name] = int(np.prod(shape[1:]))
        self.trk[name] = []
        return t

    def ps(self, name, shape, dtype=F32):
        t = self.es.enter_context(self.nc.psum_tensor(name, list(shape), dtype))
        self.W[name] = int(np.prod(shape[1:]))
        self.trk[name] = []
        return t

    def push_scope(self):
        self._scopes = getattr(self, "_scopes", [])
        self._scopes.append(self.es)
        self.es = ExitStack()

    def pop_scope(self):
        self.barrier()
        self.es.close()
        self.es = self._scopes.pop()

    def barrier(self):
        toks = [("c_" + e, c) for e, c in self.cnt.items() if c > 0]
        toks += [("d_" + s, c * 16) for s, c in self.dcnt.items() if c > 0]
        for e in self.E:
            self._wait(e, toks)

    def dram_in(self, name, shape, dtype=F32):
        return self.nc.dram_tensor(name, list(shape), dtype, kind="ExternalInput").ap()

    def dram_out(self, name, shape, dtype=F32):
        return self.nc.dram_tensor(name, list(shape), dtype, kind="ExternalOutput").ap()

    def dma_sem(self, name):
        if name not in self.dsem:
            self.dsem[name] = self.nc.alloc_semaphore("d_" + name)
            self.dcnt[name] = 0
            self.semh["d_" + name] = self.dsem[name]
        return name

    def _region(self, ap):
        name = ap.tensor.name
        if name not in self.W:
            return None
        W = self.W[name]
        off = int(ap.offset)
        a = ap.ap
        p0 = off // W
        f0 = off % W
        p1 = p0 + a[0][1]
        hi = f0 + sum((c - 1) * s for s, c in a[1:]) + 1
        return name, p0, p1, f0, hi

    def _deps(self, reads, writes):
        toks = set()
        for ap in reads:
            r = self._region(ap)
            if r is None:
                continue
            name, p0, p1, lo, hi = r
            for e in self.trk[name]:
                if e[4] and e[0] < p1 and p0 < e[1] and e[2] < hi and lo < e[3]:
                    toks.add(e[5])
        for ap in writes:
            r = self._region(ap)
            if r is None:
                continue
            name, p0, p1, lo, hi = r
            for e in self.trk[name]:
                if e[0] < p1 and p0 < e[1] and e[2] < hi and lo < e[3]:
                    toks.add(e[5])
        return toks

    def _record(self, reads, writes, tok):
        for ap in writes:
            r = self._region(ap)
            if r is None:
                continue
            name, p0, p1, lo, hi = r
            lst = self.trk[name]
            lst[:] = [e for e in lst if not (p0 <= e[0] and e[1] <= p1 and lo <= e[2] and e[3] <= hi)]
            lst.append([p0, p1, lo, hi, True, tok])
        for ap in reads:
            r = self._region(ap)
            if r is None:
                continue
            name, p0, p1, lo, hi = r
            lst = self.trk[name]
            for e in lst:
                if (not e[4]) and e[0] == p0 and e[1] == p1 and e[2] == lo and e[3] == hi and e[5][0] == tok[0]:
                    e[5] = tok
                    break
            else:
                lst.append([p0, p1, lo, hi, False, tok])

    def _wait(self, eng, toks):
        best = {}
        for s, v in toks:
            if s.startswith("d_"):
                v = max(v, self.dcnt[s[2:]] * 16)
            if v > best.get(s, 0):
                best[s] = v
        for s, v in best.items():
            if eng == "pe" and s == "c_pe":
                continue
            if self.waited[eng].get(s, 0) >= v:
                continue
            self.E[eng].wait_ge(self.semh[s], v)
            self.waited[eng][s] = v
            self.n_ins += 1

    def op(self, eng, fn, reads=(), writes=()):
        toks = self._deps(reads, writes)
        self._wait(eng, toks)
        ins = fn()
        self.cnt[eng] += 1
        tok = ("c_" + eng, self.cnt[eng])
        ins.then_inc(self.csem[eng], 1)
        self._record(reads, writes, tok)
        self.n_ins += 1
        return tok

    def dma(self, q, out, in_, sem):
        self.dma_sem(sem)
        toks = self._deps([in_], [out])
        self._wait(q, toks)
        ins = self.E[q].dma_start(out=out, in_=in_)
        self.dcnt[sem] += 1
        tok = ("d_" + sem, self.dcnt[sem] * 16)
        ins.then_inc(self.dsem[sem], 16)
        self._record([in_], [out], tok)
        self.n_ins += 1
        return tok

    def wait_all_dma(self, eng):
        toks = [("d_" + s, c * 16) for s, c in self.dcnt.items() if c > 0]
        self._wait(eng, toks)

    def mm(self, out, lhsT, rhs, start=True, stop=True, extra_reads=()):
        return self.op("pe", lambda: self.nc.tensor.matmul(out, lhsT, rhs, start=start, stop=stop,
                                                          skip_group_check=True),
                       reads=[lhsT, rhs, *extra_reads], writes=[out])

    def tr(self, out, in_, ident):
        return self.op("pe", lambda: self.nc.tensor.transpose(out, in_, ident),
                       reads=[in_, ident], writes=[out])

    def act(self, out, in_, func, bias=None, scale=1.0, eng="act"):
        reads = [in_]
        kw = {}
        if bias is not None:
            kw["bias"] = bias
            if not isinstance(bias, (int, float)):
                reads.append(bias)
        if not isinstance(scale, (int, float)):
            reads.append(scale)
        return self.op("act", lambda: self.nc.scalar.activation(out=out, in_=in_, func=func, scale=scale, **kw),
                       reads=reads, writes=[out])

    def copy(self, eng, out, in_):
        if eng == "act":
            return self.op("act", lambda: self.nc.scalar.copy(out=out, in_=in_), reads=[in_], writes=[out])
        return self.op(eng, lambda: self.E[eng].tensor_copy(out=out, in_=in_), reads=[in_], writes=[out])

    def tt(self, eng, out, in0, in1, op):
        return self.op(eng, lambda: self.E[eng].tensor_tensor(out=out, in0=in0, in1=in1, op=op),
                       reads=[in0, in1], writes=[out])

    def ts(self, eng, out, in0, s1, op0, s2=None, op1=None):
        reads = [in0] + [s for s in (s1, s2) if s is not None and not isinstance(s, (int, float))]
        if op1 is None:
            return self.op(eng, lambda: self.E[eng].tensor_scalar(out=out, in0=in0, scalar1=s1, scalar2=None, op0=op0),
                           reads=reads, writes=[out])
        return self.op(eng, lambda: self.E[eng].tensor_scalar(out=out, in0=in0, scalar1=s1, scalar2=s2, op0=op0, op1=op1),
                       reads=reads, writes=[out])

    def stt(self, eng, out, in0, scalar, in1, op0, op1):
        reads = [in0, in1] + ([] if isinstance(scalar, (int, float)) else [scalar])
        return self.op(eng, lambda: self.E[eng].scalar_tensor_tensor(out=out, in0=in0, scalar=scalar, in1=in1,
                                                                    op0=op0, op1=op1),
                       reads=reads, writes=[out])

    def memset(self, eng, out, val):
        return self.op(eng, lambda: self.E[eng].memset(out, val), reads=[], writes=[out])

    def finish(self):
        self.wait_all_dma("sp")
        self.es.close()


class Common:
    def __init__(self, kb, consts_dram, nbuf=2):
        self.kb = kb
        k = kb
        self.PS = [k.ps(f"ps{i}", [128, 1024]) for i in range(4)]
        self.rr = 0
        self.identf = k.sb("identf", [128, 128], F32)
        self.identb = k.sb("identb", [128, 128], BF16)
        self.onesb = k.sb("onesb", [128, 128], BF16)
        self.onesf = k.sb("onesf", [128, 128], F32)
        self.zcol = k.sb("zcol", [128, 1], F32)
        self.epscol = k.sb("epscol", [128, 1], F32)
        k.dma("sp", self.identf[:], consts_dram["ident"], "const")
        k.dma("pool", self.identb[:], consts_dram["ident"], "constp")
        k.memset("dve", self.onesb[:], 1.0)
        k.memset("dve", self.onesf[:], 1.0)
        k.memset("dve", self.zcol[:], 0.0)
        k.memset("dve", self.epscol[:], EPS)
        self.nbuf = nbuf
        self.sq = [k.sb(f"sq{i}", [128, 8, TB], BF16) for i in range(nbuf)]
        self.lnv = [k.sb(f"lnv{i}", [128, TB], F32) for i in range(nbuf)]
        self.rstd = [k.sb(f"rstd{i}", [128, TB], F32) for i in range(nbuf)]
        self.nrm_i = 0

    def bank(self, b, lo=0, hi=512, p0=0, p1=128):
        return self.PS[b // 2][p0:p1, (b % 2) * 512 + lo:(b % 2) * 512 + hi]

    def next_bank(self, banks=(0, 1, 2, 3, 4, 5, 6, 7)):
        b = banks[self.rr % len(banks)]
        self.rr += 1
        return b

    def rms_stats(self, chunks, w, inv_n, banks=(0, 1, 2, 3, 4, 5, 6, 7), sq_eng="pool"):
        if sq_eng == "act":
            return self._rms_stats_act(chunks, w, inv_n, banks)
        return self._rms_stats(chunks, w, inv_n, banks, sq_eng)

    def _rms_stats_act(self, chunks, w, inv_n, banks):
        k = self.kb
        i = self.nrm_i % self.nbuf
        self.nrm_i += 1
        sq, lnv, rstd = self.sq[i], self.lnv[i], self.rstd[i]
        b = self.next_bank(banks)
        n = len(chunks)
        for c, ap in enumerate(chunks):
            k.act(sq[:, c, 0:w], ap, AF.Square)
        for c in range(n):
            k.mm(self.bank(b, 0, w), self.onesb[:, :], sq[:, c, 0:w], start=(c == 0), stop=(c == n - 1))
        k.act(lnv[:, 0:w], self.bank(b, 0, w), AF.Ln, bias=self.epscol[:, 0:1], scale=inv_n)
        k.act(rstd[:, 0:w], lnv[:, 0:w], AF.Exp, scale=-0.5)
        return rstd[:, 0:w]

    def _rms_stats(self, chunks, w, inv_n, banks=(0, 1, 2, 3, 4, 5, 6, 7), sq_eng="pool"):
        k = self.kb
        i = self.nrm_i % self.nbuf
        self.nrm_i += 1
        sq, lnv, rstd = self.sq[i], self.lnv[i], self.rstd[i]
        b = self.next_bank(banks)
        n = len(chunks)
        for c, ap in enumerate(chunks):
            k.tt(sq_eng, sq[:, c, 0:w], ap, ap, ALU.mult)
        for c in range(n):
            k.mm(self.bank(b, 0, w), self.onesb[:, :], sq[:, c, 0:w], start=(c == 0), stop=(c == n - 1))
        k.act(lnv[:, 0:w], self.bank(b, 0, w), AF.Ln, bias=self.epscol[:, 0:1], scale=inv_n)
        k.act(rstd[:, 0:w], lnv[:, 0:w], AF.Exp, scale=-0.5)
        return rstd[:, 0:w]

    def load_xT_block(self, x_rows, dstT, t0, ntiles, stage, sem_prefix, dst_chunks=8):
        k = self.kb
        for i in range(ntiles):
            st = stage[i % len(stage)]
            k.dma("sp", st[:], x_rows[i * 128:(i + 1) * 128, :], f"{sem_prefix}{i % len(stage)}")
            for h in range(2):
                b = self.next_bank()
                for cc in range(4):
                    c = h * 4 + cc
                    k.tr(self.bank(b, cc * 128, cc * 128 + 128), st[:, c * 128:(c + 1) * 128], self.identf[:])
                src = self.bank(b).rearrange("p (c t) -> p c t", c=4)
                dst = dstT[:, h * 4:h * 4 + 4, t0 + i * 128:t0 + (i + 1) * 128]
                k.copy("act" if (i + h) % 2 == 0 else "dve", dst, src)

    def store_xT_block(self, srcT, t0, ntiles, out_rows, stage, sem_prefix):
        k = self.kb
        for i in range(ntiles):
            st = stage[i % len(stage)]
            for h in range(2):
                b = self.next_bank()
                for cc in range(4):
                    c = h * 4 + cc
                    k.tr(self.bank(b, cc * 128, cc * 128 + 128), srcT[:, c, t0 + i * 128:t0 + (i + 1) * 128],
                         self.identf[:])
                k.copy("act" if (i + h) % 2 == 0 else "dve", st[:, h * 512:(h + 1) * 512], self.bank(b))
            k.dma("sp", out_rows[i * 128:(i + 1) * 128, :], st[:], f"{sem_prefix}{i % len(stage)}")

    def norm_block(self, xT, t0, w, gcols, outT, o0, out_dtype_bf16=True):
        k = self.kb
        rstd = self.rms_stats([xT[:, c, t0:t0 + w] for c in range(8)], w, 1.0 / D)
        for c in range(8):
            k.stt("dve", outT[:, c, o0:o0 + w], xT[:, c, t0:t0 + w], gcols[:, c:c + 1], rstd, ALU.mult, ALU.mult)


def build_mlp_program(final_norm):
    k = KB()
    x_in = k.dram_in("x", [NT, D])
    g_in = k.dram_in("g", [128, 16])
    up_in = k.dram_in("w_up", [D, DFF])
    dn_in = k.dram_in("w_down", [DFF, D])
    ident_in = k.dram_in("ident", [128, 128])
    y_out = k.dram_out("y", [NT, D])
    cm = Common(k, {"ident": ident_in}, nbuf=1)
    gt = k.sb("gt", [128, 16], F32)
    k.dma("sp", gt[:], g_in, "const")
    wup = k.sb("wup", [128, 8, DFF], BF16)
    wdn = k.sb("wdn", [128, 32, D], BF16)
    upv = up_in.rearrange("(kc p) f -> p kc f", p=128)
    dnv = dn_in.rearrange("(fc p) d -> p fc d", p=128)
    for fb in range(8):
        k.dma("pool", wup[:, :, fb * 512:(fb + 1) * 512], upv[:, :, fb * 512:(fb + 1) * 512], f"wup{fb}")
        k.dma("pool", wdn[:, fb * 4:(fb + 1) * 4, :], dnv[:, fb * 4:(fb + 1) * 4, :], f"wdn{fb}")
    xT = [k.sb(f"xT{i}", [128, 8, TB], F32) for i in range(1)]
    hT = [k.sb(f"hT{i}", [128, 8, TB], BF16) for i in range(1)]
    stage = [k.sb(f"stg{i}", [128, D], F32) for i in range(2)]
    ostage = [k.sb(f"ostg{i}", [128, D], F32) for i in range(1)]
    rl = [k.sb(f"rl{i}", [128, TB], F32) for i in range(2)]
    aT = [k.sb(f"aT{i}", [128, 4, TB], BF16) for i in range(2)]
    ri = 0
    for j in range(NB):
        X = xT[0]
        H = hT[0]
        cm.load_xT_block(x_in[j * TB:(j + 1) * TB, :], X, 0, 4, stage, "xs")
        cm.norm_block(X, 0, TB, gt[:, 0:8], H, 0)
        for fb in range(8):
            A = aT[fb % 2]
            for q in range(4):
                b = cm.next_bank((0, 1, 2, 3))
                f0 = fb * 512 + q * 128
                for kc in range(8):
                    k.mm(cm.bank(b), wup[:, kc, f0:f0 + 128], H[:, kc, :], start=(kc == 0), stop=(kc == 7))
                r = rl[ri % 2]
                ri += 1
                k.act(r[:], cm.bank(b), AF.Relu)
                k.tt("pool", A[:, q, :], r[:], r[:], ALU.mult)
            for c in range(8):
                b = cm.next_bank((4, 5, 6, 7))
                for q in range(4):
                    k.mm(cm.bank(b), wdn[:, fb * 4 + q, c * 128:(c + 1) * 128], A[:, q, :], start=(q == 0), stop=(q == 3))
                k.tt("dve", X[:, c, :], X[:, c, :], cm.bank(b), ALU.add)
        if final_norm:
            rstd = cm.rms_stats([X[:, c, :] for c in range(8)], TB, 1.0 / D)
            for c in range(8):
                k.stt("dve", X[:, c, :], X[:, c, :], gt[:, 8 + c:9 + c], rstd, ALU.mult, ALU.mult)
        cm.store_xT_block(X, 0, 4, y_out[j * TB:(j + 1) * TB, :], ostage, "os")
    k.finish()
    return k


class Attn:
    def __init__(self, cm, npt=4):
        k = cm.kb
        self.cm = cm
        self.PT = [k.sb(f"pt{i}", [128, TB], BF16) for i in range(npt)]
        self.rsum = k.sb("rsum", [128, TB], F32)
        self.bc = k.sb("bcs", [128, TB], F32)
        self.pi = 0
        self.si = 0
        self.oi = 0

    def unit(self, tiles, qfn, scale, par, dst, extra_sum=None):
        cm, k = self.cm, self.cm.kb
        nc = k.nc
        ob = 4 + (self.oi % 2)
        self.oi += 1
        n = len(tiles)
        for ti, t in enumerate(tiles):
            c0, c1 = t["c0"], t["c1"]
            sb_ = self.si % 4
            self.si += 1
            masks = t.get("masks", ())
            k.mm(cm.bank(sb_, c0, c1), t["K"], qfn(c0, c1), start=True, stop=(len(masks) == 0))
            for mi, (m0, m1, mrhs) in enumerate(masks):
                k.mm(cm.bank(sb_, m0, m1), cm.identb[:, :], mrhs, start=False, stop=(mi == len(masks) - 1))
            pt = self.PT[self.pi % len(self.PT)]
            self.pi += 1
            k.act(pt[:, c0:c1], cm.bank(sb_, c0, c1), AF.Exp, bias=t["bias"], scale=scale)
            k.mm(cm.bank(ob, c0, c1), t["V"], pt[:, c0:c1], start=(ti == 0), stop=(ti == n - 1))
        sr = 64 if par == 0 else 0
        r0 = 0 if par == 0 else 64
        rs = self.rsum[sr:sr + 1, :]
        if extra_sum is not None:
            k.ts("dve", rs, cm.bank(ob, 0, TB, sr, sr + 1), extra_sum, ALU.add)
            k.op("dve", lambda: nc.vector.reciprocal(out=rs, in_=rs), reads=[rs], writes=[rs])
        else:
            src = cm.bank(ob, 0, TB, sr, sr + 1)
            k.op("dve", lambda: nc.vector.reciprocal(out=rs, in_=src), reads=[src], writes=[rs])
        k.mm(cm.bank(6), cm.onesf[sr:sr + 1, 0:128], rs, start=True, stop=True)
        k.copy("act", self.bc[r0:r0 + 64, :], cm.bank(6, 0, TB, r0, r0 + 64))
        k.tt("dve", dst, cm.bank(ob, 0, TB, r0, r0 + 64), self.bc[r0:r0 + 64, :], ALU.mult)


PROJ_BANKS = (0, 1, 2, 3, 7)


def proj_chunk(cm, W, c0, M, HT, h0, w, dst, eng, nk=8):
    k = cm.kb
    b = cm.next_bank(PROJ_BANKS)
    for kc in range(nk):
        k.mm(cm.bank(b, 0, w, 0, M), W[:, kc, c0:c0 + M], HT[:, kc, h0:h0 + w], start=(kc == 0), stop=(kc == nk - 1))
    if dst is not None:
        k.copy(eng, dst, cm.bank(b, 0, w, 0, M))
    return b


def mem_setup(cm, mem_in, gt_mem, Wm, stage):
    k = cm.kb
    memT = k.sb("memT", [128, 8, N_MEM], F32)
    MEMN = k.sb("memn", [128, 8, N_MEM], BF16)
    Kmem = k.sb("kmem", [128, 2, N_MEM], BF16)
    Vmem = k.sb("vmem", [128, 2, 4, 192], BF16)
    cm.load_xT_block(mem_in, memT, 0, 2, stage, "xs")
    cm.norm_block(memT, 0, N_MEM, gt_mem, MEMN, 0)
    k.memset("pool", Vmem[:], 1.0)
    for pr in range(2):
        proj_chunk(cm, Wm, pr * 128, 128, MEMN, 0, N_MEM, Kmem[:, pr, :], "act")
    for t in range(2):
        b = cm.next_bank(PROJ_BANKS)
        for kc in range(8):
            k.mm(cm.bank(b, 0, 256), MEMN[:, kc, t * 128:(t + 1) * 128], Wm[:, kc, 256:512], start=(kc == 0), stop=(kc == 7))
        k.copy("dve", Vmem[:, t, :, 64:128], cm.bank(b, 0, 256).rearrange("p (h d) -> p h d", h=4))
    return Kmem, Vmem


def cross_units(cm, at, Kmem, Vmem, QC, q0, AO, a0):
    for ch in range(4):
        par = ch % 2
        r0 = par * 64
        tiles = []
        for t in range(2):
            V = Vmem[:, t, ch, 64:192] if par == 0 else Vmem[:, t, ch, 0:128]
            tiles.append(dict(K=Kmem[r0:r0 + 64, ch // 2, t * 128:(t + 1) * 128], V=V, c0=0, c1=TB,
                              bias=cm.zcol[:, 0:1]))
        at.unit(tiles, lambda c0, c1, ch=ch, r0=r0: QC[r0:r0 + 64, ch // 2, q0 + c0:q0 + c1], 0.125, par,
                AO[r0:r0 + 64, 6 + ch // 2, a0:a0 + TB])


def out_proj_block(cm, Wo, AO, a0, X, x0):
    k = cm.kb
    for co in range(8):
        b = cm.next_bank(PROJ_BANKS)
        for kc in range(8):
            k.mm(cm.bank(b), Wo[:, kc, co * 128:(co + 1) * 128], AO[:, kc, a0:a0 + TB], start=(kc == 0), stop=(kc == 7))
        k.tt("dve", X[:, co, x0:x0 + TB], X[:, co, x0:x0 + TB], cm.bank(b), ALU.add)


SWA_POS = [0, 3, 1, 4, 2, 5, 6, 9, 7, 10, 8, 11]


def build_swa_program():
    k = KB()
    x_in = k.dram_in("x", [NT, D])
    xp_in = k.dram_in("xprev", [128, D])
    mem_in = k.dram_in("mem", [N_MEM, D])
    g_in = k.dram_in("g", [128, 16])
    wi_in = k.dram_in("w_in", [D, 1536])
    wm_in = k.dram_in("w_mem", [D, 512])
    wo_in = k.dram_in("w_o", [D, D])
    sk_in = k.dram_in("sinks", [128, 12])
    bmh_in = k.dram_in("bmhi", [128, 12, 256])
    bml_in = k.dram_in("bmlo", [128, 12, 256])
    pb_in = k.dram_in("prevbias", [128, 1])
    ident_in = k.dram_in("ident", [128, 128])
    y_out = k.dram_out("y", [NT, D])
    cm = Common(k, {"ident": ident_in}, nbuf=1)
    at = Attn(cm)
    gt = k.sb("gt", [128, 16], F32)
    sk = k.sb("sk", [128, 12], F32)
    pb = k.sb("pb", [128, 1], F32)
    k.dma("sp", gt[:], g_in, "const")
    k.dma("sp", sk[:], sk_in, "const")
    k.dma("sp", pb[:], pb_in, "const")
    Wi = k.sb("wi", [128, 8, 1536], BF16)
    Wm = k.sb("wm", [128, 8, 512], BF16)
    Wo = k.sb("wo", [128, 8, D], BF16)
    BMh = k.sb("bmh", [128, 12, 256], BF16)
    BMl = k.sb("bml", [128, 12, 256], BF16)
    k.dma("pool", Wm[:], wm_in.rearrange("(kc p) f -> p kc f", p=128), "w0")
    k.dma("pool", Wi[:], wi_in.rearrange("(kc p) f -> p kc f", p=128), "w1")
    k.dma("pool", BMh[:], bmh_in, "w2")
    k.dma("pool", BMl[:], bml_in, "w3")
    k.dma("pool", Wo[:], wo_in.rearrange("(kc p) f -> p kc f", p=128), "w4")
    k.act(sk[:], sk[:], AF.Exp)
    stage = [k.sb(f"stg{i}", [128, D], F32) for i in range(2)]
    ostage = [k.sb("ostg0", [128, D], F32)]
    Kmem, Vmem = mem_setup(cm, mem_in, gt[:, 8:16], Wm, stage)
    KS = k.sb("ks", [128, 2, NT + 128], BF16)
    VS = k.sb("vs", [128, 17, 4, 192], BF16)
    QS = k.sb("qs", [128, 6, TB], BF16)
    QC = k.sb("qc", [128, 2, TB], BF16)
    AO = k.sb("ao", [128, 8, TB], BF16)
    X = k.sb("xblk", [128, 8, TB], F32)
    HT = k.sb("ht", [128, 8, TB], BF16)
    XB = k.sb("xb", [128, 8, 128], F32)
    k.memset("pool", VS[:], 1.0)
    cm.load_xT_block(xp_in, XB, 0, 1, stage, "xs")
    cm.norm_block(XB, 0, 128, gt[:, 0:8], HT, 0)
    for c in range(2):
        proj_chunk(cm, Wi, 768 + c * 128, 128, HT, 0, 128, KS[:, c, 0:128], "act")
    b = cm.next_bank(PROJ_BANKS)
    for kc in range(8):
        k.mm(cm.bank(b, 0, 256), HT[:, kc, 0:128], Wi[:, kc, 1024:1280], start=(kc == 0), stop=(kc == 7))
    k.copy("dve", VS[:, 0, :, 64:128], cm.bank(b, 0, 256).rearrange("p (h d) -> p h d", h=4))
    for j in range(NB):
        cm.load_xT_block(x_in[j * TB:(j + 1) * TB, :], X, 0, 4, stage, "xs")
        cm.norm_block(X, 0, TB, gt[:, 0:8], HT, 0)
        for c in range(6):
            proj_chunk(cm, Wi, c * 128, 128, HT, 0, TB, QS[:, c, :], "act" if c % 2 else "dve")
        for c in range(2):
            proj_chunk(cm, Wi, 768 + c * 128, 128, HT, 0, TB, KS[:, c, 128 + j * TB:128 + (j + 1) * TB], "act")
        for c in range(2):
            proj_chunk(cm, Wi, 1280 + c * 128, 128, HT, 0, TB, QC[:, c, :], "dve")
        for t in range(4):
            b = cm.next_bank(PROJ_BANKS)
            for kc in range(8):
                k.mm(cm.bank(b, 0, 256), HT[:, kc, t * 128:(t + 1) * 128], Wi[:, kc, 1024:1280], start=(kc == 0), stop=(kc == 7))
            k.copy("dve" if t % 2 else "act", VS[:, 1 + 4 * j + t, :, 64:128],
                   cm.bank(b, 0, 256).rearrange("p (h d) -> p h d", h=4))
        cross_units(cm, at, Kmem, Vmem, QC, 0, AO, 0)
        for p in range(12):
            kh = SWA_POS[p] // 3
            par = p % 2
            r0 = par * 64
            tiles = []
            for i in range(5):
                s = 4 * j + i
                c0 = max(0, (i - 1) * 128)
                c1 = min(TB, (i + 1) * 128)
                m0 = 128 if i == 0 else 0
                V = VS[:, s, kh, 64:192] if par == 0 else VS[:, s, kh, 0:128]
                tiles.append(dict(K=KS[r0:r0 + 64, kh // 2, s * 128:(s + 1) * 128], V=V, c0=c0, c1=c1,
                                  bias=(pb[:, 0:1] if s == 0 else cm.zcol[:, 0:1]),
                                  masks=[(c0, c1, BMh[:, p, m0:m0 + (c1 - c0)]), (c0, c1, BMl[:, p, m0:m0 + (c1 - c0)])]))
            sr = 64 if par == 0 else 0
            at.unit(tiles, lambda c0, c1, p=p, r0=r0: QS[r0:r0 + 64, p // 2, c0:c1], 0.125, par,
                    AO[r0:r0 + 64, p // 2, :], extra_sum=sk[sr:sr + 1, p:p + 1])
        out_proj_block(cm, Wo, AO, 0, X, 0)
        cm.store_xT_block(X, 0, 4, y_out[j * TB:(j + 1) * TB, :], ostage, "os")
    k.finish()
    return k


def alibi_tables():
    slopes = 2.0 ** (-8.0 * (np.arange(12, dtype=np.float64) + 1.0) / 12)
    kk = np.arange(128)[:, None]
    c = np.arange(256)[None, :]
    d = (c - kk).astype(np.float64)
    valid = (d >= 0) & (d < 128)
    out = np.zeros((128, 12, 256), np.float64)
    for p in range(12):
        out[:, p, :] = np.where(valid, -8.0 * slopes[SWA_POS[p]] * d, 8.0 * NEGB)
    import ml_dtypes
    hi = out.astype(np.float32).astype(ml_dtypes.bfloat16).astype(np.float32)
    lo = (out - hi).astype(np.float32).astype(ml_dtypes.bfloat16).astype(np.float32)
    return hi, lo


def swa_host_inputs(x_own, x_prev_last, mem, g_attn, g_mem, w_in, w_mem, w_o, sinks, prevbias):
    wq = np.concatenate([w_in[:, h * 64:(h + 1) * 64] for h in SWA_POS], axis=1)
    wi = np.concatenate([wq, w_in[:, 768:]], axis=1)
    wo = np.concatenate([w_o[h * 64:(h + 1) * 64] for h in SWA_POS] + [w_o[768:]], axis=0)
    g = np.concatenate([g_attn.reshape(8, 128).T, g_mem.reshape(8, 128).T], axis=1)
    sk = np.broadcast_to(sinks[SWA_POS][None, :], (128, 12))
    hi, lo = alibi_tables()
    f = lambda a: np.ascontiguousarray(a, dtype=np.float32)
    return {"x": f(x_own), "xprev": f(x_prev_last), "mem": f(mem), "g": f(g), "w_in": f(wi), "w_mem": f(w_mem),
            "w_o": f(wo), "sinks": f(sk), "bmhi": f(hi), "bmlo": f(lo),
            "prevbias": np.full((128, 1), prevbias, np.float32), "ident": np.eye(128, dtype=np.float32)}


MLA_SCALE = 96.0 ** -0.5
NKEY = 2 * NT


def build_mla_program():
    k = KB()
    nc = k.nc
    x_in = k.dram_in("x", [NT, D])
    xp_in = k.dram_in("xprev", [NT, D])
    mem_in = k.dram_in("mem", [N_MEM, D])
    g_in = k.dram_in("g", [128, 24])
    pos_in = k.dram_in("pos", [NT], I32)
    posp_in = k.dram_in("posprev", [NT], I32)
    rc_in = k.dram_in("ropec", [128, 4])
    wi_in = k.dram_in("w_in", [D, 1088])
    wuq_in = k.dram_in("w_uq", [384, 12 * 192])
    wukv_in = k.dram_in("w_ukv", [256, 1536])
    wm_in = k.dram_in("w_mem", [D, 512])
    wo_in = k.dram_in("w_o", [D, D])
    tri_in = k.dram_in("trimask", [128, 128])
    pb_in = k.dram_in("prevbias", [128, 1])
    ident_in = k.dram_in("ident", [128, 128])
    y_out = k.dram_out("y", [NT, D])
    cm = Common(k, {"ident": ident_in}, nbuf=1)
    at = Attn(cm)
    gt = k.sb("gt", [128, 24], F32)
    rc = k.sb("rc", [128, 4], F32)
    pb = k.sb("pb", [128, 1], F32)
    tri = k.sb("tri", [128, 128], BF16)
    k.dma("sp", gt[:], g_in, "const")
    k.dma("sp", rc[:], rc_in, "const")
    k.dma("sp", pb[:], pb_in, "const")
    k.dma("pool", tri[:], tri_in, "constp")
    Wi = k.sb("wi", [128, 8, 1088], BF16)
    Wukv = k.sb("wukv", [128, 2, 1536], BF16)
    CKVN = k.sb("ckvn", [128, 2, NKEY], BF16)
    KT = k.sb("kt", [128, NKEY], BF16)
    CQN = k.sb("cqn", [128, 3, NT], BF16)
    CC = k.sb("cc", [128, NT], F32)
    SS = k.sb("ss", [128, NT], F32)
    AO = k.sb("ao", [128, 8, NT], BF16)
    rt1 = k.sb("rt1", [128, TB], F32)
    rt2 = k.sb("rt2", [128, TB], F32)
    MEMN = k.sb("memn", [128, 8, N_MEM], BF16)
    Kmem = k.sb("kmem", [128, 2, N_MEM], BF16)
    Vmem = k.sb("vmem", [128, 2, 4, 192], BF16)
    k.dma("pool", Wi[:], wi_in.rearrange("(kc p) f -> p kc f", p=128), "w1")
    k.dma("pool", Wukv[:], wukv_in.rearrange("(kc p) f -> p kc f", p=128), "w2")

    def rope(dst, A, B, cc, ss):
        k.tt("dve", rt1[64:96, :], A, cc, ALU.mult)
        k.tt("dve", rt2[64:96, :], B, ss, ALU.mult)
        k.tt("pool", dst, rt1[64:96, :], rt2[64:96, :], ALU.add)

    k.push_scope()
    X = k.sb("xblk", [128, 8, TB], F32)
    HT = k.sb("ht", [128, 8, TB], BF16)
    stage = [k.sb(f"stg{i}", [128, D], F32) for i in range(2)]
    QC = k.sb("qc", [128, 2, TB], BF16)
    Wm = k.sb("wm", [128, 8, 512], BF16)
    posi = k.sb("posi", [128, TB], I32)
    yv = k.sb("yv", [128, TB], F32)
    yi = k.sb("yi", [128, TB], I32)
    yf = k.sb("yf", [128, TB], F32)
    fr = k.sb("fr", [128, TB], F32)
    cmp_ = k.sb("cmp", [128, TB], F32)
    CCt = k.sb("cct", [128, TB], F32)
    SSt = k.sb("sst", [128, TB], F32)
    k.dma("pool", Wm[:], wm_in.rearrange("(kc p) f -> p kc f", p=128), "w0")
    cm.load_xT_block(mem_in, X, 0, 2, stage, "xs")
    cm.norm_block(X, 0, N_MEM, gt[:, 8:16], MEMN, 0)
    k.memset("pool", Vmem[:], 1.0)
    for pr in range(2):
        proj_chunk(cm, Wm, pr * 128, 128, MEMN, 0, N_MEM, Kmem[:, pr, :], "act")
    for t in range(2):
        b = cm.next_bank(PROJ_BANKS)
        for kc in range(8):
            k.mm(cm.bank(b, 0, 256), MEMN[:, kc, t * 128:(t + 1) * 128], Wm[:, kc, 256:512], start=(kc == 0), stop=(kc == 7))
        k.copy("dve", Vmem[:, t, :, 64:128], cm.bank(b, 0, 256).rearrange("p (h d) -> p h d", h=4))

    def frac_sin(dst, add, scale_ap):
        src = yv
        if add != 0.0:
            k.ts("dve", yf[:], yv[:], add, ALU.add)
            src = yf
            k.copy("dve", yi[:], yf[:])
        else:
            k.copy("dve", yi[:], yv[:])
        k.copy("dve", fr[:], yi[:])
        k.tt("dve", fr[:], src[:], fr[:], ALU.subtract)
        k.ts("dve", cmp_[:], fr[:], 0.5, ALU.is_gt)
        k.tt("dve", fr[:], fr[:], cmp_[:], ALU.subtract)
        k.ts("dve", cmp_[:], fr[:], -0.5, ALU.is_lt)
        k.tt("dve", fr[:], fr[:], cmp_[:], ALU.add)
        k.act(dst, fr[:], AF.Sin, scale=scale_ap)

    def rope_tables(pos_slice, cc, ss):
        k.dma("sp", posi[:], pos_slice.partition_broadcast(128), "posd")
        k.copy("dve", yv[:], posi[:])
        k.ts("dve", yv[:], yv[:], rc[:, 0:1], ALU.mult)
        frac_sin(ss, 0.0, rc[:, 1:2])
        frac_sin(cc, 0.25, rc[:, 2:3])

    def latents_block(xsrc, t0, koff, pos_src, own):
        cm.load_xT_block(xsrc[t0:t0 + TB, :], X, 0, 4, stage, "xs")
        cm.norm_block(X, 0, TB, gt[:, 0:8], HT, 0)
        if own:
            cc, ss = CC[:, t0:t0 + TB], SS[:, t0:t0 + TB]
        else:
            cc, ss = CCt[:, :], SSt[:, :]
        rope_tables(pos_src[t0:t0 + TB], cc, ss)
        if own:
            for c in range(3):
                for kc in range(8):
                    k.mm(cm.bank(c), Wi[:, kc, c * 128:(c + 1) * 128], HT[:, kc, :], start=(kc == 0), stop=(kc == 7))
            rstd = cm.rms_stats([cm.bank(c) for c in range(3)], TB, 1.0 / 384, banks=(5,), sq_eng="act")
            for c in range(3):
                k.stt("dve", CQN[:, c, t0:t0 + TB], cm.bank(c), gt[:, 16 + c:17 + c], rstd, ALU.mult, ALU.mult)
        for c in range(2):
            for kc in range(8):
                k.mm(cm.bank(3 + c), Wi[:, kc, 384 + c * 128:384 + (c + 1) * 128], HT[:, kc, :], start=(kc == 0), stop=(kc == 7))
        rstd = cm.rms_stats([cm.bank(3 + c) for c in range(2)], TB, 1.0 / 256, banks=(5,), sq_eng="act")
        for c in range(2):
            k.stt("dve", CKVN[:, c, koff + t0:koff + t0 + TB], cm.bank(3 + c), gt[:, 19 + c:20 + c], rstd, ALU.mult, ALU.mult)
        for i, b in enumerate((6, 7)):
            for kc in range(8):
                k.mm(cm.bank(b, 0, TB, 0, 96), Wi[:, kc, 896 + i * 96:896 + (i + 1) * 96], HT[:, kc, :], start=(kc == 0), stop=(kc == 7))
        rope(KT[64:96, koff + t0:koff + t0 + TB], cm.bank(6, 0, TB, 64, 96), cm.bank(7, 0, TB, 64, 96), cc[64:96, :], ss[64:96, :])
        if own:
            for c in range(2):
                proj_chunk(cm, Wi, 640 + c * 128, 128, HT, 0, TB, QC[:, c, :], "dve")
            cross_units(cm, at, Kmem, Vmem, QC, 0, AO, t0)

    for j in range(NB):
        latents_block(xp_in, j * TB, 0, posp_in, False)
    for j in range(NB):
        latents_block(x_in, j * TB, NT, pos_in, True)
    k.pop_scope()

    k.push_scope()
    Wo = Wi
    k.dma("pool", Wo[:, :, 0:D], wo_in.rearrange("(kc p) f -> p kc f", p=128), "w3")
    VV = [k.sb("ve", [128, 32, 128], BF16), k.sb("vo", [128, 32, 128], BF16)]
    QT = [k.sb(f"qt{i}", [128, NT], BF16) for i in range(2)]
    Wq = [k.sb(f"wq{i}", [128, 3, 192], BF16) for i in range(2)]
    k.memset("pool", VV[0][:], 1.0)
    k.memset("pool", VV[1][:], 1.0)
    wuqv = wuq_in.rearrange("(kc p) f -> p kc f", p=128)
    P2 = (7, 6)
    for h in range(12):
        par = h % 2
        r0 = par * 64
        wq, qt, Vh = Wq[par], QT[par], VV[par]
        voff = 0 if par == 0 else 64
        k.dma("pool", wq[:], wuqv[:, :, h * 192:(h + 1) * 192], f"wq{par}")
        for j in range(NB):
            t0 = j * TB
            bA = cm.next_bank(P2)
            for kc in range(3):
                k.mm(cm.bank(bA, 0, TB, 0, 96), wq[:, kc, 0:96], CQN[:, kc, t0:t0 + TB], start=(kc == 0), stop=(kc == 2))
            bB = cm.next_bank(P2)
            for kc in range(3):
                k.mm(cm.bank(bB, 0, TB, 0, 96), wq[:, kc, 96:192], CQN[:, kc, t0:t0 + TB], start=(kc == 0), stop=(kc == 2))
            k.copy("dve", qt[0:64, t0:t0 + TB], cm.bank(bA, 0, TB, 0, 64))
            rope(qt[64:96, t0:t0 + TB], cm.bank(bA, 0, TB, 64, 96), cm.bank(bB, 0, TB, 64, 96), CC[64:96, t0:t0 + TB], SS[64:96, t0:t0 + TB])
        for kb in range(NKEY // TB):
            b = cm.next_bank(P2)
            for kc in range(2):
                k.mm(cm.bank(b, 0, TB, 0, 64), Wukv[:, kc, 128 * h:128 * h + 64], CKVN[:, kc, kb * TB:(kb + 1) * TB], start=(kc == 0), stop=(kc == 1))
            k.copy("dve", KT[0:64, kb * TB:(kb + 1) * TB], cm.bank(b, 0, TB, 0, 64))
        for g in range(NKEY // 1024):
            b = cm.next_bank(P2)
            for tl in range(8):
                kt_ = g * 8 + tl
                for kc in range(2):
                    k.mm(cm.bank(b, tl * 64, (tl + 1) * 64), CKVN[:, kc, kt_ * 128:(kt_ + 1) * 128],
                         Wukv[:, kc, 128 * h + 64:128 * h + 128], start=(kc == 0), stop=(kc == 1))
            k.copy("dve", Vh[:, g * 8:(g + 1) * 8, voff:voff + 64], cm.bank(b).rearrange("p (t d) -> p t d", t=8))
        for j in range(NB):
            t0 = j * TB
            tiles = []
            for kt_ in range(16):
                tiles.append(dict(K=KT[0:96, kt_ * 128:(kt_ + 1) * 128], V=Vh[:, kt_, :], c0=0, c1=TB, bias=pb[:, 0:1]))
            for kt_ in range(16, 16 + 4 * j):
                tiles.append(dict(K=KT[0:96, kt_ * 128:(kt_ + 1) * 128], V=Vh[:, kt_, :], c0=0, c1=TB, bias=cm.zcol[:, 0:1]))
            for t in range(4):
                kt_ = 16 + 4 * j + t
                tiles.append(dict(K=KT[0:96, kt_ * 128:(kt_ + 1) * 128], V=Vh[:, kt_, :], c0=128 * t, c1=TB,
                                  bias=cm.zcol[:, 0:1], masks=[(128 * t, 128 * t + 128, tri[:, :])]))
            at.unit(tiles, lambda c0, c1, qt=qt, t0=t0: qt[0:96, t0 + c0:t0 + c1], MLA_SCALE, par,
                    AO[r0:r0 + 64, h // 2, t0:t0 + TB])
    k.pop_scope()

    k.push_scope()
    X = k.sb("xblk3", [128, 8, TB], F32)
    stage = [k.sb(f"stg3{i}", [128, D], F32) for i in range(2)]
    ostage = [k.sb("ostg3", [128, D], F32)]
    for j in range(NB):
        cm.load_xT_block(x_in[j * TB:(j + 1) * TB, :], X, 0, 4, stage, "xs3")
        out_proj_block(cm, Wo, AO, j * TB, X, 0)
        cm.store_xT_block(X, 0, 4, y_out[j * TB:(j + 1) * TB, :], ostage, "os")
    k.pop_scope()
    k.finish()
    return k


def mla_host_inputs(x_own, x_prev, mem, pos_own, pos_prev, g_attn, g_mem, g_q, g_kv, w_in, w_uq, w_ukv, w_mem, w_o, prevbias):
    f = lambda a: np.ascontiguousarray(a, dtype=np.float32)
    kr = w_in[:, 640:672]
    krp = np.concatenate([kr[:, 16:32], kr[:, 0:16]], axis=1)
    pad = w_in[:, 576:640]
    wi = np.concatenate([w_in[:, 0:640], w_in[:, 672:928], pad, kr, pad, krp], axis=1)
    hq = []
    for h in range(12):
        wh = w_uq[:, h * 96:(h + 1) * 96]
        hq += [wh, wh[:, 0:64], wh[:, 80:96], wh[:, 64:80]]
    wuq = np.concatenate(hq, axis=1)
    g = np.zeros((128, 24), np.float32)
    g[:, 0:8] = g_attn.reshape(8, 128).T
    g[:, 8:16] = g_mem.reshape(8, 128).T
    g[:, 16:19] = g_q.reshape(3, 128).T
    g[:, 19:21] = g_kv.reshape(2, 128).T
    r = np.arange(128)
    inv = 10000.0 ** (-(np.arange(16, dtype=np.float64) * 2.0) / 32)
    rcv = np.zeros((128, 4), np.float64)
    rcv[:, 0] = inv[r % 16] / (2 * np.pi)
    rcv[:, 1] = np.where((r % 32) < 16, -1.0, 1.0) * 2 * np.pi
    rcv[:, 2] = 2 * np.pi
    kk = np.arange(128)[:, None]
    c = np.arange(128)[None, :]
    tri = np.where(kk <= c, 0.0, 8.0 * NEGB)
    return {"x": f(x_own), "xprev": f(x_prev), "mem": f(mem), "g": f(g),
            "pos": np.ascontiguousarray(pos_own, dtype=np.int32), "posprev": np.ascontiguousarray(pos_prev, dtype=np.int32),
            "ropec": f(rcv), "w_in": f(wi), "w_uq": f(wuq), "w_ukv": f(w_ukv), "w_mem": f(w_mem), "w_o": f(w_o),
            "trimask": f(tri), "prevbias": np.full((128, 1), prevbias, np.float32), "ident": np.eye(128, dtype=np.float32)}


N_CORES = 8


def _run(kb, in_maps):
    res = run_bass_kernel_spmd(kb.nc, in_maps, core_ids=list(range(N_CORES)))
    return [np.asarray(r["y"]) for r in res.results]


def kernel(x, mem, positions, attn_norm_g, mlp_norm_g, mem_norm_g, final_norm_g,
           mla_w_in, mla_q_norm_g, mla_kv_norm_g, mla_w_uq, mla_w_ukv,
           swa_w_in, swa_sinks, w_mem_kv, w_o, mlp_w_up, mlp_w_down):
    x = np.asarray(x, dtype=np.float32)
    mem = np.asarray(mem, dtype=np.float32)
    positions = np.asarray(positions)
    B, S, _ = x.shape
    depth = attn_norm_g.shape[0]
    xs = [np.ascontiguousarray(x[c // 2, (c % 2) * NT:(c % 2 + 1) * NT]) for c in range(N_CORES)]
    ident = np.eye(128, dtype=np.float32)
    for l in range(depth):
        j = l // 2
        ins = []
        for c in range(N_CORES):
            b, half = c // 2, c % 2
            prev = xs[c - half]
            pbias = 0.0 if half else NEGB
            if l % 2 == 0:
                ins.append(mla_host_inputs(xs[c], prev, mem[b], positions[b, half * NT:(half + 1) * NT],
                                           positions[b, 0:NT], attn_norm_g[l], mem_norm_g, mla_q_norm_g[j],
                                           mla_kv_norm_g[j], mla_w_in[j], mla_w_uq[j], mla_w_ukv[j],
                                           w_mem_kv[l], w_o[l], pbias))
            else:
                ins.append(swa_host_inputs(xs[c], prev[NT - 128:NT], mem[b], attn_norm_g[l], mem_norm_g,
                                           swa_w_in[j], w_mem_kv[l], w_o[l], swa_sinks[j], pbias))
        kb = build_mla_program() if l % 2 == 0 else build_swa_program()
        xs = _run(kb, ins)
        last = (l == depth - 1)
        g = np.concatenate([np.asarray(mlp_norm_g[l]).reshape(8, 128).T,
                            np.asarray(final_norm_g).reshape(8, 128).T], axis=1).astype(np.float32)
        ins = [{"x": np.ascontiguousarray(xs[c], dtype=np.float32), "g": np.ascontiguousarray(g),
                "w_up": np.ascontiguousarray(mlp_w_up[l], dtype=np.float32),
                "w_down": np.ascontiguousarray(mlp_w_down[l], dtype=np.float32), "ident": ident}
               for c in range(N_CORES)]
        kb = build_mlp_program(last)
        xs = _run(kb, ins)
    out = np.empty((B, S, D), np.float32)
    for c in range(N_CORES):
        out[c // 2, (c % 2) * NT:(c % 2 + 1) * NT] = xs[c]
    return out
```

```python
import numpy as np
from contextlib import ExitStack
import concourse.bass as bass
import concourse.mybir as mybir
from concourse.bass_utils import run_bass_kernel_spmd

F32, BF16, I32 = mybir.dt.float32, mybir.dt.bfloat16, mybir.dt.int32
ALU = mybir.AluOpType
AF = mybir.ActivationFunctionType

D = 1024
NT = 2048
TB = 512
NB = NT // TB
DFF = 4096
EPS = 1e-6
NEGB = -30000.0
N_MEM = 256


class KB:
    def __init__(self):
        self.nc = bass.Bass("TRN2", target_bir_lowering=False)
        nc = self.nc
        self.E = dict(pe=nc.tensor, act=nc.scalar, dve=nc.vector, pool=nc.gpsimd, sp=nc.sync)
        self.csem = {e: nc.alloc_semaphore("s_" + e) for e in ("pe", "act", "dve", "pool")}
        self.cnt = {e: 0 for e in self.csem}
        self.waited = {e: {} for e in self.E}
        self.trk = {}
        self.W = {}
        self.dsem = {}
        self.dcnt = {}
        self.semh = {}
        for e, h in self.csem.items():
            self.semh["c_" + e] = h
        self.es = ExitStack()
        self.n_ins = 0

    def sb(self, name, shape, dtype):
        t = self.es.enter_context(self.nc.sbuf_tensor(name, list(shape), dtype))
        self.W[name] = int(np.prod(shape[1:]))
        self.trk[name] = []
        return t

    def ps(self, name, shape, dtype=F32):
        t = self.es.enter_context(self.nc.psum_tensor(name, list(shape), dtype))
        self.W[name] = int(np.prod(shape[1:]))
        self.trk[name] = []
        return t

    def push_scope(self):
        self._scopes = getattr(self, "_scopes", [])
        self._scopes.append(self.es)
        self.es = ExitStack()

    def pop_scope(self):
        self.barrier()
        self.es.close()
        self.es = self._scopes.pop()

    def barrier(self):
        toks = [("c_" + e, c) for e, c in self.cnt.items() if c > 0]
        toks += [("d_" + s, c * 16) for s, c in self.dcnt.items() if c > 0]
        for e in self.E:
            self._wait(e, toks)

    def dram_in(self, name, shape, dtype=F32):
        return self.nc.dram_tensor(name, list(shape), dtype, kind="ExternalInput").ap()

    def dram_out(self, name, shape, dtype=F32):
        return self.nc.dram_tensor(name, list(shape), dtype, kind="ExternalOutput").ap()

    def dma_sem(self, name):
        if name not in self.dsem:
            self.dsem[name] = self.nc.alloc_semaphore("d_" + name)
            self.dcnt[name] = 0
            self.semh["d_" + name] = self.dsem[name]
        return name

    def _region(self, ap):
        name = ap.tensor.name
        if name not in self.W:
            return None
        W = self.W[name]
        off = int(ap.offset)
        a = ap.ap
        p0 = off // W
        f0 = off % W
        p1 = p0 + a[0][1]
        hi = f0 + sum((c - 1) * s for s, c in a[1:]) + 1
        if name.startswith("ps"):
            return name, 0, 128, (f0 // 512) * 512, ((hi + 511) // 512) * 512
        return name, p0, p1, f0, hi

    def _deps(self, reads, writes):
        toks = set()
        for ap in reads:
            r = self._region(ap)
            if r is None:
                continue
            name, p0, p1, lo, hi = r
            for e in self.trk[name]:
                if e[4] and e[0] < p1 and p0 < e[1] and e[2] < hi and lo < e[3]:
                    toks.add(e[5])
        for ap in writes:
            r = self._region(ap)
            if r is None:
                continue
            name, p0, p1, lo, hi = r
            for e in self.trk[name]:
                if e[0] < p1 and p0 < e[1] and e[2] < hi and lo < e[3]:
                    toks.add(e[5])
        return toks

    def _record(self, reads, writes, tok):
        for ap in writes:
            r = self._region(ap)
            if r is None:
                continue
            name, p0, p1, lo, hi = r
            lst = self.trk[name]
            lst[:] = [e for e in lst if not (p0 <= e[0] and e[1] <= p1 and lo <= e[2] and e[3] <= hi)]
            lst.append([p0, p1, lo, hi, True, tok])
        for ap in reads:
            r = self._region(ap)
            if r is None:
                continue
            name, p0, p1, lo, hi = r
            lst = self.trk[name]
            for e in lst:
                if (not e[4]) and e[0] == p0 and e[1] == p1 and e[2] == lo and e[3] == hi and e[5][0] == tok[0]:
                    e[5] = tok
                    break
            else:
                lst.append([p0, p1, lo, hi, False, tok])

    def _wait(self, eng, toks):
        best = {}
        for s, v in toks:
            if s.startswith("d_"):
                v = max(v, self.dcnt[s[2:]] * 16)
            if v > best.get(s, 0):
                best[s] = v
        for s, v in best.items():
            if eng == "pe" and s == "c_pe":
                continue
            if self.waited[eng].get(s, 0) >= v:
                continue
            self.E[eng].wait_ge(self.semh[s], v)
            self.waited[eng][s] = v
            self.n_ins += 1

    def op(self, eng, fn, reads=(), writes=()):
        toks = self._deps(reads, writes)
        self._wait(eng, toks)
        ins = fn()
        self.cnt[eng] += 1
        tok = ("c_" + eng, self.cnt[eng])
        ins.then_inc(self.csem[eng], 1)
        self._record(reads, writes, tok)
        self.n_ins += 1
        return tok

    def dma(self, q, out, in_, sem):
        self.dma_sem(sem)
        toks = self._deps([in_], [out])
        self._wait(q, toks)
        ins = self.E[q].dma_start(out=out, in_=in_)
        self.dcnt[sem] += 1
        tok = ("d_" + sem, self.dcnt[sem] * 16)
        ins.then_inc(self.dsem[sem], 16)
        self._record([in_], [out], tok)
        self.n_ins += 1
        return tok

    def wait_all_dma(self, eng):
        toks = [("d_" + s, c * 16) for s, c in self.dcnt.items() if c > 0]
        self._wait(eng, toks)

    def mm(self, out, lhsT, rhs, start=True, stop=True, extra_reads=()):
        return self.op("pe", lambda: self.nc.tensor.matmul(out, lhsT, rhs, start=start, stop=stop,
                                                          skip_group_check=True),
                       reads=[lhsT, rhs, *extra_reads], writes=[out])

    def tr(self, out, in_, ident):
        return self.op("pe", lambda: self.nc.tensor.transpose(out, in_, ident),
                       reads=[in_, ident], writes=[out])

    def act(self, out, in_, func, bias=None, scale=1.0, eng="act"):
        reads = [in_]
        kw = {}
        if bias is not None:
            kw["bias"] = bias
            if not isinstance(bias, (int, float)):
                reads.append(bias)
        if not isinstance(scale, (int, float)):
            reads.append(scale)
        return self.op("act", lambda: self.nc.scalar.activation(out=out, in_=in_, func=func, scale=scale, **kw),
                       reads=reads, writes=[out])

    def copy(self, eng, out, in_):
        if eng == "act":
            return self.op("act", lambda: self.nc.scalar.copy(out=out, in_=in_), reads=[in_], writes=[out])
        return self.op(eng, lambda: self.E[eng].tensor_copy(out=out, in_=in_), reads=[in_], writes=[out])

    def tt(self, eng, out, in0, in1, op):
        return self.op(eng, lambda: self.E[eng].tensor_tensor(out=out, in0=in0, in1=in1, op=op),
                       reads=[in0, in1], writes=[out])

    def ts(self, eng, out, in0, s1, op0, s2=None, op1=None):
        reads = [in0] + [s for s in (s1, s2) if s is not None and not isinstance(s, (int, float))]
        if op1 is None:
            return self.op(eng, lambda: self.E[eng].tensor_scalar(out=out, in0=in0, scalar1=s1, scalar2=None, op0=op0),
                           reads=reads, writes=[out])
        return self.op(eng, lambda: self.E[eng].tensor_scalar(out=out, in0=in0, scalar1=s1, scalar2=s2, op0=op0, op1=op1),
                       reads=reads, writes=[out])

    def stt(self, eng, out, in0, scalar, in1, op0, op1):
        reads = [in0, in1] + ([] if isinstance(scalar, (int, float)) else [scalar])
        return self.op(eng, lambda: self.E[eng].scalar_tensor_tensor(out=out, in0=in0, scalar=scalar, in1=in1,
                                                                    op0=op0, op1=op1),
                       reads=reads, writes=[out])

    def memset(self, eng, out, val):
        return self.op(eng, lambda: self.E[eng].memset(out, val), reads=[], writes=[out])

    def finish(self):
        self.wait_all_dma("sp")
        self.es.close()


class Common:
    def __init__(self, kb, consts_dram, nbuf=2):
        self.kb = kb
        k = kb
        self.PS = [k.ps(f"ps{i}", [128, 1024]) for i in range(4)]
        self.rr = 0
        self.identf = k.sb("identf", [128, 128], F32)
        self.identb = k.sb("identb", [128, 128], BF16)
        self.onesb = k.sb("onesb", [128, 128], BF16)
        self.onesf = k.sb("onesf", [128, 128], F32)
        self.zcol = k.sb("zcol", [128, 1], F32)
        self.epscol = k.sb("epscol", [128, 1], F32)
        k.dma("sp", self.identf[:], consts_dram["ident"], "const")
        k.dma("pool", self.identb[:], consts_dram["ident"], "constp")
        k.memset("dve", self.onesb[:], 1.0)
        k.memset("dve", self.onesf[:], 1.0)
        k.memset("dve", self.zcol[:], 0.0)
        k.memset("dve", self.epscol[:], EPS)
        self.nbuf = nbuf
        self.sq = [k.sb(f"sq{i}", [128, 8, TB], BF16) for i in range(nbuf)]
        self.lnv = [k.sb(f"lnv{i}", [128, TB], F32) for i in range(nbuf)]
        self.rstd = [k.sb(f"rstd{i}", [128, TB], F32) for i in range(nbuf)]
        self.nrm_i = 0

    def bank(self, b, lo=0, hi=512, p0=0, p1=128):
        return self.PS[b // 2][p0:p1, (b % 2) * 512 + lo:(b % 2) * 512 + hi]

    def next_bank(self, banks=(0, 1, 2, 3, 4, 5, 6, 7)):
        b = banks[self.rr % len(banks)]
        self.rr += 1
        return b

    def rms_stats(self, chunks, w, inv_n, banks=(0, 1, 2, 3, 4, 5, 6, 7), sq_eng="pool"):
        if sq_eng == "act":
            return self._rms_stats_act(chunks, w, inv_n, banks)
        return self._rms_stats(chunks, w, inv_n, banks, sq_eng)

    def _rms_stats_act(self, chunks, w, inv_n, banks):
        k = self.kb
        i = self.nrm_i % self.nbuf
        self.nrm_i += 1
        sq, lnv, rstd = self.sq[i], self.lnv[i], self.rstd[i]
        b = self.next_bank(banks)
        n = len(chunks)
        for c, ap in enumerate(chunks):
            k.act(sq[:, c, 0:w], ap, AF.Square)
        for c in range(n):
            k.mm(self.bank(b, 0, w), self.onesb[:, :], sq[:, c, 0:w], start=(c == 0), stop=(c == n - 1))
        k.act(lnv[:, 0:w], self.bank(b, 0, w), AF.Ln, bias=self.epscol[:, 0:1], scale=inv_n)
        k.act(rstd[:, 0:w], lnv[:, 0:w], AF.Exp, scale=-0.5)
        return rstd[:, 0:w]

    def _rms_stats(self, chunks, w, inv_n, banks=(0, 1, 2, 3, 4, 5, 6, 7), sq_eng="pool"):
        k = self.kb
        i = self.nrm_i % self.nbuf
        self.nrm_i += 1
        sq, lnv, rstd = self.sq[i], self.lnv[i], self.rstd[i]
        b = self.next_bank(banks)
        n = len(chunks)
        for c, ap in enumerate(chunks):
            k.tt(sq_eng, sq[:, c, 0:w], ap, ap, ALU.mult)
        for c in range(n):
            k.mm(self.bank(b, 0, w), self.onesb[:, :], sq[:, c, 0:w], start=(c == 0), stop=(c == n - 1))
        k.act(lnv[:, 0:w], self.bank(b, 0, w), AF.Ln, bias=self.epscol[:, 0:1], scale=inv_n)
        k.act(rstd[:, 0:w], lnv[:, 0:w], AF.Exp, scale=-0.5)
        return rstd[:, 0:w]

    def load_xT_block(self, x_rows, dstT, t0, ntiles, stage, sem_prefix, dst_chunks=8):
        k = self.kb
        for i in range(ntiles):
            st = stage[i % len(stage)]
            k.dma("sp", st[:], x_rows[i * 128:(i + 1) * 128, :], f"{sem_prefix}{i % len(stage)}")
            for h in range(2):
                b = self.next_bank()
                for cc in range(4):
                    c = h * 4 + cc
                    k.tr(self.bank(b, cc * 128, cc * 128 + 128), st[:, c * 128:(c + 1) * 128], self.identf[:])
                src = self.bank(b).rearrange("p (c t) -> p c t", c=4)
                dst = dstT[:, h * 4:h * 4 + 4, t0 + i * 128:t0 + (i + 1) * 128]
                k.copy("act" if (i + h) % 2 == 0 else "dve", dst, src)

    def store_xT_block(self, srcT, t0, ntiles, out_rows, stage, sem_prefix):
        k = self.kb
        for i in range(ntiles):
            st = stage[i % len(stage)]
            for h in range(2):
                b = self.next_bank()
                for cc in range(4):
                    c = h * 4 + cc
                    k.tr(self.bank(b, cc * 128, cc * 128 + 128), srcT[:, c, t0 + i * 128:t0 + (i + 1) * 128],
                         self.identf[:])
                k.copy("act" if (i + h) % 2 == 0 else "dve", st[:, h * 512:(h + 1) * 512], self.bank(b))
            k.dma("sp", out_rows[i * 128:(i + 1) * 128, :], st[:], f"{sem_prefix}{i % len(stage)}")

    def norm_block(self, xT, t0, w, gcols, outT, o0, out_dtype_bf16=True):
        k = self.kb
        rstd = self.rms_stats([xT[:, c, t0:t0 + w] for c in range(8)], w, 1.0 / D)
        for c in range(8):
            k.stt("dve", outT[:, c, o0:o0 + w], xT[:, c, t0:t0 + w], gcols[:, c:c + 1], rstd, ALU.mult, ALU.mult)


class Attn:
    def __init__(self, cm, npt=4):
        k = cm.kb
        self.cm = cm
        self.PT = [k.sb(f"pt{i}", [128, 2 * TB], BF16) for i in range(3)]
        self.rsum = k.sb("rsum", [128, TB], F32)
        self.bc = k.sb("bcs", [128, TB], F32)
        self.pi = 0
        self.si = 0
        self.oi = 0

    def unit(self, tiles, qfn, scale, par, dst, extra_sum=None):
        cm, k = self.cm, self.cm.kb
        nc = k.nc
        ob = 4 + (self.oi % 2)
        self.oi += 1
        n = len(tiles)
        LOOK = 2
        pts = {}

        def full(t):
            return t["c0"] == 0 and t["c1"] == TB and not t.get("masks")

        def same(a, b):
            return a.tensor.name == b.tensor.name and int(a.offset) == int(b.offset)

        steps = []
        i = 0
        while i < n:
            t = tiles[i]
            if i + 1 < n and full(t) and full(tiles[i + 1]) and same(t["bias"], tiles[i + 1]["bias"]):
                steps.append([i, i + 1])
                i += 2
            else:
                steps.append([i])
                i += 1

        def s_stage(si_):
            step = steps[si_]
            pi_ = self.si % 2
            self.si += 1
            for idx, ti in enumerate(step):
                t = tiles[ti]
                c0, c1 = t["c0"], t["c1"]
                sb_ = 2 * pi_ + idx
                masks = t.get("masks", ())
                k.mm(cm.bank(sb_, c0, c1), t["K"], qfn(c0, c1), start=True, stop=(len(masks) == 0))
                for mi, (m0, m1, mrhs) in enumerate(masks):
                    k.mm(cm.bank(sb_, m0, m1), cm.identb[:, :], mrhs, start=False, stop=(mi == len(masks) - 1))
            pt = self.PT[self.pi % len(self.PT)]
            self.pi += 1
            t0_ = tiles[step[0]]
            if len(step) == 2:
                k.act(pt[:, 0:2 * TB], cm.PS[pi_][:, 0:2 * TB], AF.Exp, bias=t0_["bias"], scale=scale)
            else:
                c0, c1 = t0_["c0"], t0_["c1"]
                k.act(pt[:, c0:c1], cm.bank(2 * pi_, c0, c1), AF.Exp, bias=t0_["bias"], scale=scale)
            pts[si_] = pt

        def pv_stage(si_):
            step = steps[si_]
            for idx, ti in enumerate(step):
                t = tiles[ti]
                c0, c1 = t["c0"], t["c1"]
                k.mm(cm.bank(ob, c0, c1), t["V"], pts[si_][:, idx * TB + c0:idx * TB + c1], start=(ti == 0), stop=(ti == n - 1))

        ns = len(steps)
        for i in range(ns + LOOK):
            if i < ns:
                s_stage(i)
            if i - LOOK >= 0:
                pv_stage(i - LOOK)
        sr = 64 if par == 0 else 0
        r0 = 0 if par == 0 else 64
        rs = self.rsum[sr:sr + 1, :]
        if extra_sum is not None:
            k.ts("dve", rs, cm.bank(ob, 0, TB, sr, sr + 1), extra_sum, ALU.add)
            k.op("dve", lambda: nc.vector.reciprocal(out=rs, in_=rs), reads=[rs], writes=[rs])
        else:
            src = cm.bank(ob, 0, TB, sr, sr + 1)
            k.op("dve", lambda: nc.vector.reciprocal(out=rs, in_=src), reads=[src], writes=[rs])
        k.mm(cm.bank(6), cm.onesf[sr:sr + 1, 0:128], rs, start=True, stop=True)
        k.copy("act", self.bc[r0:r0 + 64, :], cm.bank(6, 0, TB, r0, r0 + 64))
        k.tt("dve", dst, cm.bank(ob, 0, TB, r0, r0 + 64), self.bc[r0:r0 + 64, :], ALU.mult)


PROJ_BANKS = (0, 1, 2, 3, 7)


def proj_chunk(cm, W, c0, M, HT, h0, w, dst, eng, nk=8):
    k = cm.kb
    b = cm.next_bank(PROJ_BANKS)
    for kc in range(nk):
        k.mm(cm.bank(b, 0, w, 0, M), W[:, kc, c0:c0 + M], HT[:, kc, h0:h0 + w], start=(kc == 0), stop=(kc == nk - 1))
    if dst is not None:
        k.copy(eng, dst, cm.bank(b, 0, w, 0, M))
    return b


def mem_setup(cm, mem_in, gt_mem, Wm, stage):
    k = cm.kb
    memT = k.sb("memT", [128, 8, N_MEM], F32)
    MEMN = k.sb("memn", [128, 8, N_MEM], BF16)
    Kmem = k.sb("kmem", [128, 2, N_MEM], BF16)
    Vmem = k.sb("vmem", [128, 2, 4, 192], BF16)
    cm.load_xT_block(mem_in, memT, 0, 2, stage, "xs")
    cm.norm_block(memT, 0, N_MEM, gt_mem, MEMN, 0)
    k.memset("pool", Vmem[:], 1.0)
    for pr in range(2):
        proj_chunk(cm, Wm, pr * 128, 128, MEMN, 0, N_MEM, Kmem[:, pr, :], "act")
    for t in range(2):
        b = cm.next_bank(PROJ_BANKS)
        for kc in range(8):
            k.mm(cm.bank(b, 0, 256), MEMN[:, kc, t * 128:(t + 1) * 128], Wm[:, kc, 256:512], start=(kc == 0), stop=(kc == 7))
        k.copy("dve", Vmem[:, t, :, 64:128], cm.bank(b, 0, 256).rearrange("p (h d) -> p h d", h=4))
    return Kmem, Vmem


def cross_units(cm, at, Kmem, Vmem, QC, q0, AO, a0):
    for ch in range(4):
        par = ch % 2
        r0 = par * 64
        tiles = []
        for t in range(2):
            V = Vmem[:, t, ch, 64:192] if par == 0 else Vmem[:, t, ch, 0:128]
            tiles.append(dict(K=Kmem[r0:r0 + 64, ch // 2, t * 128:(t + 1) * 128], V=V, c0=0, c1=TB,
                              bias=cm.zcol[:, 0:1]))
        at.unit(tiles, lambda c0, c1, ch=ch, r0=r0: QC[r0:r0 + 64, ch // 2, q0 + c0:q0 + c1], 0.125, par,
                AO[r0:r0 + 64, 6 + ch // 2, a0:a0 + TB])


def out_proj_block(cm, Wo, AO, a0, X, x0):
    k = cm.kb
    for co in range(8):
        b = cm.next_bank(PROJ_BANKS)
        for kc in range(8):
            k.mm(cm.bank(b), Wo[:, kc, co * 128:(co + 1) * 128], AO[:, kc, a0:a0 + TB], start=(kc == 0), stop=(kc == 7))
        k.tt("dve", X[:, co, x0:x0 + TB], X[:, co, x0:x0 + TB], cm.bank(b), ALU.add)


SWA_POS = [0, 3, 1, 4, 2, 5, 6, 9, 7, 10, 8, 11]


def alibi_tables():
    slopes = 2.0 ** (-8.0 * (np.arange(12, dtype=np.float64) + 1.0) / 12)
    kk = np.arange(128)[:, None]
    c = np.arange(256)[None, :]
    d = (c - kk).astype(np.float64)
    valid = (d >= 0) & (d < 128)
    out = np.zeros((128, 12, 256), np.float64)
    for p in range(12):
        out[:, p, :] = np.where(valid, -8.0 * slopes[SWA_POS[p]] * d, 8.0 * NEGB)
    import ml_dtypes
    hi = out.astype(np.float32).astype(ml_dtypes.bfloat16).astype(np.float32)
    lo = (out - hi).astype(np.float32).astype(ml_dtypes.bfloat16).astype(np.float32)
    return hi, lo


MLA_SCALE = 96.0 ** -0.5


class Ctx:
    def __init__(self, k, cm, at):
        self.k, self.cm, self.at = k, cm, at
        self.uid = 0

    def names(self):
        self.uid += 1
        u = self.uid
        return lambda n: f"{n}_{u}"


def mem_kv(ctx, U, MEMN, Wm, Kmem=None, Vmem=None):
    k, cm = ctx.k, ctx.cm
    if Kmem is None:
        Kmem = k.sb(U("kmem"), [128, 2, N_MEM], BF16)
        Vmem = k.sb(U("vmem"), [128, 2, 4, 192], BF16)
    k.memset("pool", Vmem[:], 1.0)
    for pr in range(2):
        proj_chunk(cm, Wm, pr * 128, 128, MEMN, 0, N_MEM, Kmem[:, pr, :], "act")
    for t in range(2):
        b = cm.next_bank(PROJ_BANKS)
        for kc in range(8):
            k.mm(cm.bank(b, 0, 256), MEMN[:, kc, t * 128:(t + 1) * 128], Wm[:, kc, 256:512], start=(kc == 0), stop=(kc == 7))
        k.copy("dve", Vmem[:, t, :, 64:128], cm.bank(b, 0, 256).rearrange("p (h d) -> p h d", h=4))
    return Kmem, Vmem


def emit_mlp(ctx, x_src, nb, out, gcol, up_in, dn_in, final_gcol=None):
    k, cm = ctx.k, ctx.cm
    U = ctx.names()
    k.push_scope()
    wup = k.sb(U("wup"), [128, 8, DFF], BF16)
    wdn = k.sb(U("wdn"), [128, 32, D], BF16)
    upv = up_in.rearrange("(kc p) f -> p kc f", p=128)
    dnv = dn_in.rearrange("(fc p) d -> p fc d", p=128)
    for fb in range(8):
        k.dma("pool", wup[:, :, fb * 512:(fb + 1) * 512], upv[:, :, fb * 512:(fb + 1) * 512], f"wup{fb}")
        k.dma("pool", wdn[:, fb * 4:(fb + 1) * 4, :], dnv[:, fb * 4:(fb + 1) * 4, :], f"wdn{fb}")
    X = k.sb(U("xT"), [128, 8, TB], F32)
    H = k.sb(U("hT"), [128, 8, TB], BF16)
    stage = [k.sb(U(f"stg{i}"), [128, D], F32) for i in range(2)]
    ostage = [k.sb(U("ostg"), [128, D], F32)]
    rl = [k.sb(U(f"rl{i}"), [128, TB], F32) for i in range(2)]
    aT = [k.sb(U(f"aT{i}"), [128, 4, TB], BF16) for i in range(2)]
    ri = 0
    for j in range(nb):
        cm.load_xT_block(x_src[j * TB:(j + 1) * TB, :], X, 0, 4, stage, "xs")
        cm.norm_block(X, 0, TB, gcol, H, 0)
        def up_stage(fb):
            nonlocal ri
            A = aT[fb % 2]
            for q in range(4):
                b = cm.next_bank((0, 1, 2, 3))
                f0 = fb * 512 + q * 128
                for kc in range(8):
                    k.mm(cm.bank(b), wup[:, kc, f0:f0 + 128], H[:, kc, :], start=(kc == 0), stop=(kc == 7))
                r = rl[ri % 2]
                ri += 1
                k.act(r[:], cm.bank(b), AF.Relu)
                k.tt("pool", A[:, q, :], r[:], r[:], ALU.mult)

        def down_stage(fb):
            A = aT[fb % 2]
            for c in range(8):
                b = cm.next_bank((4, 5, 6, 7))
                for q in range(4):
                    k.mm(cm.bank(b), wdn[:, fb * 4 + q, c * 128:(c + 1) * 128], A[:, q, :], start=(q == 0), stop=(q == 3))
                k.tt("dve", X[:, c, :], X[:, c, :], cm.bank(b), ALU.add)

        for fb in range(9):
            if fb < 8:
                up_stage(fb)
            if fb >= 1:
                down_stage(fb - 1)
        if final_gcol is not None:
            rstd = cm.rms_stats([X[:, c, :] for c in range(8)], TB, 1.0 / D)
            for c in range(8):
                k.stt("dve", X[:, c, :], X[:, c, :], final_gcol[:, c:c + 1], rstd, ALU.mult, ALU.mult)
        cm.store_xT_block(X, 0, 4, out[j * TB:(j + 1) * TB, :], ostage, "os")
    k.pop_scope()


def emit_swa(ctx, x_src, nb, bnd_src, bnd_bias, out, gcol, MEMN, wi_in, wm_in, wo_in, sk, bmh_in, bml_in):
    k, cm, at = ctx.k, ctx.cm, ctx.at
    U = ctx.names()
    k.push_scope()
    Wi = k.sb(U("wi"), [128, 8, 1536], BF16)
    Wm = k.sb(U("wm"), [128, 8, 512], BF16)
    Wo = k.sb(U("wo"), [128, 8, D], BF16)
    BMh = k.sb(U("bmh"), [128, 12, 256], BF16)
    BMl = k.sb(U("bml"), [128, 12, 256], BF16)
    k.dma("pool", Wm[:], wm_in.rearrange("(kc p) f -> p kc f", p=128), "w0")
    k.dma("pool", Wi[:], wi_in.rearrange("(kc p) f -> p kc f", p=128), "w1")
    k.dma("pool", BMh[:], bmh_in, "w2")
    k.dma("pool", BMl[:], bml_in, "w3")
    k.dma("pool", Wo[:], wo_in.rearrange("(kc p) f -> p kc f", p=128), "w4")
    stage = [k.sb(U(f"stg{i}"), [128, D], F32) for i in range(2)]
    ostage = [k.sb(U("ostg"), [128, D], F32)]
    Kmem, Vmem = mem_kv(ctx, U, MEMN, Wm)
    nt = nb * TB
    KS = k.sb(U("ks"), [128, 2, nt + 128], BF16)
    VS = k.sb(U("vs"), [128, 4 * nb + 1, 4, 192], BF16)
    QS = k.sb(U("qs"), [128, 6, TB], BF16)
    QC = k.sb(U("qc"), [128, 2, TB], BF16)
    AO = k.sb(U("ao"), [128, 8, TB], BF16)
    X = k.sb(U("xblk"), [128, 8, TB], F32)
    HT = k.sb(U("ht"), [128, 8, TB], BF16)
    XB = k.sb(U("xb"), [128, 8, 128], F32)
    k.memset("pool", VS[:], 1.0)
    if bnd_src is not None:
        cm.load_xT_block(bnd_src, XB, 0, 1, stage, "xs")
        cm.norm_block(XB, 0, 128, gcol, HT, 0)
        for c in range(2):
            proj_chunk(cm, Wi, 768 + c * 128, 128, HT, 0, 128, KS[:, c, 0:128], "act")
        b = cm.next_bank(PROJ_BANKS)
        for kc in range(8):
            k.mm(cm.bank(b, 0, 256), HT[:, kc, 0:128], Wi[:, kc, 1024:1280], start=(kc == 0), stop=(kc == 7))
        k.copy("dve", VS[:, 0, :, 64:128], cm.bank(b, 0, 256).rearrange("p (h d) -> p h d", h=4))
    else:
        k.memset("pool", KS[:, :, 0:128], 0.0)
    for j in range(nb):
        cm.load_xT_block(x_src[j * TB:(j + 1) * TB, :], X, 0, 4, stage, "xs")
        cm.norm_block(X, 0, TB, gcol, HT, 0)
        for c in range(6):
            proj_chunk(cm, Wi, c * 128, 128, HT, 0, TB, QS[:, c, :], "act" if c % 2 else "dve")
        for c in range(2):
            proj_chunk(cm, Wi, 768 + c * 128, 128, HT, 0, TB, KS[:, c, 128 + j * TB:128 + (j + 1) * TB], "act")
        for c in range(2):
            proj_chunk(cm, Wi, 1280 + c * 128, 128, HT, 0, TB, QC[:, c, :], "dve")
        for t in range(4):
            b = cm.next_bank(PROJ_BANKS)
            for kc in range(8):
                k.mm(cm.bank(b, 0, 256), HT[:, kc, t * 128:(t + 1) * 128], Wi[:, kc, 1024:1280], start=(kc == 0), stop=(kc == 7))
            k.copy("dve" if t % 2 else "act", VS[:, 1 + 4 * j + t, :, 64:128],
                   cm.bank(b, 0, 256).rearrange("p (h d) -> p h d", h=4))
        cross_units(cm, at, Kmem, Vmem, QC, 0, AO, 0)
        for p in range(12):
            kh = SWA_POS[p] // 3
            par = p % 2
            r0 = par * 64
            tiles = []
            for i in range(5):
                s = 4 * j + i
                c0 = max(0, (i - 1) * 128)
                c1 = min(TB, (i + 1) * 128)
                m0 = 128 if i == 0 else 0
                V = VS[:, s, kh, 64:192] if par == 0 else VS[:, s, kh, 0:128]
                tiles.append(dict(K=KS[r0:r0 + 64, kh // 2, s * 128:(s + 1) * 128], V=V, c0=c0, c1=c1,
                                  bias=(bnd_bias if s == 0 else cm.zcol[:, 0:1]),
                                  masks=[(c0, c1, BMh[:, p, m0:m0 + (c1 - c0)]), (c0, c1, BMl[:, p, m0:m0 + (c1 - c0)])]))
            sr = 64 if par == 0 else 0
            at.unit(tiles, lambda c0, c1, p=p, r0=r0: QS[r0:r0 + 64, p // 2, c0:c1], 0.125, par,
                    AO[r0:r0 + 64, p // 2, :], extra_sum=sk[sr:sr + 1, p:p + 1])
        out_proj_block(cm, Wo, AO, 0, X, 0)
        cm.store_xT_block(X, 0, 4, out[j * TB:(j + 1) * TB, :], ostage, "os")
    k.pop_scope()


def emit_mla(ctx, x_src, nb, prev_src, npb, pos_own, pos_prev, prev_bias, out, g_attn, g_q, g_kv, MEMN,
             rc, tri, wi_in, wuq_in, wukv_in, wm_in, wo_in):
    k, cm, at = ctx.k, ctx.cm, ctx.at
    U = ctx.names()
    nt = nb * TB
    npv = npb * TB
    nkey = nt + npv
    nkt = nkey // 128
    k.push_scope()
    Wi = k.sb(U("wi"), [128, 8, 1088], BF16)
    Wukv = k.sb(U("wukv"), [128, 2, 1536], BF16)
    CKVN = k.sb(U("ckvn"), [128, 2, nkey], BF16)
    KTs = [k.sb(U("kt0"), [128, nkey], BF16), k.sb(U("kt1"), [128, nkey], BF16)]
    CQN = k.sb(U("cqn"), [128, 3, nt], BF16)
    CC = k.sb(U("cc"), [128, nt], F32)
    SS = k.sb(U("ss"), [128, nt], F32)
    AO = k.sb(U("ao"), [128, 8, nt], BF16)
    rt1 = k.sb(U("rt1"), [128, TB], F32)
    rt2 = k.sb(U("rt2"), [128, TB], F32)
    k.dma("pool", Wi[:], wi_in.rearrange("(kc p) f -> p kc f", p=128), "w1")
    k.dma("pool", Wukv[:], wukv_in.rearrange("(kc p) f -> p kc f", p=128), "w2")

    def rope(dsts, A, B, cc, ss):
        k.tt("dve", rt1[64:96, :], A, cc, ALU.mult)
        k.tt("dve", rt2[64:96, :], B, ss, ALU.mult)
        for dst in dsts:
            k.tt("pool", dst, rt1[64:96, :], rt2[64:96, :], ALU.add)

    k.push_scope()
    X = k.sb(U("xblk"), [128, 8, TB], F32)
    HT = k.sb(U("ht"), [128, 8, TB], BF16)
    stage = [k.sb(U(f"stg{i}"), [128, D], F32) for i in range(2)]
    QC = k.sb(U("qc"), [128, 2, TB], BF16)
    Kmem = k.sb(U("kmem"), [128, 2, N_MEM], BF16)
    Vmem = k.sb(U("vmem"), [128, 2, 4, 192], BF16)
    k.push_scope()
    Wm = k.sb(U("wm"), [128, 8, 512], BF16)
    k.dma("pool", Wm[:], wm_in.rearrange("(kc p) f -> p kc f", p=128), "w0")
    mem_kv(ctx, U, MEMN, Wm, Kmem, Vmem)
    k.pop_scope()
    posi = k.sb(U("posi"), [128, TB], I32)
    yv = k.sb(U("yv"), [128, TB], F32)
    yi = k.sb(U("yi"), [128, TB], I32)
    yf = k.sb(U("yf"), [128, TB], F32)
    fr = k.sb(U("fr"), [128, TB], F32)
    cmp_ = k.sb(U("cmp"), [128, TB], F32)
    CCt = k.sb(U("cct"), [128, TB], F32)
    SSt = k.sb(U("sst"), [128, TB], F32)

    def frac_sin(dst, add, scale_ap):
        src = yv
        if add != 0.0:
            k.ts("dve", yf[:], yv[:], add, ALU.add)
            src = yf
            k.copy("dve", yi[:], yf[:])
        else:
            k.copy("dve", yi[:], yv[:])
        k.copy("dve", fr[:], yi[:])
        k.tt("dve", fr[:], src[:], fr[:], ALU.subtract)
        k.ts("dve", cmp_[:], fr[:], 0.5, ALU.is_gt)
        k.tt("dve", fr[:], fr[:], cmp_[:], ALU.subtract)
        k.ts("dve", cmp_[:], fr[:], -0.5, ALU.is_lt)
        k.tt("dve", fr[:], fr[:], cmp_[:], ALU.add)
        k.act(dst, fr[:], AF.Sin, scale=scale_ap)

    def rope_tables(pos_slice, cc, ss):
        k.dma("sp", posi[:], pos_slice.partition_broadcast(128), "posd")
        k.copy("dve", yv[:], posi[:])
        k.ts("dve", yv[:], yv[:], rc[:, 0:1], ALU.mult)
        frac_sin(ss, 0.0, rc[:, 1:2])
        frac_sin(cc, 0.25, rc[:, 2:3])

    def latents_block(xsrc, t0, koff, pos_src, own):
        cm.load_xT_block(xsrc[t0:t0 + TB, :], X, 0, 4, stage, "xs")
        cm.norm_block(X, 0, TB, g_attn, HT, 0)
        if own:
            cc, ss = CC[:, t0:t0 + TB], SS[:, t0:t0 + TB]
        else:
            cc, ss = CCt[:, :], SSt[:, :]
        rope_tables(pos_src[t0:t0 + TB], cc, ss)
        if own:
            for c in range(3):
                for kc in range(8):
                    k.mm(cm.bank(c), Wi[:, kc, c * 128:(c + 1) * 128], HT[:, kc, :], start=(kc == 0), stop=(kc == 7))
            rstd = cm.rms_stats([cm.bank(c) for c in range(3)], TB, 1.0 / 384, banks=(5,), sq_eng="act")
            for c in range(3):
                k.stt("dve", CQN[:, c, t0:t0 + TB], cm.bank(c), g_q[:, c:c + 1], rstd, ALU.mult, ALU.mult)
        for c in range(2):
            for kc in range(8):
                k.mm(cm.bank(3 + c), Wi[:, kc, 384 + c * 128:384 + (c + 1) * 128], HT[:, kc, :], start=(kc == 0), stop=(kc == 7))
        rstd = cm.rms_stats([cm.bank(3 + c) for c in range(2)], TB, 1.0 / 256, banks=(5,), sq_eng="act")
        for c in range(2):
            k.stt("dve", CKVN[:, c, koff + t0:koff + t0 + TB], cm.bank(3 + c), g_kv[:, c:c + 1], rstd, ALU.mult, ALU.mult)
        for i, b in enumerate((6, 7)):
            for kc in range(8):
                k.mm(cm.bank(b, 0, TB, 0, 96), Wi[:, kc, 896 + i * 96:896 + (i + 1) * 96], HT[:, kc, :], start=(kc == 0), stop=(kc == 7))
        rope([KT_[64:96, koff + t0:koff + t0 + TB] for KT_ in KTs], cm.bank(6, 0, TB, 64, 96), cm.bank(7, 0, TB, 64, 96),
             cc[64:96, :], ss[64:96, :])
        if own:
            for c in range(2):
                proj_chunk(cm, Wi, 640 + c * 128, 128, HT, 0, TB, QC[:, c, :], "dve")
            cross_units(cm, at, Kmem, Vmem, QC, 0, AO, t0)

    for j in range(npb):
        latents_block(prev_src, j * TB, 0, pos_prev, False)
    for j in range(nb):
        latents_block(x_src, j * TB, npv, pos_own, True)
    k.pop_scope()

    k.push_scope()
    Wo = Wi
    k.dma("pool", Wo[:, :, 0:D], wo_in.rearrange("(kc p) f -> p kc f", p=128), "w3")
    VV = [k.sb(U("ve"), [128, nkt, 128], BF16), k.sb(U("vo"), [128, nkt, 128], BF16)]
    QT = [k.sb(U(f"qt{i}"), [128, nt], BF16) for i in range(2)]
    Wq = [k.sb(U(f"wq{i}"), [128, 3, 192], BF16) for i in range(2)]
    k.memset("pool", VV[0][:], 1.0)
    k.memset("pool", VV[1][:], 1.0)
    wuqv = wuq_in.rearrange("(kc p) f -> p kc f", p=128)
    P2 = (7, 6)
    zb = cm.zcol[:, 0:1]

    def prologue(h):
        par = h % 2
        wq, qt, Vh, KT = Wq[par], QT[par], VV[par], KTs[par]
        voff = 0 if par == 0 else 64
        k.dma("pool", wq[:], wuqv[:, :, h * 192:(h + 1) * 192], f"wq{par}")
        for j in range(nb):
            t0 = j * TB
            bA = cm.next_bank(P2)
            for kc in range(3):
                k.mm(cm.bank(bA, 0, TB, 0, 96), wq[:, kc, 0:96], CQN[:, kc, t0:t0 + TB], start=(kc == 0), stop=(kc == 2))
            bB = cm.next_bank(P2)
            for kc in range(3):
                k.mm(cm.bank(bB, 0, TB, 0, 96), wq[:, kc, 96:192], CQN[:, kc, t0:t0 + TB], start=(kc == 0), stop=(kc == 2))
            k.copy("dve", qt[0:64, t0:t0 + TB], cm.bank(bA, 0, TB, 0, 64))
            rope([qt[64:96, t0:t0 + TB]], cm.bank(bA, 0, TB, 64, 96), cm.bank(bB, 0, TB, 64, 96), CC[64:96, t0:t0 + TB], SS[64:96, t0:t0 + TB])
        for kb in range(nkey // TB):
            b = cm.next_bank(P2)
            for kc in range(2):
                k.mm(cm.bank(b, 0, TB, 0, 64), Wukv[:, kc, 128 * h:128 * h + 64], CKVN[:, kc, kb * TB:(kb + 1) * TB], start=(kc == 0), stop=(kc == 1))
            k.copy("dve", KT[0:64, kb * TB:(kb + 1) * TB], cm.bank(b, 0, TB, 0, 64))
        for g in range(nkt // 8):
            b = cm.next_bank(P2)
            for tl in range(8):
                kt_ = g * 8 + tl
                for kc in range(2):
                    k.mm(cm.bank(b, tl * 64, (tl + 1) * 64), CKVN[:, kc, kt_ * 128:(kt_ + 1) * 128],
                         Wukv[:, kc, 128 * h + 64:128 * h + 128], start=(kc == 0), stop=(kc == 1))
            k.copy("dve", Vh[:, g * 8:(g + 1) * 8, voff:voff + 64], cm.bank(b).rearrange("p (t d) -> p t d", t=8))

    def attention(h):
        par = h % 2
        r0 = par * 64
        qt, Vh, KT = QT[par], VV[par], KTs[par]
        for j in range(nb):
            t0 = j * TB
            tiles = []
            for kt_ in range(npb * 4):
                tiles.append(dict(K=KT[0:96, kt_ * 128:(kt_ + 1) * 128], V=Vh[:, kt_, :], c0=0, c1=TB, bias=prev_bias))
            for kt_ in range(npb * 4, npb * 4 + 4 * j):
                tiles.append(dict(K=KT[0:96, kt_ * 128:(kt_ + 1) * 128], V=Vh[:, kt_, :], c0=0, c1=TB, bias=zb))
            for t in range(4):
                kt_ = npb * 4 + 4 * j + t
                tiles.append(dict(K=KT[0:96, kt_ * 128:(kt_ + 1) * 128], V=Vh[:, kt_, :], c0=128 * t, c1=TB,
                                  bias=zb, masks=[(128 * t, 128 * t + 128, tri[:, :])]))
            at.unit(tiles, lambda c0, c1, qt=qt, t0=t0: qt[0:96, t0 + c0:t0 + c1], MLA_SCALE, par,
                    AO[r0:r0 + 64, h // 2, t0:t0 + TB])

    prologue(0)
    for h in range(12):
        if h + 1 < 12:
            prologue(h + 1)
        attention(h)
    k.pop_scope()

    k.push_scope()
    X = k.sb(U("xblk3"), [128, 8, TB], F32)
    stage = [k.sb(U(f"stg3{i}"), [128, D], F32) for i in range(2)]
    ostage = [k.sb(U("ostg3"), [128, D], F32)]
    for j in range(nb):
        cm.load_xT_block(x_src[j * TB:(j + 1) * TB, :], X, 0, 4, stage, "xs3")
        out_proj_block(cm, Wo, AO, j * TB, X, 0)
        cm.store_xT_block(X, 0, 4, out[j * TB:(j + 1) * TB, :], ostage, "os")
    k.pop_scope()
    k.pop_scope()


GCOLS = 96


def build_fused():
    k = KB()
    nc = k.nc
    x_in = k.dram_in("x", [NT, D])
    xp_in = k.dram_in("xprev", [NT, D])
    mem_in = k.dram_in("mem", [N_MEM, D])
    g_in = k.dram_in("g", [128, GCOLS])
    pos_in = k.dram_in("pos", [NT], I32)
    posp_in = k.dram_in("posprev", [NT], I32)
    rc_in = k.dram_in("ropec", [128, 4])
    tri_in = k.dram_in("trimask", [128, 128])
    pb_in = k.dram_in("prevbias", [128, 1])
    ident_in = k.dram_in("ident", [128, 128])
    sk_in = k.dram_in("sinks", [128, 24])
    bmh_in = k.dram_in("bmhi", [128, 12, 256])
    bml_in = k.dram_in("bmlo", [128, 12, 256])
    W = {}
    for l in range(4):
        W[f"wm{l}"] = k.dram_in(f"w_mem{l}", [D, 512])
        W[f"wo{l}"] = k.dram_in(f"w_o{l}", [D, D])
        W[f"up{l}"] = k.dram_in(f"w_up{l}", [D, DFF])
        W[f"dn{l}"] = k.dram_in(f"w_dn{l}", [DFF, D])
        if l % 2 == 0:
            W[f"wi{l}"] = k.dram_in(f"w_in{l}", [D, 1088])
            W[f"wuq{l}"] = k.dram_in(f"w_uq{l}", [384, 12 * 192])
            W[f"wukv{l}"] = k.dram_in(f"w_ukv{l}", [256, 1536])
        else:
            W[f"wi{l}"] = k.dram_in(f"w_in{l}", [D, 1536])
    y_out = k.dram_out("y", [NT, D])
    scr = {n: nc.dram_tensor("scr_" + n, [NT, D], F32).ap() for n in ("p1a", "p1b", "p1c", "p1d", "p1e", "p2a", "p2b")}
    cm = Common(k, {"ident": ident_in}, nbuf=1)
    at = Attn(cm)
    ctx = Ctx(k, cm, at)
    G = k.sb("G", [128, GCOLS], F32)
    rc = k.sb("rc", [128, 4], F32)
    pb = k.sb("pb", [128, 1], F32)
    negc = k.sb("negc", [128, 1], F32)
    sk = k.sb("sk", [128, 24], F32)
    tri = k.sb("tri", [128, 128], BF16)
    MEMN = k.sb("memn", [128, 8, N_MEM], BF16)
    k.dma("sp", G[:], g_in, "const")
    k.dma("sp", rc[:], rc_in, "const")
    k.dma("sp", pb[:], pb_in, "const")
    k.dma("sp", sk[:], sk_in, "const")
    k.dma("pool", tri[:], tri_in, "constp")
    k.memset("dve", negc[:], NEGB)
    k.act(sk[:], sk[:], AF.Exp)
    k.push_scope()
    mT = k.sb("memT", [128, 8, N_MEM], F32)
    mstage = [k.sb(f"mstg{i}", [128, D], F32) for i in range(2)]
    cm.load_xT_block(mem_in, mT, 0, 2, mstage, "xs")
    cm.norm_block(mT, 0, N_MEM, G[:, 64:72], MEMN, 0)
    k.pop_scope()

    def ga(l):
        return G[:, l * 16:l * 16 + 8]

    def gm(l):
        return G[:, l * 16 + 8:l * 16 + 16]

    def mla(l, x_src, nb, prev_src, npb, pos_own, pos_prev, prev_bias, out):
        j = l // 2
        emit_mla(ctx, x_src, nb, prev_src, npb, pos_own, pos_prev, prev_bias, out, ga(l),
                 G[:, 80 + 5 * j:83 + 5 * j], G[:, 83 + 5 * j:85 + 5 * j], MEMN, rc, tri,
                 W[f"wi{l}"], W[f"wuq{l}"], W[f"wukv{l}"], W[f"wm{l}"], W[f"wo{l}"])

    def swa(l, x_src, nb, bnd_src, bnd_bias, out):
        j = l // 2
        emit_swa(ctx, x_src, nb, bnd_src, bnd_bias, out, ga(l), MEMN, W[f"wi{l}"], W[f"wm{l}"], W[f"wo{l}"],
                 sk[:, 12 * j:12 * j + 12], bmh_in, bml_in)

    def mlp(l, x_src, nb, out, final=False):
        emit_mlp(ctx, x_src, nb, out, gm(l), W[f"up{l}"], W[f"dn{l}"], G[:, 72:80] if final else None)

    z = cm.zcol[:, 0:1]
    mla(0, xp_in, NB, None, 0, posp_in, None, z, scr["p1a"])
    mlp(0, scr["p1a"], NB, scr["p1b"])
    swa(1, scr["p1b"], NB, None, negc[:, 0:1], scr["p1a"])
    mlp(1, scr["p1a"], NB, scr["p1c"])
    mla(2, scr["p1c"][3 * TB:4 * TB, :], 1, scr["p1c"][0:3 * TB, :], 3, posp_in[3 * TB:4 * TB], posp_in[0:3 * TB], z,
        scr["p1d"][0:TB, :])
    mlp(2, scr["p1d"][0:TB, :], 1, scr["p1e"][0:TB, :])
    mla(0, x_in, NB, xp_in, NB, pos_in, posp_in, pb[:, 0:1], scr["p2a"])
    mlp(0, scr["p2a"], NB, scr["p2b"])
    swa(1, scr["p2b"], NB, scr["p1b"][NT - 128:NT, :], pb[:, 0:1], scr["p2a"])
    mlp(1, scr["p2a"], NB, scr["p2b"])
    mla(2, scr["p2b"], NB, scr["p1c"], NB, pos_in, posp_in, pb[:, 0:1], scr["p2a"])
    mlp(2, scr["p2a"], NB, scr["p2b"])
    swa(3, scr["p2b"], NB, scr["p1e"][TB - 128:TB, :], pb[:, 0:1], scr["p2a"])
    mlp(3, scr["p2a"], NB, y_out, final=True)
    k.finish()
    return k


N_CORES = 8


def _f(a):
    return np.ascontiguousarray(a, dtype=np.float32)


def kernel(x, mem, positions, attn_norm_g, mlp_norm_g, mem_norm_g, final_norm_g,
           mla_w_in, mla_q_norm_g, mla_kv_norm_g, mla_w_uq, mla_w_ukv,
           swa_w_in, swa_sinks, w_mem_kv, w_o, mlp_w_up, mlp_w_down):
    x = np.asarray(x, dtype=np.float32)
    mem = np.asarray(mem, dtype=np.float32)
    positions = np.asarray(positions)
    B, S, _ = x.shape
    shared = {}
    g = np.zeros((128, GCOLS), np.float32)
    for l in range(4):
        g[:, l * 16:l * 16 + 8] = np.asarray(attn_norm_g[l]).reshape(8, 128).T
        g[:, l * 16 + 8:l * 16 + 16] = np.asarray(mlp_norm_g[l]).reshape(8, 128).T
    g[:, 64:72] = np.asarray(mem_norm_g).reshape(8, 128).T
    g[:, 72:80] = np.asarray(final_norm_g).reshape(8, 128).T
    for j in range(2):
        g[:, 80 + 5 * j:83 + 5 * j] = np.asarray(mla_q_norm_g[j]).reshape(3, 128).T
        g[:, 83 + 5 * j:85 + 5 * j] = np.asarray(mla_kv_norm_g[j]).reshape(2, 128).T
    shared["g"] = g
    r = np.arange(128)
    inv = 10000.0 ** (-(np.arange(16, dtype=np.float64) * 2.0) / 32)
    rcv = np.zeros((128, 4), np.float64)
    rcv[:, 0] = inv[r % 16] / (2 * np.pi)
    rcv[:, 1] = np.where((r % 32) < 16, -1.0, 1.0) * 2 * np.pi
    rcv[:, 2] = 2 * np.pi
    shared["ropec"] = _f(rcv)
    kk = np.arange(128)[:, None]
    cc = np.arange(128)[None, :]
    shared["trimask"] = _f(np.where(kk <= cc, 0.0, 8.0 * NEGB))
    shared["ident"] = np.eye(128, dtype=np.float32)
    sk = np.concatenate([np.asarray(swa_sinks[j])[SWA_POS] for j in range(2)])
    shared["sinks"] = _f(np.broadcast_to(sk[None, :], (128, 24)))
    hi, lo = alibi_tables()
    shared["bmhi"], shared["bmlo"] = _f(hi), _f(lo)
    for l in range(4):
        j = l // 2
        shared[f"w_mem{l}"] = _f(w_mem_kv[l])
        shared[f"w_up{l}"] = _f(mlp_w_up[l])
        shared[f"w_dn{l}"] = _f(mlp_w_down[l])
        if l % 2 == 0:
            w_in = np.asarray(mla_w_in[j])
            kr = w_in[:, 640:672]
            krp = np.concatenate([kr[:, 16:32], kr[:, 0:16]], axis=1)
            pad = w_in[:, 576:640]
            shared[f"w_in{l}"] = _f(np.concatenate([w_in[:, 0:640], w_in[:, 672:928], pad, kr, pad, krp], axis=1))
            w_uq = np.asarray(mla_w_uq[j])
            hq = []
            for h in range(12):
                wh = w_uq[:, h * 96:(h + 1) * 96]
                hq += [wh, wh[:, 0:64], wh[:, 80:96], wh[:, 64:80]]
            shared[f"w_uq{l}"] = _f(np.concatenate(hq, axis=1))
            shared[f"w_ukv{l}"] = _f(mla_w_ukv[j])
            shared[f"w_o{l}"] = _f(w_o[l])
        else:
            w_in = np.asarray(swa_w_in[j])
            wq = np.concatenate([w_in[:, h * 64:(h + 1) * 64] for h in SWA_POS], axis=1)
            shared[f"w_in{l}"] = _f(np.concatenate([wq, w_in[:, 768:]], axis=1))
            wo = np.asarray(w_o[l])
            shared[f"w_o{l}"] = _f(np.concatenate([wo[h * 64:(h + 1) * 64] for h in SWA_POS] + [wo[768:]], axis=0))
    in_maps = []
    for c in range(N_CORES):
        b, half = c // 2, c % 2
        m = dict(shared)
        m["x"] = _f(x[b, half * NT:(half + 1) * NT])
        m["xprev"] = _f(x[b, 0:NT])
        m["mem"] = _f(mem[b])
        m["pos"] = np.ascontiguousarray(positions[b, half * NT:(half + 1) * NT], dtype=np.int32)
        m["posprev"] = np.ascontiguousarray(positions[b, 0:NT], dtype=np.int32)
        m["prevbias"] = np.full((128, 1), 0.0 if half else NEGB, np.float32)
        in_maps.append(m)
    kb = build_fused()
    res = run_bass_kernel_spmd(kb.nc, in_maps, core_ids=list(range(N_CORES)))
    out = np.empty((B, S, D), np.float32)
    for c in range(N_CORES):
        out[c // 2, (c % 2) * NT:(c % 2 + 1) * NT] = np.asarray(res.results[c]["y"])
    return out
```

```python
import numpy as np
from contextlib import ExitStack
import concourse.bass as bass
import concourse.mybir as mybir
from concourse.bass_utils import run_bass_kernel_spmd

F32, BF16, I32 = mybir.dt.float32, mybir.dt.bfloat16, mybir.dt.int32
ALU = mybir.AluOpType
AF = mybir.ActivationFunctionType

D = 1024
NT = 2048
TB = 512
NB = NT // TB
DFF = 4096
EPS = 1e-6
NEGB = -30000.0
N_MEM = 256


class KB:
    def __init__(self):
        self.nc = bass.Bass("TRN2", target_bir_lowering=False)
        nc = self.nc
        self.E = dict(pe=nc.tensor, act=nc.scalar, dve=nc.vector, pool=nc.gpsimd, sp=nc.sync)
        self.csem = {e: nc.alloc_semaphore("s_" + e) for e in ("pe", "act", "dve", "pool")}
        self.cnt = {e: 0 for e in self.csem}
        self.waited = {e: {} for e in self.E}
        self.trk = {}
        self.W = {}
        self.dsem = {}
        self.dcnt = {}
        self.semh = {}
        for e, h in self.csem.items():
            self.semh["c_" + e] = h
        self.es = ExitStack()
        self.n_ins = 0

    def sb(self, name, shape, dtype):
        t = self.es.enter_context(self.nc.sbuf_tensor(name, list(shape), dtype))
        self.W[name] = int(np.prod(shape[1:]))
        self.trk[name] = []
        return t

    def ps(self, name, shape, dtype=F32):
        t = self.es.enter_context(self.nc.psum_tensor(name, list(shape), dtype))
        self.W[name] = int(np.prod(shape[1:]))
        self.trk[name] = []
        return t

    def push_scope(self):
        self._scopes = getattr(self, "_scopes", [])
        self._scopes.append(self.es)
        self.es = ExitStack()

    def pop_scope(self):
        self.barrier()
        self.es.close()
        self.es = self._scopes.pop()

    def barrier(self):
        toks = [("c_" + e, c) for e, c in self.cnt.items() if c > 0]
        toks += [("d_" + s, c * 16) for s, c in self.dcnt.items() if c > 0]
        for e in self.E:
            self._wait(e, toks)

    def dram_in(self, name, shape, dtype=F32):
        return self.nc.dram_tensor(name, list(shape), dtype, kind="ExternalInput").ap()

    def dram_out(self, name, shape, dtype=F32):
        return self.nc.dram_tensor(name, list(shape), dtype, kind="ExternalOutput").ap()

    def dma_sem(self, name):
        if name not in self.dsem:
            self.dsem[name] = self.nc.alloc_semaphore("d_" + name)
            self.dcnt[name] = 0
            self.semh["d_" + name] = self.dsem[name]
        return name

    def _region(self, ap):
        name = ap.tensor.name
        if name not in self.W:
            return None
        W = self.W[name]
        off = int(ap.offset)
        a = ap.ap
        p0 = off // W
        f0 = off % W
        p1 = p0 + a[0][1]
        hi = f0 + sum((c - 1) * s for s, c in a[1:]) + 1
        if name.startswith("ps"):
            return name, 0, 128, (f0 // 512) * 512, ((hi + 511) // 512) * 512
        return name, p0, p1, f0, hi

    def _deps(self, reads, writes):
        toks = set()
        for ap in reads:
            r = self._region(ap)
            if r is None:
                continue
            name, p0, p1, lo, hi = r
            for e in self.trk[name]:
                if e[4] and e[0] < p1 and p0 < e[1] and e[2] < hi and lo < e[3]:
                    toks.add(e[5])
        for ap in writes:
            r = self._region(ap)
            if r is None:
                continue
            name, p0, p1, lo, hi = r
            for e in self.trk[name]:
                if e[0] < p1 and p0 < e[1] and e[2] < hi and lo < e[3]:
                    toks.add(e[5])
        return toks

    def _record(self, reads, writes, tok):
        for ap in writes:
            r = self._region(ap)
            if r is None:
                continue
            name, p0, p1, lo, hi = r
            lst = self.trk[name]
            lst[:] = [e for e in lst if not (p0 <= e[0] and e[1] <= p1 and lo <= e[2] and e[3] <= hi)]
            lst.append([p0, p1, lo, hi, True, tok])
        for ap in reads:
            r = self._region(ap)
            if r is None:
                continue
            name, p0, p1, lo, hi = r
            lst = self.trk[name]
            for e in lst:
                if (not e[4]) and e[0] == p0 and e[1] == p1 and e[2] == lo and e[3] == hi and e[5][0] == tok[0]:
                    e[5] = tok
                    break
            else:
                lst.append([p0, p1, lo, hi, False, tok])

    def _wait(self, eng, toks):
        best = {}
        for s, v in toks:
            if s.startswith("d_"):
                v = max(v, self.dcnt[s[2:]] * 16)
            if v > best.get(s, 0):
                best[s] = v
        for s, v in best.items():
            if eng == "pe" and s == "c_pe":
                continue
            if self.waited[eng].get(s, 0) >= v:
                continue
            self.E[eng].wait_ge(self.semh[s], v)
            self.waited[eng][s] = v
            self.n_ins += 1

    def op(self, eng, fn, reads=(), writes=()):
        toks = self._deps(reads, writes)
        self._wait(eng, toks)
        ins = fn()
        self.cnt[eng] += 1
        tok = ("c_" + eng, self.cnt[eng])
        ins.then_inc(self.csem[eng], 1)
        self._record(reads, writes, tok)
        self.n_ins += 1
        return tok

    def dma(self, q, out, in_, sem):
        self.dma_sem(sem)
        toks = self._deps([in_], [out])
        self._wait(q, toks)
        ins = self.E[q].dma_start(out=out, in_=in_)
        self.dcnt[sem] += 1
        tok = ("d_" + sem, self.dcnt[sem] * 16)
        ins.then_inc(self.dsem[sem], 16)
        self._record([in_], [out], tok)
        self.n_ins += 1
        return tok

    def wait_all_dma(self, eng):
        toks = [("d_" + s, c * 16) for s, c in self.dcnt.items() if c > 0]
        self._wait(eng, toks)

    def mm(self, out, lhsT, rhs, start=True, stop=True, extra_reads=()):
        return self.op("pe", lambda: self.nc.tensor.matmul(out, lhsT, rhs, start=start, stop=stop,
                                                          skip_group_check=True),
                       reads=[lhsT, rhs, *extra_reads], writes=[out])

    def tr(self, out, in_, ident):
        return self.op("pe", lambda: self.nc.tensor.transpose(out, in_, ident),
                       reads=[in_, ident], writes=[out])

    def act(self, out, in_, func, bias=None, scale=1.0, eng="act"):
        reads = [in_]
        kw = {}
        if bias is not None:
            kw["bias"] = bias
            if not isinstance(bias, (int, float)):
                reads.append(bias)
        if not isinstance(scale, (int, float)):
            reads.append(scale)
        return self.op("act", lambda: self.nc.scalar.activation(out=out, in_=in_, func=func, scale=scale, **kw),
                       reads=reads, writes=[out])

    def copy(self, eng, out, in_):
        if eng == "act":
            return self.op("act", lambda: self.nc.scalar.copy(out=out, in_=in_), reads=[in_], writes=[out])
        return self.op(eng, lambda: self.E[eng].tensor_copy(out=out, in_=in_), reads=[in_], writes=[out])

    def tt(self, eng, out, in0, in1, op):
        return self.op(eng, lambda: self.E[eng].tensor_tensor(out=out, in0=in0, in1=in1, op=op),
                       reads=[in0, in1], writes=[out])

    def ts(self, eng, out, in0, s1, op0, s2=None, op1=None):
        reads = [in0] + [s for s in (s1, s2) if s is not None and not isinstance(s, (int, float))]
        if op1 is None:
            return self.op(eng, lambda: self.E[eng].tensor_scalar(out=out, in0=in0, scalar1=s1, scalar2=None, op0=op0),
                           reads=reads, writes=[out])
        return self.op(eng, lambda: self.E[eng].tensor_scalar(out=out, in0=in0, scalar1=s1, scalar2=s2, op0=op0, op1=op1),
                       reads=reads, writes=[out])

    def stt(self, eng, out, in0, scalar, in1, op0, op1):
        reads = [in0, in1] + ([] if isinstance(scalar, (int, float)) else [scalar])
        return self.op(eng, lambda: self.E[eng].scalar_tensor_tensor(out=out, in0=in0, scalar=scalar, in1=in1,
                                                                    op0=op0, op1=op1),
                       reads=reads, writes=[out])

    def memset(self, eng, out, val):
        return self.op(eng, lambda: self.E[eng].memset(out, val), reads=[], writes=[out])

    def finish(self):
        self.wait_all_dma("sp")
        self.es.close()


class Common:
    def __init__(self, kb, consts_dram, nbuf=2):
        self.kb = kb
        k = kb
        self.PS = [k.ps(f"ps{i}", [128, 1024]) for i in range(4)]
        self.rr = 0
        self.identf = k.sb("identf", [128, 128], F32)
        self.identb = k.sb("identb", [128, 128], BF16)
        self.onesb = k.sb("onesb", [128, 128], BF16)
        self.onesf = k.sb("onesf", [128, 128], F32)
        self.zcol = k.sb("zcol", [128, 1], F32)
        self.epscol = k.sb("epscol", [128, 1], F32)
        k.dma("sp", self.identf[:], consts_dram["ident"], "const")
        k.dma("pool", self.identb[:], consts_dram["ident"], "constp")
        k.memset("dve", self.onesb[:], 1.0)
        k.memset("dve", self.onesf[:], 1.0)
        k.memset("dve", self.zcol[:], 0.0)
        k.memset("dve", self.epscol[:], EPS)
        self.nbuf = nbuf
        self.sq = [k.sb(f"sq{i}", [128, 8, TB], BF16) for i in range(nbuf)]
        self.lnv = [k.sb(f"lnv{i}", [128, TB], F32) for i in range(nbuf)]
        self.rstd = [k.sb(f"rstd{i}", [128, TB], F32) for i in range(nbuf)]
        self.nrm_i = 0

    def bank(self, b, lo=0, hi=512, p0=0, p1=128):
        return self.PS[b // 2][p0:p1, (b % 2) * 512 + lo:(b % 2) * 512 + hi]

    def next_bank(self, banks=(0, 1, 2, 3, 4, 5, 6, 7)):
        b = banks[self.rr % len(banks)]
        self.rr += 1
        return b

    def rms_stats(self, chunks, w, inv_n, banks=(0, 1, 2, 3, 4, 5, 6, 7), sq_eng="pool"):
        if sq_eng == "act":
            return self._rms_stats_act(chunks, w, inv_n, banks)
        return self._rms_stats(chunks, w, inv_n, banks, sq_eng)

    def _rms_stats_act(self, chunks, w, inv_n, banks):
        k = self.kb
        i = self.nrm_i % self.nbuf
        self.nrm_i += 1
        sq, lnv, rstd = self.sq[i], self.lnv[i], self.rstd[i]
        b = self.next_bank(banks)
        n = len(chunks)
        for c, ap in enumerate(chunks):
            k.act(sq[:, c, 0:w], ap, AF.Square)
        for c in range(n):
            k.mm(self.bank(b, 0, w), self.onesb[:, :], sq[:, c, 0:w], start=(c == 0), stop=(c == n - 1))
        k.act(lnv[:, 0:w], self.bank(b, 0, w), AF.Ln, bias=self.epscol[:, 0:1], scale=inv_n)
        k.act(rstd[:, 0:w], lnv[:, 0:w], AF.Exp, scale=-0.5)
        return rstd[:, 0:w]

    def _rms_stats(self, chunks, w, inv_n, banks=(0, 1, 2, 3, 4, 5, 6, 7), sq_eng="pool"):
        k = self.kb
        i = self.nrm_i % self.nbuf
        self.nrm_i += 1
        sq, lnv, rstd = self.sq[i], self.lnv[i], self.rstd[i]
        b = self.next_bank(banks)
        n = len(chunks)
        for c, ap in enumerate(chunks):
            k.tt(sq_eng, sq[:, c, 0:w], ap, ap, ALU.mult)
        for c in range(n):
            k.mm(self.bank(b, 0, w), self.onesb[:, :], sq[:, c, 0:w], start=(c == 0), stop=(c == n - 1))
        k.act(lnv[:, 0:w], self.bank(b, 0, w), AF.Ln, bias=self.epscol[:, 0:1], scale=inv_n)
        k.act(rstd[:, 0:w], lnv[:, 0:w], AF.Exp, scale=-0.5)
        return rstd[:, 0:w]

    def load_xT_block(self, x_rows, dstT, t0, ntiles, stage, sem_prefix, dst_chunks=8):
        k = self.kb
        for i in range(ntiles):
            st = stage[i % len(stage)]
            k.dma("sp", st[:], x_rows[i * 128:(i + 1) * 128, :], f"{sem_prefix}{i % len(stage)}")
            for h in range(2):
                b = self.next_bank()
                for cc in range(4):
                    c = h * 4 + cc
                    k.tr(self.bank(b, cc * 128, cc * 128 + 128), st[:, c * 128:(c + 1) * 128], self.identf[:])
                src = self.bank(b).rearrange("p (c t) -> p c t", c=4)
                dst = dstT[:, h * 4:h * 4 + 4, t0 + i * 128:t0 + (i + 1) * 128]
                k.copy("act" if (i + h) % 2 == 0 else "dve", dst, src)

    def store_xT_block(self, srcT, t0, ntiles, out_rows, stage, sem_prefix):
        k = self.kb
        for i in range(ntiles):
            st = stage[i % len(stage)]
            for h in range(2):
                b = self.next_bank()
                for cc in range(4):
                    c = h * 4 + cc
                    k.tr(self.bank(b, cc * 128, cc * 128 + 128), srcT[:, c, t0 + i * 128:t0 + (i + 1) * 128],
                         self.identf[:])
                k.copy("act" if (i + h) % 2 == 0 else "dve", st[:, h * 512:(h + 1) * 512], self.bank(b))
            k.dma("sp", out_rows[i * 128:(i + 1) * 128, :], st[:], f"{sem_prefix}{i % len(stage)}")

    def norm_block(self, xT, t0, w, gcols, outT, o0, out_dtype_bf16=True):
        k = self.kb
        rstd = self.rms_stats([xT[:, c, t0:t0 + w] for c in range(8)], w, 1.0 / D)
        for c in range(8):
            k.stt("dve", outT[:, c, o0:o0 + w], xT[:, c, t0:t0 + w], gcols[:, c:c + 1], rstd, ALU.mult, ALU.mult)


class Attn:
    def __init__(self, cm, npt=4):
        k = cm.kb
        self.cm = cm
        self.PT = [k.sb(f"pt{i}", [128, 2 * TB], BF16) for i in range(3)]
        self.rsum = k.sb("rsum", [128, TB], F32)
        self.bc = k.sb("bcs", [128, TB], F32)
        self.pi = 0
        self.si = 0
        self.oi = 0

    def unit(self, tiles, qfn, scale, par, dst, extra_sum=None):
        cm, k = self.cm, self.cm.kb
        nc = k.nc
        ob = 4 + (self.oi % 2)
        self.oi += 1
        n = len(tiles)
        LOOK = 2
        pts = {}

        def full(t):
            return t["c0"] == 0 and t["c1"] == TB and not t.get("masks")

        def same(a, b):
            return a.tensor.name == b.tensor.name and int(a.offset) == int(b.offset)

        steps = []
        i = 0
        while i < n:
            t = tiles[i]
            if i + 1 < n and full(t) and full(tiles[i + 1]) and same(t["bias"], tiles[i + 1]["bias"]):
                steps.append([i, i + 1])
                i += 2
            else:
                steps.append([i])
                i += 1

        def s_stage(si_):
            step = steps[si_]
            pi_ = self.si % 2
            self.si += 1
            for idx, ti in enumerate(step):
                t = tiles[ti]
                c0, c1 = t["c0"], t["c1"]
                sb_ = 2 * pi_ + idx
                masks = t.get("masks", ())
                k.mm(cm.bank(sb_, c0, c1), t["K"], qfn(c0, c1), start=True, stop=(len(masks) == 0))
                for mi, (m0, m1, mrhs) in enumerate(masks):
                    k.mm(cm.bank(sb_, m0, m1), cm.identb[:, :], mrhs, start=False, stop=(mi == len(masks) - 1))
            pt = self.PT[self.pi % len(self.PT)]
            self.pi += 1
            t0_ = tiles[step[0]]
            if len(step) == 2:
                k.act(pt[:, 0:2 * TB], cm.PS[pi_][:, 0:2 * TB], AF.Exp, bias=t0_["bias"], scale=scale)
            else:
                c0, c1 = t0_["c0"], t0_["c1"]
                k.act(pt[:, c0:c1], cm.bank(2 * pi_, c0, c1), AF.Exp, bias=t0_["bias"], scale=scale)
            pts[si_] = pt

        def pv_stage(si_):
            step = steps[si_]
            for idx, ti in enumerate(step):
                t = tiles[ti]
                c0, c1 = t["c0"], t["c1"]
                k.mm(cm.bank(ob, c0, c1), t["V"], pts[si_][:, idx * TB + c0:idx * TB + c1], start=(ti == 0), stop=(ti == n - 1))

        ns = len(steps)
        for i in range(ns + LOOK):
            if i < ns:
                s_stage(i)
            if i - LOOK >= 0:
                pv_stage(i - LOOK)
        sr = 64 if par == 0 else 0
        r0 = 0 if par == 0 else 64
        rs = self.rsum[sr:sr + 1, :]
        src = cm.bank(ob, 0, TB, sr, sr + 1)
        k.act(rs, src, AF.Ln, bias=(extra_sum if extra_sum is not None else cm.zcol[sr:sr + 1, 0:1]), scale=1.0)
        k.act(rs, rs, AF.Exp, scale=-1.0)
        k.mm(cm.bank(6), cm.onesf[sr:sr + 1, 0:128], rs, start=True, stop=True)
        k.copy("act", self.bc[r0:r0 + 64, :], cm.bank(6, 0, TB, r0, r0 + 64))
        k.tt("dve", dst, cm.bank(ob, 0, TB, r0, r0 + 64), self.bc[r0:r0 + 64, :], ALU.mult)


PROJ_BANKS = (0, 1, 2, 3, 7)


def proj_chunk(cm, W, c0, M, HT, h0, w, dst, eng, nk=8):
    k = cm.kb
    b = cm.next_bank(PROJ_BANKS)
    for kc in range(nk):
        k.mm(cm.bank(b, 0, w, 0, M), W[:, kc, c0:c0 + M], HT[:, kc, h0:h0 + w], start=(kc == 0), stop=(kc == nk - 1))
    if dst is not None:
        k.copy(eng, dst, cm.bank(b, 0, w, 0, M))
    return b


def mem_setup(cm, mem_in, gt_mem, Wm, stage):
    k = cm.kb
    memT = k.sb("memT", [128, 8, N_MEM], F32)
    MEMN = k.sb("memn", [128, 8, N_MEM], BF16)
    Kmem = k.sb("kmem", [128, 2, N_MEM], BF16)
    Vmem = k.sb("vmem", [128, 2, 4, 192], BF16)
    cm.load_xT_block(mem_in, memT, 0, 2, stage, "xs")
    cm.norm_block(memT, 0, N_MEM, gt_mem, MEMN, 0)
    k.memset("pool", Vmem[:], 1.0)
    for pr in range(2):
        proj_chunk(cm, Wm, pr * 128, 128, MEMN, 0, N_MEM, Kmem[:, pr, :], "act")
    for t in range(2):
        b = cm.next_bank(PROJ_BANKS)
        for kc in range(8):
            k.mm(cm.bank(b, 0, 256), MEMN[:, kc, t * 128:(t + 1) * 128], Wm[:, kc, 256:512], start=(kc == 0), stop=(kc == 7))
        k.copy("dve", Vmem[:, t, :, 64:128], cm.bank(b, 0, 256).rearrange("p (h d) -> p h d", h=4))
    return Kmem, Vmem


def cross_units(cm, at, Kmem, Vmem, QC, q0, AO, a0):
    for ch in range(4):
        par = ch % 2
        r0 = par * 64
        tiles = []
        for t in range(2):
            V = Vmem[:, t, ch, 64:192] if par == 0 else Vmem[:, t, ch, 0:128]
            tiles.append(dict(K=Kmem[r0:r0 + 64, ch // 2, t * 128:(t + 1) * 128], V=V, c0=0, c1=TB,
                              bias=cm.zcol[:, 0:1]))
        at.unit(tiles, lambda c0, c1, ch=ch, r0=r0: QC[r0:r0 + 64, ch // 2, q0 + c0:q0 + c1], 0.125, par,
                AO[r0:r0 + 64, 6 + ch // 2, a0:a0 + TB])


def out_proj_block(cm, Wo, AO, a0, X, x0):
    k = cm.kb
    for co in range(8):
        b = cm.next_bank(PROJ_BANKS)
        for kc in range(8):
            k.mm(cm.bank(b), Wo[:, kc, co * 128:(co + 1) * 128], AO[:, kc, a0:a0 + TB], start=(kc == 0), stop=(kc == 7))
        k.tt("dve", X[:, co, x0:x0 + TB], X[:, co, x0:x0 + TB], cm.bank(b), ALU.add)


SWA_POS = [0, 3, 1, 4, 2, 5, 6, 9, 7, 10, 8, 11]


def alibi_tables():
    slopes = 2.0 ** (-8.0 * (np.arange(12, dtype=np.float64) + 1.0) / 12)
    kk = np.arange(128)[:, None]
    c = np.arange(256)[None, :]
    d = (c - kk).astype(np.float64)
    valid = (d >= 0) & (d < 128)
    out = np.zeros((128, 12, 256), np.float64)
    for p in range(12):
        out[:, p, :] = np.where(valid, -8.0 * slopes[SWA_POS[p]] * d, 8.0 * NEGB)
    import ml_dtypes
    hi = out.astype(np.float32).astype(ml_dtypes.bfloat16).astype(np.float32)
    lo = (out - hi).astype(np.float32).astype(ml_dtypes.bfloat16).astype(np.float32)
    return hi, lo


MLA_SCALE = 96.0 ** -0.5


class Ctx:
    def __init__(self, k, cm, at):
        self.k, self.cm, self.at = k, cm, at
        self.uid = 0

    def names(self):
        self.uid += 1
        u = self.uid
        return lambda n: f"{n}_{u}"


def mem_kv(ctx, U, MEMN, Wm, Kmem=None, Vmem=None):
    k, cm = ctx.k, ctx.cm
    if Kmem is None:
        Kmem = k.sb(U("kmem"), [128, 2, N_MEM], BF16)
        Vmem = k.sb(U("vmem"), [128, 2, 4, 192], BF16)
    k.memset("pool", Vmem[:], 1.0)
    for pr in range(2):
        proj_chunk(cm, Wm, pr * 128, 128, MEMN, 0, N_MEM, Kmem[:, pr, :], "act")
    for t in range(2):
        b = cm.next_bank(PROJ_BANKS)
        for kc in range(8):
            k.mm(cm.bank(b, 0, 256), MEMN[:, kc, t * 128:(t + 1) * 128], Wm[:, kc, 256:512], start=(kc == 0), stop=(kc == 7))
        k.copy("dve", Vmem[:, t, :, 64:128], cm.bank(b, 0, 256).rearrange("p (h d) -> p h d", h=4))
    return Kmem, Vmem


def emit_mlp(ctx, x_src, nb, out, gcol, up_in, dn_in, final_gcol=None):
    k, cm = ctx.k, ctx.cm
    U = ctx.names()
    k.push_scope()
    upv = up_in.rearrange("(kc p) f -> p kc f", p=128)
    dnv = dn_in.rearrange("(fc p) d -> p fc d", p=128)
    wup = [k.sb(U(f"wup{i}"), [128, 8, 512], BF16) for i in range(2)]
    wdn = [k.sb(U(f"wdn{i}"), [128, 4, D], BF16) for i in range(2)]
    X = k.sb(U("xT"), [128, 8, nb * TB], F32)
    H = k.sb(U("hT"), [128, 8, nb * TB], BF16)
    stage = [k.sb(U(f"stg{i}"), [128, D], F32) for i in range(2)]
    ostage = [k.sb(U("ostg"), [128, D], F32)]
    rl = [k.sb(U(f"rl{i}"), [128, TB], F32) for i in range(2)]
    aT = [k.sb(U(f"aT{i}"), [128, 4, TB], BF16) for i in range(2)]
    ri = 0

    def wload(fb):
        k.dma("pool", wup[fb % 2][:], upv[:, :, fb * 512:(fb + 1) * 512], f"wup{fb % 2}")
        k.dma("pool", wdn[fb % 2][:], dnv[:, fb * 4:(fb + 1) * 4, :], f"wdn{fb % 2}")

    def up_stage(fb, j, A):
        nonlocal ri
        for q in range(4):
            b = cm.next_bank((0, 1, 2, 3))
            for kc in range(8):
                k.mm(cm.bank(b), wup[fb % 2][:, kc, q * 128:(q + 1) * 128], H[:, kc, j * TB:(j + 1) * TB],
                     start=(kc == 0), stop=(kc == 7))
            r = rl[ri % 2]
            ri += 1
            k.act(r[:], cm.bank(b), AF.Relu)
            k.tt("pool", A[:, q, :], r[:], r[:], ALU.mult)

    def down_stage(fb, j, A):
        for c in range(8):
            b = cm.next_bank((4, 5, 6, 7))
            for q in range(4):
                k.mm(cm.bank(b), wdn[fb % 2][:, q, c * 128:(c + 1) * 128], A[:, q, :], start=(q == 0), stop=(q == 3))
            k.tt("dve", X[:, c, j * TB:(j + 1) * TB], X[:, c, j * TB:(j + 1) * TB], cm.bank(b), ALU.add)
        if fb == 7:
            if final_gcol is not None:
                rstd = cm.rms_stats([X[:, c, j * TB:(j + 1) * TB] for c in range(8)], TB, 1.0 / D)
                for c in range(8):
                    k.stt("dve", X[:, c, j * TB:(j + 1) * TB], X[:, c, j * TB:(j + 1) * TB], final_gcol[:, c:c + 1], rstd,
                          ALU.mult, ALU.mult)
            cm.store_xT_block(X, j * TB, 4, out[j * TB:(j + 1) * TB, :], ostage, "os")

    wload(0)
    wload(1)
    seq = [(fb, j) for fb in range(8) for j in range(nb)]
    for i in range(len(seq) + 1):
        if i < len(seq):
            fb, j = seq[i]
            if fb == 0:
                cm.load_xT_block(x_src[j * TB:(j + 1) * TB, :], X, j * TB, 4, stage, "xs")
                cm.norm_block(X, j * TB, TB, gcol, H, j * TB)
            up_stage(fb, j, aT[i % 2])
        if i >= 1:
            fbp, jp = seq[i - 1]
            down_stage(fbp, jp, aT[(i - 1) % 2])
            if jp == nb - 1 and fbp + 2 < 8:
                wload(fbp + 2)
    k.pop_scope()


def emit_swa(ctx, x_src, nb, bnd_src, bnd_bias, out, gcol, MEMN, wi_in, wm_in, wo_in, sk, bmh_in, bml_in):
    k, cm, at = ctx.k, ctx.cm, ctx.at
    U = ctx.names()
    k.push_scope()
    Wi = k.sb(U("wi"), [128, 8, 1536], BF16)
    Wm = k.sb(U("wm"), [128, 8, 512], BF16)
    Wo = k.sb(U("wo"), [128, 8, D], BF16)
    BMh = k.sb(U("bmh"), [128, 12, 256], BF16)
    BMl = k.sb(U("bml"), [128, 12, 256], BF16)
    k.dma("pool", Wm[:], wm_in.rearrange("(kc p) f -> p kc f", p=128), "w0")
    k.dma("pool", Wi[:], wi_in.rearrange("(kc p) f -> p kc f", p=128), "w1")
    k.dma("pool", BMh[:], bmh_in, "w2")
    k.dma("pool", BMl[:], bml_in, "w3")
    k.dma("pool", Wo[:], wo_in.rearrange("(kc p) f -> p kc f", p=128), "w4")
    stage = [k.sb(U(f"stg{i}"), [128, D], F32) for i in range(2)]
    ostage = [k.sb(U("ostg"), [128, D], F32)]
    Kmem, Vmem = mem_kv(ctx, U, MEMN, Wm)
    nt = nb * TB
    KS = k.sb(U("ks"), [128, 2, nt + 128], BF16)
    VS = k.sb(U("vs"), [128, 4 * nb + 1, 4, 192], BF16)
    QS = k.sb(U("qs"), [128, 6, TB], BF16)
    QC = k.sb(U("qc"), [128, 2, TB], BF16)
    AO = k.sb(U("ao"), [128, 8, TB], BF16)
    X = k.sb(U("xblk"), [128, 8, TB], F32)
    HT = k.sb(U("ht"), [128, 8, TB], BF16)
    XB = k.sb(U("xb"), [128, 8, 128], F32)
    k.memset("pool", VS[:], 1.0)
    if bnd_src is not None:
        cm.load_xT_block(bnd_src, XB, 0, 1, stage, "xs")
        cm.norm_block(XB, 0, 128, gcol, HT, 0)
        for c in range(2):
            proj_chunk(cm, Wi, 768 + c * 128, 128, HT, 0, 128, KS[:, c, 0:128], "act")
        b = cm.next_bank(PROJ_BANKS)
        for kc in range(8):
            k.mm(cm.bank(b, 0, 256), HT[:, kc, 0:128], Wi[:, kc, 1024:1280], start=(kc == 0), stop=(kc == 7))
        k.copy("dve", VS[:, 0, :, 64:128], cm.bank(b, 0, 256).rearrange("p (h d) -> p h d", h=4))
    else:
        k.memset("pool", KS[:, :, 0:128], 0.0)
    for j in range(nb):
        cm.load_xT_block(x_src[j * TB:(j + 1) * TB, :], X, 0, 4, stage, "xs")
        cm.norm_block(X, 0, TB, gcol, HT, 0)
        for c in range(6):
            proj_chunk(cm, Wi, c * 128, 128, HT, 0, TB, QS[:, c, :], "act" if c % 2 else "dve")
        for c in range(2):
            proj_chunk(cm, Wi, 768 + c * 128, 128, HT, 0, TB, KS[:, c, 128 + j * TB:128 + (j + 1) * TB], "act")
        for c in range(2):
            proj_chunk(cm, Wi, 1280 + c * 128, 128, HT, 0, TB, QC[:, c, :], "dve")
        for t in range(4):
            b = cm.next_bank(PROJ_BANKS)
            for kc in range(8):
                k.mm(cm.bank(b, 0, 256), HT[:, kc, t * 128:(t + 1) * 128], Wi[:, kc, 1024:1280], start=(kc == 0), stop=(kc == 7))
            k.copy("dve" if t % 2 else "act", VS[:, 1 + 4 * j + t, :, 64:128],
                   cm.bank(b, 0, 256).rearrange("p (h d) -> p h d", h=4))
        cross_units(cm, at, Kmem, Vmem, QC, 0, AO, 0)
        for p in range(12):
            kh = SWA_POS[p] // 3
            par = p % 2
            r0 = par * 64
            tiles = []
            for i in range(5):
                s = 4 * j + i
                c0 = max(0, (i - 1) * 128)
                c1 = min(TB, (i + 1) * 128)
                m0 = 128 if i == 0 else 0
                V = VS[:, s, kh, 64:192] if par == 0 else VS[:, s, kh, 0:128]
                tiles.append(dict(K=KS[r0:r0 + 64, kh // 2, s * 128:(s + 1) * 128], V=V, c0=c0, c1=c1,
                                  bias=(bnd_bias if s == 0 else cm.zcol[:, 0:1]),
                                  masks=[(c0, c1, BMh[:, p, m0:m0 + (c1 - c0)]), (c0, c1, BMl[:, p, m0:m0 + (c1 - c0)])]))
            sr = 64 if par == 0 else 0
            at.unit(tiles, lambda c0, c1, p=p, r0=r0: QS[r0:r0 + 64, p // 2, c0:c1], 0.125, par,
                    AO[r0:r0 + 64, p // 2, :], extra_sum=sk[sr:sr + 1, p:p + 1])
        out_proj_block(cm, Wo, AO, 0, X, 0)
        cm.store_xT_block(X, 0, 4, out[j * TB:(j + 1) * TB, :], ostage, "os")
    k.pop_scope()


def emit_mla(ctx, x_src, nb, prev_src, npb, pos_own, pos_prev, prev_bias, out, g_attn, g_q, g_kv, MEMN,
             rc, tri, wi_in, wuq_in, wukv_in, wm_in, wo_in):
    k, cm, at = ctx.k, ctx.cm, ctx.at
    U = ctx.names()
    nt = nb * TB
    npv = npb * TB
    nkey = nt + npv
    nkt = nkey // 128
    k.push_scope()
    Wi = k.sb(U("wi"), [128, 8, 1088], BF16)
    Wukv = k.sb(U("wukv"), [128, 2, 1536], BF16)
    CKVN = k.sb(U("ckvn"), [128, 2, nkey], BF16)
    KTs = [k.sb(U("kt0"), [128, nkey], BF16), k.sb(U("kt1"), [128, nkey], BF16)]
    CQN = k.sb(U("cqn"), [128, 3, nt], BF16)
    CC = k.sb(U("cc"), [128, nt], F32)
    SS = k.sb(U("ss"), [128, nt], F32)
    AO = k.sb(U("ao"), [128, 8, nt], BF16)
    rt1 = k.sb(U("rt1"), [128, TB], F32)
    rt2 = k.sb(U("rt2"), [128, TB], F32)
    k.dma("pool", Wi[:], wi_in.rearrange("(kc p) f -> p kc f", p=128), "w1")
    k.dma("pool", Wukv[:], wukv_in.rearrange("(kc p) f -> p kc f", p=128), "w2")

    def rope(dsts, A, B, cc, ss):
        k.tt("dve", rt1[64:96, :], A, cc, ALU.mult)
        k.tt("dve", rt2[64:96, :], B, ss, ALU.mult)
        for dst in dsts:
            k.tt("pool", dst, rt1[64:96, :], rt2[64:96, :], ALU.add)

    k.push_scope()
    X = k.sb(U("xblk"), [128, 8, TB], F32)
    HT = k.sb(U("ht"), [128, 8, TB], BF16)
    stage = [k.sb(U(f"stg{i}"), [128, D], F32) for i in range(2)]
    QC = k.sb(U("qc"), [128, 2, TB], BF16)
    Kmem = k.sb(U("kmem"), [128, 2, N_MEM], BF16)
    Vmem = k.sb(U("vmem"), [128, 2, 4, 192], BF16)
    k.push_scope()
    Wm = k.sb(U("wm"), [128, 8, 512], BF16)
    k.dma("pool", Wm[:], wm_in.rearrange("(kc p) f -> p kc f", p=128), "w0")
    mem_kv(ctx, U, MEMN, Wm, Kmem, Vmem)
    k.pop_scope()
    posi = k.sb(U("posi"), [128, TB], I32)
    yv = k.sb(U("yv"), [128, TB], F32)
    yi = k.sb(U("yi"), [128, TB], I32)
    yf = k.sb(U("yf"), [128, TB], F32)
    fr = k.sb(U("fr"), [128, TB], F32)
    cmp_ = k.sb(U("cmp"), [128, TB], F32)
    CCt = k.sb(U("cct"), [128, TB], F32)
    SSt = k.sb(U("sst"), [128, TB], F32)

    def frac_sin(dst, add, scale_ap):
        src = yv
        if add != 0.0:
            k.ts("dve", yf[:], yv[:], add, ALU.add)
            src = yf
            k.copy("dve", yi[:], yf[:])
        else:
            k.copy("dve", yi[:], yv[:])
        k.copy("dve", fr[:], yi[:])
        k.tt("dve", fr[:], src[:], fr[:], ALU.subtract)
        k.ts("dve", cmp_[:], fr[:], 0.5, ALU.is_gt)
        k.tt("dve", fr[:], fr[:], cmp_[:], ALU.subtract)
        k.ts("dve", cmp_[:], fr[:], -0.5, ALU.is_lt)
        k.tt("dve", fr[:], fr[:], cmp_[:], ALU.add)
        k.act(dst, fr[:], AF.Sin, scale=scale_ap)

    def rope_tables(pos_slice, cc, ss):
        k.dma("sp", posi[:], pos_slice.partition_broadcast(128), "posd")
        k.copy("dve", yv[:], posi[:])
        k.ts("dve", yv[:], yv[:], rc[:, 0:1], ALU.mult)
        frac_sin(ss, 0.0, rc[:, 1:2])
        frac_sin(cc, 0.25, rc[:, 2:3])

    def latents_block(xsrc, t0, koff, pos_src, own):
        cm.load_xT_block(xsrc[t0:t0 + TB, :], X, 0, 4, stage, "xs")
        cm.norm_block(X, 0, TB, g_attn, HT, 0)
        if own:
            cc, ss = CC[:, t0:t0 + TB], SS[:, t0:t0 + TB]
        else:
            cc, ss = CCt[:, :], SSt[:, :]
        rope_tables(pos_src[t0:t0 + TB], cc, ss)
        if own:
            for c in range(3):
                for kc in range(8):
                    k.mm(cm.bank(c), Wi[:, kc, c * 128:(c + 1) * 128], HT[:, kc, :], start=(kc == 0), stop=(kc == 7))
            rstd = cm.rms_stats([cm.bank(c) for c in range(3)], TB, 1.0 / 384, banks=(5,), sq_eng="act")
            for c in range(3):
                k.stt("dve", CQN[:, c, t0:t0 + TB], cm.bank(c), g_q[:, c:c + 1], rstd, ALU.mult, ALU.mult)
        for c in range(2):
            for kc in range(8):
                k.mm(cm.bank(3 + c), Wi[:, kc, 384 + c * 128:384 + (c + 1) * 128], HT[:, kc, :], start=(kc == 0), stop=(kc == 7))
        rstd = cm.rms_stats([cm.bank(3 + c) for c in range(2)], TB, 1.0 / 256, banks=(5,), sq_eng="act")
        for c in range(2):
            k.stt("dve", CKVN[:, c, koff + t0:koff + t0 + TB], cm.bank(3 + c), g_kv[:, c:c + 1], rstd, ALU.mult, ALU.mult)
        for i, b in enumerate((6, 7)):
            for kc in range(8):
                k.mm(cm.bank(b, 0, TB, 0, 96), Wi[:, kc, 896 + i * 96:896 + (i + 1) * 96], HT[:, kc, :], start=(kc == 0), stop=(kc == 7))
        rope([KT_[64:96, koff + t0:koff + t0 + TB] for KT_ in KTs], cm.bank(6, 0, TB, 64, 96), cm.bank(7, 0, TB, 64, 96),
             cc[64:96, :], ss[64:96, :])
        if own:
            for c in range(2):
                proj_chunk(cm, Wi, 640 + c * 128, 128, HT, 0, TB, QC[:, c, :], "dve")
            cross_units(cm, at, Kmem, Vmem, QC, 0, AO, t0)

    for j in range(npb):
        latents_block(prev_src, j * TB, 0, pos_prev, False)
    for j in range(nb):
        latents_block(x_src, j * TB, npv, pos_own, True)
    k.pop_scope()

    k.push_scope()
    Wo = Wi
    k.dma("pool", Wo[:, :, 0:D], wo_in.rearrange("(kc p) f -> p kc f", p=128), "w3")
    VV = [k.sb(U("ve"), [128, nkt, 128], BF16), k.sb(U("vo"), [128, nkt, 128], BF16)]
    QT = [k.sb(U(f"qt{i}"), [128, nt], BF16) for i in range(2)]
    Wq = [k.sb(U(f"wq{i}"), [128, 3, 192], BF16) for i in range(2)]
    k.memset("pool", VV[0][:], 1.0)
    k.memset("pool", VV[1][:], 1.0)
    wuqv = wuq_in.rearrange("(kc p) f -> p kc f", p=128)
    P2 = (7, 6)
    zb = cm.zcol[:, 0:1]

    def prologue(h):
        par = h % 2
        wq, qt, Vh, KT = Wq[par], QT[par], VV[par], KTs[par]
        voff = 0 if par == 0 else 64
        k.dma("pool", wq[:], wuqv[:, :, h * 192:(h + 1) * 192], f"wq{par}")
        for j in range(nb):
            t0 = j * TB
            bA = cm.next_bank(P2)
            for kc in range(3):
                k.mm(cm.bank(bA, 0, TB, 0, 96), wq[:, kc, 0:96], CQN[:, kc, t0:t0 + TB], start=(kc == 0), stop=(kc == 2))
            bB = cm.next_bank(P2)
            for kc in range(3):
                k.mm(cm.bank(bB, 0, TB, 0, 96), wq[:, kc, 96:192], CQN[:, kc, t0:t0 + TB], start=(kc == 0), stop=(kc == 2))
            k.copy("dve", qt[0:64, t0:t0 + TB], cm.bank(bA, 0, TB, 0, 64))
            rope([qt[64:96, t0:t0 + TB]], cm.bank(bA, 0, TB, 64, 96), cm.bank(bB, 0, TB, 64, 96), CC[64:96, t0:t0 + TB], SS[64:96, t0:t0 + TB])
        for kb in range(nkey // TB):
            b = cm.next_bank(P2)
            for kc in range(2):
                k.mm(cm.bank(b, 0, TB, 0, 64), Wukv[:, kc, 128 * h:128 * h + 64], CKVN[:, kc, kb * TB:(kb + 1) * TB], start=(kc == 0), stop=(kc == 1))
            k.copy("dve", KT[0:64, kb * TB:(kb + 1) * TB], cm.bank(b, 0, TB, 0, 64))
        for g in range(nkt // 8):
            b = cm.next_bank(P2)
            for tl in range(8):
                kt_ = g * 8 + tl
                for kc in range(2):
                    k.mm(cm.bank(b, tl * 64, (tl + 1) * 64), CKVN[:, kc, kt_ * 128:(kt_ + 1) * 128],
                         Wukv[:, kc, 128 * h + 64:128 * h + 128], start=(kc == 0), stop=(kc == 1))
            k.copy("dve", Vh[:, g * 8:(g + 1) * 8, voff:voff + 64], cm.bank(b).rearrange("p (t d) -> p t d", t=8))

    def attention(h):
        par = h % 2
        r0 = par * 64
        qt, Vh, KT = QT[par], VV[par], KTs[par]
        for j in range(nb):
            t0 = j * TB
            tiles = []
            for kt_ in range(npb * 4):
                tiles.append(dict(K=KT[0:96, kt_ * 128:(kt_ + 1) * 128], V=Vh[:, kt_, :], c0=0, c1=TB, bias=prev_bias))
            for kt_ in range(npb * 4, npb * 4 + 4 * j):
                tiles.append(dict(K=KT[0:96, kt_ * 128:(kt_ + 1) * 128], V=Vh[:, kt_, :], c0=0, c1=TB, bias=zb))
            for t in range(4):
                kt_ = npb * 4 + 4 * j + t
                tiles.append(dict(K=KT[0:96, kt_ * 128:(kt_ + 1) * 128], V=Vh[:, kt_, :], c0=128 * t, c1=TB,
                                  bias=zb, masks=[(128 * t, 128 * t + 128, tri[:, :])]))
            at.unit(tiles, lambda c0, c1, qt=qt, t0=t0: qt[0:96, t0 + c0:t0 + c1], MLA_SCALE, par,
                    AO[r0:r0 + 64, h // 2, t0:t0 + TB])

    prologue(0)
    for h in range(12):
        if h + 1 < 12:
            prologue(h + 1)
        attention(h)
    k.pop_scope()

    k.push_scope()
    X = k.sb(U("xblk3"), [128, 8, TB], F32)
    stage = [k.sb(U(f"stg3{i}"), [128, D], F32) for i in range(2)]
    ostage = [k.sb(U("ostg3"), [128, D], F32)]
    for j in range(nb):
        cm.load_xT_block(x_src[j * TB:(j + 1) * TB, :], X, 0, 4, stage, "xs3")
        out_proj_block(cm, Wo, AO, j * TB, X, 0)
        cm.store_xT_block(X, 0, 4, out[j * TB:(j + 1) * TB, :], ostage, "os")
    k.pop_scope()
    k.pop_scope()


GCOLS = 96


def build_fused(only=None):
    k = KB()
    nc = k.nc
    x_in = k.dram_in("x", [NT, D])
    xp_in = k.dram_in("xprev", [NT, D])
    mem_in = k.dram_in("mem", [N_MEM, D])
    g_in = k.dram_in("g", [128, GCOLS])
    pos_in = k.dram_in("pos", [NT], I32)
    posp_in = k.dram_in("posprev", [NT], I32)
    rc_in = k.dram_in("ropec", [128, 4])
    tri_in = k.dram_in("trimask", [128, 128])
    pb_in = k.dram_in("prevbias", [128, 1])
    ident_in = k.dram_in("ident", [128, 128])
    sk_in = k.dram_in("sinks", [128, 24])
    bmh_in = k.dram_in("bmhi", [128, 12, 256])
    bml_in = k.dram_in("bmlo", [128, 12, 256])
    W = {}
    for l in range(4):
        W[f"wm{l}"] = k.dram_in(f"w_mem{l}", [D, 512])
        W[f"wo{l}"] = k.dram_in(f"w_o{l}", [D, D])
        W[f"up{l}"] = k.dram_in(f"w_up{l}", [D, DFF])
        W[f"dn{l}"] = k.dram_in(f"w_dn{l}", [DFF, D])
        if l % 2 == 0:
            W[f"wi{l}"] = k.dram_in(f"w_in{l}", [D, 1088])
            W[f"wuq{l}"] = k.dram_in(f"w_uq{l}", [384, 12 * 192])
            W[f"wukv{l}"] = k.dram_in(f"w_ukv{l}", [256, 1536])
        else:
            W[f"wi{l}"] = k.dram_in(f"w_in{l}", [D, 1536])
    y_out = k.dram_out("y", [NT, D])
    scr = {n: nc.dram_tensor("scr_" + n, [NT, D], F32).ap() for n in ("p1a", "p1b", "p1c", "p1d", "p1e", "p2a", "p2b")}
    cm = Common(k, {"ident": ident_in}, nbuf=1)
    at = Attn(cm)
    ctx = Ctx(k, cm, at)
    G = k.sb("G", [128, GCOLS], F32)
    rc = k.sb("rc", [128, 4], F32)
    pb = k.sb("pb", [128, 1], F32)
    negc = k.sb("negc", [128, 1], F32)
    sk = k.sb("sk", [128, 24], F32)
    tri = k.sb("tri", [128, 128], BF16)
    MEMN = k.sb("memn", [128, 8, N_MEM], BF16)
    k.dma("sp", G[:], g_in, "const")
    k.dma("sp", rc[:], rc_in, "const")
    k.dma("sp", pb[:], pb_in, "const")
    k.dma("sp", sk[:], sk_in, "const")
    k.dma("pool", tri[:], tri_in, "constp")
    k.memset("dve", negc[:], NEGB)
    k.act(sk[:], sk[:], AF.Exp)
    k.push_scope()
    mT = k.sb("memT", [128, 8, N_MEM], F32)
    mstage = [k.sb(f"mstg{i}", [128, D], F32) for i in range(2)]
    cm.load_xT_block(mem_in, mT, 0, 2, mstage, "xs")
    cm.norm_block(mT, 0, N_MEM, G[:, 64:72], MEMN, 0)
    k.pop_scope()

    def ga(l):
        return G[:, l * 16:l * 16 + 8]

    def gm(l):
        return G[:, l * 16 + 8:l * 16 + 16]

    def mla(l, x_src, nb, prev_src, npb, pos_own, pos_prev, prev_bias, out):
        j = l // 2
        emit_mla(ctx, x_src, nb, prev_src, npb, pos_own, pos_prev, prev_bias, out, ga(l),
                 G[:, 80 + 5 * j:83 + 5 * j], G[:, 83 + 5 * j:85 + 5 * j], MEMN, rc, tri,
                 W[f"wi{l}"], W[f"wuq{l}"], W[f"wukv{l}"], W[f"wm{l}"], W[f"wo{l}"])

    def swa(l, x_src, nb, bnd_src, bnd_bias, out):
        j = l // 2
        emit_swa(ctx, x_src, nb, bnd_src, bnd_bias, out, ga(l), MEMN, W[f"wi{l}"], W[f"wm{l}"], W[f"wo{l}"],
                 sk[:, 12 * j:12 * j + 12], bmh_in, bml_in)

    def mlp(l, x_src, nb, out, final=False):
        emit_mlp(ctx, x_src, nb, out, gm(l), W[f"up{l}"], W[f"dn{l}"], G[:, 72:80] if final else None)

    z = cm.zcol[:, 0:1]
    if only is not None:
        if only == "mla":
            mla(0, x_in, NB, xp_in, NB, pos_in, posp_in, pb[:, 0:1], y_out)
        elif only == "mlalite":
            mla(0, xp_in, NB, None, 0, posp_in, None, z, y_out)
        elif only == "swa":
            swa(1, x_in, NB, xp_in[NT - 128:NT, :], pb[:, 0:1], y_out)
        elif only == "mlp":
            mlp(0, x_in, NB, y_out)
        k.finish()
        return k
    mla(0, xp_in, NB, None, 0, posp_in, None, z, scr["p1a"])
    mlp(0, scr["p1a"], NB, scr["p1b"])
    swa(1, scr["p1b"], NB, None, negc[:, 0:1], scr["p1a"])
    mlp(1, scr["p1a"], NB, scr["p1c"])
    mla(2, scr["p1c"][3 * TB:4 * TB, :], 1, scr["p1c"][0:3 * TB, :], 3, posp_in[3 * TB:4 * TB], posp_in[0:3 * TB], z,
        scr["p1d"][0:TB, :])
    mlp(2, scr["p1d"][0:TB, :], 1, scr["p1e"][0:TB, :])
    mla(0, x_in, NB, xp_in, NB, pos_in, posp_in, pb[:, 0:1], scr["p2a"])
    mlp(0, scr["p2a"], NB, scr["p2b"])
    swa(1, scr["p2b"], NB, scr["p1b"][NT - 128:NT, :], pb[:, 0:1], scr["p2a"])
    mlp(1, scr["p2a"], NB, scr["p2b"])
    mla(2, scr["p2b"], NB, scr["p1c"], NB, pos_in, posp_in, pb[:, 0:1], scr["p2a"])
    mlp(2, scr["p2a"], NB, scr["p2b"])
    swa(3, scr["p2b"], NB, scr["p1e"][TB - 128:TB, :], pb[:, 0:1], scr["p2a"])
    mlp(3, scr["p2a"], NB, y_out, final=True)
    k.finish()
    return k


N_CORES = 8


def _f(a):
    return np.ascontiguousarray(a, dtype=np.float32)


def kernel(x, mem, positions, attn_norm_g, mlp_norm_g, mem_norm_g, final_norm_g,
           mla_w_in, mla_q_norm_g, mla_kv_norm_g, mla_w_uq, mla_w_ukv,
           swa_w_in, swa_sinks, w_mem_kv, w_o, mlp_w_up, mlp_w_down):
    x = np.asarray(x, dtype=np.float32)
    mem = np.asarray(mem, dtype=np.float32)
    positions = np.asarray(positions)
    B, S, _ = x.shape
    shared = {}
    g = np.zeros((128, GCOLS), np.float32)
    for l in range(4):
        g[:, l * 16:l * 16 + 8] = np.asarray(attn_norm_g[l]).reshape(8, 128).T
        g[:, l * 16 + 8:l * 16 + 16] = np.asarray(mlp_norm_g[l]).reshape(8, 128).T
    g[:, 64:72] = np.asarray(mem_norm_g).reshape(8, 128).T
    g[:, 72:80] = np.asarray(final_norm_g).reshape(8, 128).T
    for j in range(2):
        g[:, 80 + 5 * j:83 + 5 * j] = np.asarray(mla_q_norm_g[j]).reshape(3, 128).T
        g[:, 83 + 5 * j:85 + 5 * j] = np.asarray(mla_kv_norm_g[j]).reshape(2, 128).T
    shared["g"] = g
    r = np.arange(128)
    inv = 10000.0 ** (-(np.arange(16, dtype=np.float64) * 2.0) / 32)
    rcv = np.zeros((128, 4), np.float64)
    rcv[:, 0] = inv[r % 16] / (2 * np.pi)
    rcv[:, 1] = np.where((r % 32) < 16, -1.0, 1.0) * 2 * np.pi
    rcv[:, 2] = 2 * np.pi
    shared["ropec"] = _f(rcv)
    kk = np.arange(128)[:, None]
    cc = np.arange(128)[None, :]
    shared["trimask"] = _f(np.where(kk <= cc, 0.0, 8.0 * NEGB))
    shared["ident"] = np.eye(128, dtype=np.float32)
    sk = np.concatenate([np.asarray(swa_sinks[j])[SWA_POS] for j in range(2)])
    shared["sinks"] = _f(np.broadcast_to(sk[None, :], (128, 24)))
    hi, lo = alibi_tables()
    shared["bmhi"], shared["bmlo"] = _f(hi), _f(lo)
    for l in range(4):
        j = l // 2
        shared[f"w_mem{l}"] = _f(w_mem_kv[l])
        shared[f"w_up{l}"] = _f(mlp_w_up[l])
        shared[f"w_dn{l}"] = _f(mlp_w_down[l])
        if l % 2 == 0:
            w_in = np.asarray(mla_w_in[j])
            kr = w_in[:, 640:672]
            krp = np.concatenate([kr[:, 16:32], kr[:, 0:16]], axis=1)
            pad = w_in[:, 576:640]
            shared[f"w_in{l}"] = _f(np.concatenate([w_in[:, 0:640], w_in[:, 672:928], pad, kr, pad, krp], axis=1))
            w_uq = np.asarray(mla_w_uq[j])
            hq = []
            for h in range(12):
                wh = w_uq[:, h * 96:(h + 1) * 96]
                hq += [wh, wh[:, 0:64], wh[:, 80:96], wh[:, 64:80]]
            shared[f"w_uq{l}"] = _f(np.concatenate(hq, axis=1))
            shared[f"w_ukv{l}"] = _f(mla_w_ukv[j])
            shared[f"w_o{l}"] = _f(w_o[l])
        else:
            w_in = np.asarray(swa_w_in[j])
            wq = np.concatenate([w_in[:, h * 64:(h + 1) * 64] for h in SWA_POS], axis=1)
            shared[f"w_in{l}"] = _f(np.concatenate([wq, w_in[:, 768:]], axis=1))
            wo = np.asarray(w_o[l])
            shared[f"w_o{l}"] = _f(np.concatenate([wo[h * 64:(h + 1) * 64] for h in SWA_POS] + [wo[768:]], axis=0))
    in_maps = []
    for c in range(N_CORES):
        b, half = c // 2, c % 2
        m = dict(shared)
        m["x"] = _f(x[b, half * NT:(half + 1) * NT])
        m["xprev"] = _f(x[b, 0:NT])
        m["mem"] = _f(mem[b])
        m["pos"] = np.ascontiguousarray(positions[b, half * NT:(half + 1) * NT], dtype=np.int32)
        m["posprev"] = np.ascontiguousarray(positions[b, 0:NT], dtype=np.int32)
        m["prevbias"] = np.full((128, 1), 0.0 if half else NEGB, np.float32)
        in_maps.append(m)
    kb = build_fused()
    res = run_bass_kernel_spmd(kb.nc, in_maps, core_ids=list(range(N_CORES)))
    out = np.empty((B, S, D), np.float32)
    for c in range(N_CORES):
        out[c // 2, (c % 2) * NT:(c % 2 + 1) * NT] = np.asarray(res.results[c]["y"])
    return out
```

```python
import numpy as np
from contextlib import ExitStack
import concourse.bass as bass
import concourse.mybir as mybir
from concourse.bass_utils import run_bass_kernel_spmd

F32, BF16, I32 = mybir.dt.float32, mybir.dt.bfloat16, mybir.dt.int32
ALU = mybir.AluOpType
AF = mybir.ActivationFunctionType

D = 1024
NT = 2048
TB = 512
NB = NT // TB
DFF = 4096
EPS = 1e-6
NEGB = -30000.0
N_MEM = 256


class KB:
    def __init__(self):
        self.nc = bass.Bass("TRN2", target_bir_lowering=False)
        nc = self.nc
        self.E = dict(pe=nc.tensor, act=nc.scalar, dve=nc.vector, pool=nc.gpsimd, sp=nc.sync)
        self.csem = {e: nc.alloc_semaphore("s_" + e) for e in ("pe", "act", "dve", "pool")}
        self.cnt = {e: 0 for e in self.csem}
        self.waited = {e: {} for e in self.E}
        self.trk = {}
        self.W = {}
        self.dsem = {}
        self.dcnt = {}
        self.semh = {}
        for e, h in self.csem.items():
            self.semh["c_" + e] = h
        self.es = ExitStack()
        self.n_ins = 0

    def sb(self, name, shape, dtype):
        t = self.es.enter_context(self.nc.sbuf_tensor(name, list(shape), dtype))
        self.W[name] = int(np.prod(shape[1:]))
        self.trk[name] = []
        return t

    def ps(self, name, shape, dtype=F32):
        t = self.es.enter_context(self.nc.psum_tensor(name, list(shape), dtype))
        self.W[name] = int(np.prod(shape[1:]))
        self.trk[name] = []
        return t

    def push_scope(self):
        self._scopes = getattr(self, "_scopes", [])
        self._scopes.append(self.es)
        self.es = ExitStack()

    def pop_scope(self):
        self.barrier()
        self.es.close()
        self.es = self._scopes.pop()

    def barrier(self):
        toks = [("c_" + e, c) for e, c in self.cnt.items() if c > 0]
        toks += [("d_" + s, c * 16) for s, c in self.dcnt.items() if c > 0]
        for e in self.E:
            self._wait(e, toks)

    def dram_in(self, name, shape, dtype=F32):
        return self.nc.dram_tensor(name, list(shape), dtype, kind="ExternalInput").ap()

    def dram_out(self, name, shape, dtype=F32):
        return self.nc.dram_tensor(name, list(shape), dtype, kind="ExternalOutput").ap()

    def dma_sem(self, name):
        if name not in self.dsem:
            self.dsem[name] = self.nc.alloc_semaphore("d_" + name)
            self.dcnt[name] = 0
            self.semh["d_" + name] = self.dsem[name]
        return name

    def _region(self, ap):
        name = ap.tensor.name
        if name not in self.W:
            return None
        W = self.W[name]
        off = int(ap.offset)
        a = ap.ap
        p0 = off // W
        f0 = off % W
        p1 = p0 + a[0][1]
        hi = f0 + sum((c - 1) * s for s, c in a[1:]) + 1
        if name.startswith("ps"):
            return name, 0, 128, (f0 // 512) * 512, ((hi + 511) // 512) * 512
        return name, p0, p1, f0, hi

    def _deps(self, reads, writes):
        toks = set()
        for ap in reads:
            r = self._region(ap)
            if r is None:
                continue
            name, p0, p1, lo, hi = r
            for e in self.trk[name]:
                if e[4] and e[0] < p1 and p0 < e[1] and e[2] < hi and lo < e[3]:
                    toks.add(e[5])
        for ap in writes:
            r = self._region(ap)
            if r is None:
                continue
            name, p0, p1, lo, hi = r
            for e in self.trk[name]:
                if e[0] < p1 and p0 < e[1] and e[2] < hi and lo < e[3]:
                    toks.add(e[5])
        return toks

    def _record(self, reads, writes, tok):
        for ap in writes:
            r = self._region(ap)
            if r is None:
                continue
            name, p0, p1, lo, hi = r
            lst = self.trk[name]
            lst[:] = [e for e in lst if not (p0 <= e[0] and e[1] <= p1 and lo <= e[2] and e[3] <= hi)]
            lst.append([p0, p1, lo, hi, True, tok])
        for ap in reads:
            r = self._region(ap)
            if r is None:
                continue
            name, p0, p1, lo, hi = r
            lst = self.trk[name]
            for e in lst:
                if (not e[4]) and e[0] == p0 and e[1] == p1 and e[2] == lo and e[3] == hi and e[5][0] == tok[0]:
                    e[5] = tok
                    break
            else:
                lst.append([p0, p1, lo, hi, False, tok])

    def _wait(self, eng, toks):
        best = {}
        for s, v in toks:
            if s.startswith("d_"):
                v = max(v, self.dcnt[s[2:]] * 16)
            if v > best.get(s, 0):
                best[s] = v
        for s, v in best.items():
            if eng == "pe" and s == "c_pe":
                continue
            if self.waited[eng].get(s, 0) >= v:
                continue
            self.E[eng].wait_ge(self.semh[s], v)
            self.waited[eng][s] = v
            self.n_ins += 1

    def op(self, eng, fn, reads=(), writes=()):
        toks = self._deps(reads, writes)
        self._wait(eng, toks)
        ins = fn()
        self.cnt[eng] += 1
        tok = ("c_" + eng, self.cnt[eng])
        ins.then_inc(self.csem[eng], 1)
        self._record(reads, writes, tok)
        self.n_ins += 1
        return tok

    def dma(self, q, out, in_, sem):
        self.dma_sem(sem)
        toks = self._deps([in_], [out])
        self._wait(q, toks)
        ins = self.E[q].dma_start(out=out, in_=in_)
        self.dcnt[sem] += 1
        tok = ("d_" + sem, self.dcnt[sem] * 16)
        ins.then_inc(self.dsem[sem], 16)
        self._record([in_], [out], tok)
        self.n_ins += 1
        return tok

    def wait_all_dma(self, eng):
        toks = [("d_" + s, c * 16) for s, c in self.dcnt.items() if c > 0]
        self._wait(eng, toks)

    def mm(self, out, lhsT, rhs, start=True, stop=True, extra_reads=()):
        return self.op("pe", lambda: self.nc.tensor.matmul(out, lhsT, rhs, start=start, stop=stop,
                                                          skip_group_check=True),
                       reads=[lhsT, rhs, *extra_reads], writes=[out])

    def tr(self, out, in_, ident):
        return self.op("pe", lambda: self.nc.tensor.transpose(out, in_, ident),
                       reads=[in_, ident], writes=[out])

    def act(self, out, in_, func, bias=None, scale=1.0, eng="act"):
        reads = [in_]
        kw = {}
        if bias is not None:
            kw["bias"] = bias
            if not isinstance(bias, (int, float)):
                reads.append(bias)
        if not isinstance(scale, (int, float)):
            reads.append(scale)
        return self.op("act", lambda: self.nc.scalar.activation(out=out, in_=in_, func=func, scale=scale, **kw),
                       reads=reads, writes=[out])

    def copy(self, eng, out, in_):
        if eng == "act":
            return self.op("act", lambda: self.nc.scalar.copy(out=out, in_=in_), reads=[in_], writes=[out])
        return self.op(eng, lambda: self.E[eng].tensor_copy(out=out, in_=in_), reads=[in_], writes=[out])

    def tt(self, eng, out, in0, in1, op):
        return self.op(eng, lambda: self.E[eng].tensor_tensor(out=out, in0=in0, in1=in1, op=op),
                       reads=[in0, in1], writes=[out])

    def ts(self, eng, out, in0, s1, op0, s2=None, op1=None):
        reads = [in0] + [s for s in (s1, s2) if s is not None and not isinstance(s, (int, float))]
        if op1 is None:
            return self.op(eng, lambda: self.E[eng].tensor_scalar(out=out, in0=in0, scalar1=s1, scalar2=None, op0=op0),
                           reads=reads, writes=[out])
        return self.op(eng, lambda: self.E[eng].tensor_scalar(out=out, in0=in0, scalar1=s1, scalar2=s2, op0=op0, op1=op1),
                       reads=reads, writes=[out])

    def stt(self, eng, out, in0, scalar, in1, op0, op1):
        reads = [in0, in1] + ([] if isinstance(scalar, (int, float)) else [scalar])
        return self.op(eng, lambda: self.E[eng].scalar_tensor_tensor(out=out, in0=in0, scalar=scalar, in1=in1,
                                                                    op0=op0, op1=op1),
                       reads=reads, writes=[out])

    def memset(self, eng, out, val):
        return self.op(eng, lambda: self.E[eng].memset(out, val), reads=[], writes=[out])

    def finish(self):
        self.wait_all_dma("sp")
        self.es.close()


class Common:
    def __init__(self, kb, consts_dram, nbuf=2):
        self.kb = kb
        k = kb
        self.PS = [k.ps(f"ps{i}", [128, 1024]) for i in range(4)]
        self.rr = 0
        self.identf = k.sb("identf", [128, 128], F32)
        self.identb = k.sb("identb", [128, 128], BF16)
        self.onesb = k.sb("onesb", [128, 128], BF16)
        self.onesf = k.sb("onesf", [128, 128], F32)
        self.zcol = k.sb("zcol", [128, 1], F32)
        self.epscol = k.sb("epscol", [128, 1], F32)
        k.dma("sp", self.identf[:], consts_dram["ident"], "const")
        k.dma("pool", self.identb[:], consts_dram["ident"], "constp")
        k.memset("dve", self.onesb[:], 1.0)
        k.memset("dve", self.onesf[:], 1.0)
        k.memset("dve", self.zcol[:], 0.0)
        k.memset("dve", self.epscol[:], EPS)
        self.nbuf = nbuf
        self.sq = [k.sb(f"sq{i}", [128, 8, TB], BF16) for i in range(nbuf)]
        self.lnv = [k.sb(f"lnv{i}", [128, TB], F32) for i in range(nbuf)]
        self.rstd = [k.sb(f"rstd{i}", [128, TB], F32) for i in range(nbuf)]
        self.nrm_i = 0

    def bank(self, b, lo=0, hi=512, p0=0, p1=128):
        return self.PS[b // 2][p0:p1, (b % 2) * 512 + lo:(b % 2) * 512 + hi]

    def next_bank(self, banks=(0, 1, 2, 3, 4, 5, 6, 7)):
        b = banks[self.rr % len(banks)]
        self.rr += 1
        return b

    def rms_stats(self, chunks, w, inv_n, banks=(0, 1, 2, 3, 4, 5, 6, 7), sq_eng="pool"):
        if sq_eng == "act":
            return self._rms_stats_act(chunks, w, inv_n, banks)
        return self._rms_stats(chunks, w, inv_n, banks, sq_eng)

    def _rms_stats_act(self, chunks, w, inv_n, banks):
        k = self.kb
        i = self.nrm_i % self.nbuf
        self.nrm_i += 1
        sq, lnv, rstd = self.sq[i], self.lnv[i], self.rstd[i]
        b = self.next_bank(banks)
        n = len(chunks)
        for c, ap in enumerate(chunks):
            k.act(sq[:, c, 0:w], ap, AF.Square)
        for c in range(n):
            k.mm(self.bank(b, 0, w), self.onesb[:, :], sq[:, c, 0:w], start=(c == 0), stop=(c == n - 1))
        k.act(lnv[:, 0:w], self.bank(b, 0, w), AF.Ln, bias=self.epscol[:, 0:1], scale=inv_n)
        k.act(rstd[:, 0:w], lnv[:, 0:w], AF.Exp, scale=-0.5)
        return rstd[:, 0:w]

    def _rms_stats(self, chunks, w, inv_n, banks=(0, 1, 2, 3, 4, 5, 6, 7), sq_eng="pool"):
        k = self.kb
        i = self.nrm_i % self.nbuf
        self.nrm_i += 1
        sq, lnv, rstd = self.sq[i], self.lnv[i], self.rstd[i]
        b = self.next_bank(banks)
        n = len(chunks)
        for c, ap in enumerate(chunks):
            k.tt(sq_eng, sq[:, c, 0:w], ap, ap, ALU.mult)
        for c in range(n):
            k.mm(self.bank(b, 0, w), self.onesb[:, :], sq[:, c, 0:w], start=(c == 0), stop=(c == n - 1))
        k.act(lnv[:, 0:w], self.bank(b, 0, w), AF.Ln, bias=self.epscol[:, 0:1], scale=inv_n)
        k.act(rstd[:, 0:w], lnv[:, 0:w], AF.Exp, scale=-0.5)
        return rstd[:, 0:w]

    def load_xT_block(self, x_rows, dstT, t0, ntiles, stage, sem_prefix, dst_chunks=8):
        k = self.kb
        for i in range(ntiles):
            st = stage[i % len(stage)]
            k.dma("sp", st[:], x_rows[i * 128:(i + 1) * 128, :], f"{sem_prefix}{i % len(stage)}")
            for h in range(2):
                b = self.next_bank()
                for cc in range(4):
                    c = h * 4 + cc
                    k.tr(self.bank(b, cc * 128, cc * 128 + 128), st[:, c * 128:(c + 1) * 128], self.identf[:])
                src = self.bank(b).rearrange("p (c t) -> p c t", c=4)
                dst = dstT[:, h * 4:h * 4 + 4, t0 + i * 128:t0 + (i + 1) * 128]
                k.copy("act" if (i + h) % 2 == 0 else "dve", dst, src)

    def store_xT_block(self, srcT, t0, ntiles, out_rows, stage, sem_prefix):
        k = self.kb
        for i in range(ntiles):
            st = stage[i % len(stage)]
            for h in range(2):
                b = self.next_bank()
                for cc in range(4):
                    c = h * 4 + cc
                    k.tr(self.bank(b, cc * 128, cc * 128 + 128), srcT[:, c, t0 + i * 128:t0 + (i + 1) * 128],
                         self.identf[:])
                k.copy("act" if (i + h) % 2 == 0 else "dve", st[:, h * 512:(h + 1) * 512], self.bank(b))
            k.dma("sp", out_rows[i * 128:(i + 1) * 128, :], st[:], f"{sem_prefix}{i % len(stage)}")

    def norm_block(self, xT, t0, w, gcols, outT, o0, out_dtype_bf16=True):
        k = self.kb
        rstd = self.rms_stats([xT[:, c, t0:t0 + w] for c in range(8)], w, 1.0 / D)
        for c in range(8):
            k.stt("dve", outT[:, c, o0:o0 + w], xT[:, c, t0:t0 + w], gcols[:, c:c + 1], rstd, ALU.mult, ALU.mult)


class Attn:
    def __init__(self, cm, npt=4):
        k = cm.kb
        self.cm = cm
        self.PT = [k.sb(f"pt{i}", [128, 2 * TB], BF16) for i in range(3)]
        self.rsum2 = [k.sb("rsumE", [128, TB], F32), k.sb("rsumO", [128, TB], F32)]
        k.memset("dve", self.rsum2[0][:], 0.0)
        k.memset("dve", self.rsum2[1][:], 0.0)
        self.bc = k.sb("bcs", [128, TB], F32)
        self.pi = 0
        self.si = 0
        self.oi = 0

    def unit(self, tiles, qfn, scale, par, dst, extra_sum=None):
        cm, k = self.cm, self.cm.kb
        nc = k.nc
        ob = 4 + (self.oi % 2)
        self.oi += 1
        n = len(tiles)
        LOOK = 2
        pts = {}

        def full(t):
            return t["c0"] == 0 and t["c1"] == TB and not t.get("masks")

        def same(a, b):
            return a.tensor.name == b.tensor.name and int(a.offset) == int(b.offset)

        steps = []
        i = 0
        while i < n:
            t = tiles[i]
            if i + 1 < n and full(t) and full(tiles[i + 1]) and same(t["bias"], tiles[i + 1]["bias"]):
                steps.append([i, i + 1])
                i += 2
            else:
                steps.append([i])
                i += 1

        def s_stage(si_):
            step = steps[si_]
            pi_ = self.si % 2
            self.si += 1
            for idx, ti in enumerate(step):
                t = tiles[ti]
                c0, c1 = t["c0"], t["c1"]
                sb_ = 2 * pi_ + idx
                masks = t.get("masks", ())
                k.mm(cm.bank(sb_, c0, c1), t["K"], qfn(c0, c1), start=True, stop=(len(masks) == 0))
                for mi, (m0, m1, mrhs) in enumerate(masks):
                    k.mm(cm.bank(sb_, m0, m1), cm.identb[:, :], mrhs, start=False, stop=(mi == len(masks) - 1))
            pt = self.PT[self.pi % len(self.PT)]
            self.pi += 1
            t0_ = tiles[step[0]]
            if len(step) == 2:
                k.act(pt[:, 0:2 * TB], cm.PS[pi_][:, 0:2 * TB], AF.Exp, bias=t0_["bias"], scale=scale)
            else:
                c0, c1 = t0_["c0"], t0_["c1"]
                k.act(pt[:, c0:c1], cm.bank(2 * pi_, c0, c1), AF.Exp, bias=t0_["bias"], scale=scale)
            pts[si_] = pt

        def pv_stage(si_):
            step = steps[si_]
            for idx, ti in enumerate(step):
                t = tiles[ti]
                c0, c1 = t["c0"], t["c1"]
                k.mm(cm.bank(ob, c0, c1), t["V"], pts[si_][:, idx * TB + c0:idx * TB + c1], start=(ti == 0), stop=(ti == n - 1))

        ns = len(steps)
        for i in range(ns + LOOK):
            if i < ns:
                s_stage(i)
            if i - LOOK >= 0:
                pv_stage(i - LOOK)
        sr = 64 if par == 0 else 0
        r0 = 0 if par == 0 else 64
        rsb = self.rsum2[par]
        rs = rsb[sr:sr + 1, :]
        src = cm.bank(ob, 0, TB, sr, sr + 1)
        k.act(rs, src, AF.Ln, bias=(extra_sum if extra_sum is not None else cm.zcol[sr:sr + 1, 0:1]), scale=1.0)
        k.act(rs, rs, AF.Exp, scale=-1.0)
        k.mm(cm.bank(6), cm.onesf[:, :], rsb[:, :], start=True, stop=True)
        k.copy("act", self.bc[r0:r0 + 64, :], cm.bank(6, 0, TB, r0, r0 + 64))
        k.tt("dve", dst, cm.bank(ob, 0, TB, r0, r0 + 64), self.bc[r0:r0 + 64, :], ALU.mult)


PROJ_BANKS = (0, 1, 2, 3, 7)


def proj_chunk(cm, W, c0, M, HT, h0, w, dst, eng, nk=8):
    k = cm.kb
    b = cm.next_bank(PROJ_BANKS)
    for kc in range(nk):
        k.mm(cm.bank(b, 0, w, 0, M), W[:, kc, c0:c0 + M], HT[:, kc, h0:h0 + w], start=(kc == 0), stop=(kc == nk - 1))
    if dst is not None:
        k.copy(eng, dst, cm.bank(b, 0, w, 0, M))
    return b


def mem_setup(cm, mem_in, gt_mem, Wm, stage):
    k = cm.kb
    memT = k.sb("memT", [128, 8, N_MEM], F32)
    MEMN = k.sb("memn", [128, 8, N_MEM], BF16)
    Kmem = k.sb("kmem", [128, 2, N_MEM], BF16)
    Vmem = k.sb("vmem", [128, 2, 4, 192], BF16)
    cm.load_xT_block(mem_in, memT, 0, 2, stage, "xs")
    cm.norm_block(memT, 0, N_MEM, gt_mem, MEMN, 0)
    k.memset("pool", Vmem[:], 1.0)
    for pr in range(2):
        proj_chunk(cm, Wm, pr * 128, 128, MEMN, 0, N_MEM, Kmem[:, pr, :], "act")
    for t in range(2):
        b = cm.next_bank(PROJ_BANKS)
        for kc in range(8):
            k.mm(cm.bank(b, 0, 256), MEMN[:, kc, t * 128:(t + 1) * 128], Wm[:, kc, 256:512], start=(kc == 0), stop=(kc == 7))
        k.copy("dve", Vmem[:, t, :, 64:128], cm.bank(b, 0, 256).rearrange("p (h d) -> p h d", h=4))
    return Kmem, Vmem


def cross_units(cm, at, Kmem, Vmem, QC, q0, AO, a0):
    for ch in range(4):
        par = ch % 2
        r0 = par * 64
        tiles = []
        for t in range(2):
            V = Vmem[:, t, ch, 64:192] if par == 0 else Vmem[:, t, ch, 0:128]
            tiles.append(dict(K=Kmem[:, ch // 2, t * 128:(t + 1) * 128], V=V, c0=0, c1=TB,
                              bias=cm.zcol[:, 0:1]))
        at.unit(tiles, lambda c0, c1, ch=ch: QC[:, ch, q0 + c0:q0 + c1], 0.125, par,
                AO[r0:r0 + 64, 6 + ch // 2, a0:a0 + TB])


def out_proj_block(cm, Wo, AO, a0, X, x0):
    k = cm.kb
    for co in range(8):
        b = cm.next_bank(PROJ_BANKS)
        for kc in range(8):
            k.mm(cm.bank(b), Wo[:, kc, co * 128:(co + 1) * 128], AO[:, kc, a0:a0 + TB], start=(kc == 0), stop=(kc == 7))
        k.tt("dve", X[:, co, x0:x0 + TB], X[:, co, x0:x0 + TB], cm.bank(b), ALU.add)


SWA_POS = [0, 3, 1, 4, 2, 5, 6, 9, 7, 10, 8, 11]


def alibi_tables():
    slopes = 2.0 ** (-8.0 * (np.arange(12, dtype=np.float64) + 1.0) / 12)
    kk = np.arange(128)[:, None]
    c = np.arange(256)[None, :]
    d = (c - kk).astype(np.float64)
    valid = (d >= 0) & (d < 128)
    out = np.zeros((128, 12, 256), np.float64)
    for p in range(12):
        out[:, p, :] = np.where(valid, -8.0 * slopes[SWA_POS[p]] * d, 8.0 * NEGB)
    import ml_dtypes
    hi = out.astype(np.float32).astype(ml_dtypes.bfloat16).astype(np.float32)
    lo = (out - hi).astype(np.float32).astype(ml_dtypes.bfloat16).astype(np.float32)
    return hi, lo


MLA_SCALE = 96.0 ** -0.5


class Ctx:
    def __init__(self, k, cm, at):
        self.k, self.cm, self.at = k, cm, at
        self.uid = 0

    def names(self):
        self.uid += 1
        u = self.uid
        return lambda n: f"{n}_{u}"


def mem_kv(ctx, U, MEMN, Wm, Kmem=None, Vmem=None):
    k, cm = ctx.k, ctx.cm
    if Kmem is None:
        Kmem = k.sb(U("kmem"), [128, 2, N_MEM], BF16)
        Vmem = k.sb(U("vmem"), [128, 2, 4, 192], BF16)
    k.memset("pool", Vmem[:], 1.0)
    for pr in range(2):
        proj_chunk(cm, Wm, pr * 128, 128, MEMN, 0, N_MEM, Kmem[:, pr, :], "act")
    for t in range(2):
        b = cm.next_bank(PROJ_BANKS)
        for kc in range(8):
            k.mm(cm.bank(b, 0, 256), MEMN[:, kc, t * 128:(t + 1) * 128], Wm[:, kc, 256:512], start=(kc == 0), stop=(kc == 7))
        k.copy("dve", Vmem[:, t, :, 64:128], cm.bank(b, 0, 256).rearrange("p (h d) -> p h d", h=4))
    return Kmem, Vmem


def emit_mlp(ctx, x_src, nb, out, gcol, up_in, dn_in, final_gcol=None):
    k, cm = ctx.k, ctx.cm
    U = ctx.names()
    k.push_scope()
    upv = up_in.rearrange("(kc p) f -> p kc f", p=128)
    dnv = dn_in.rearrange("(fc p) d -> p fc d", p=128)
    wup = [k.sb(U(f"wup{i}"), [128, 8, 512], BF16) for i in range(2)]
    wdn = [k.sb(U(f"wdn{i}"), [128, 4, D], BF16) for i in range(2)]
    X = k.sb(U("xT"), [128, 8, nb * TB], F32)
    H = k.sb(U("hT"), [128, 8, nb * TB], BF16)
    stage = [k.sb(U(f"stg{i}"), [128, D], F32) for i in range(2)]
    ostage = [k.sb(U("ostg"), [128, D], F32)]
    rl = [k.sb(U(f"rl{i}"), [128, TB], F32) for i in range(2)]
    aT = [k.sb(U(f"aT{i}"), [128, 4, TB], BF16) for i in range(2)]
    ri = 0

    def wload(fb):
        k.dma("pool", wup[fb % 2][:], upv[:, :, fb * 512:(fb + 1) * 512], f"wup{fb % 2}")
        k.dma("pool", wdn[fb % 2][:], dnv[:, fb * 4:(fb + 1) * 4, :], f"wdn{fb % 2}")

    def up_stage(fb, j, A):
        nonlocal ri
        for q in range(4):
            b = cm.next_bank((0, 1, 2, 3))
            for kc in range(8):
                k.mm(cm.bank(b), wup[fb % 2][:, kc, q * 128:(q + 1) * 128], H[:, kc, j * TB:(j + 1) * TB],
                     start=(kc == 0), stop=(kc == 7))
            r = rl[ri % 2]
            ri += 1
            k.act(r[:], cm.bank(b), AF.Relu)
            k.tt("pool", A[:, q, :], r[:], r[:], ALU.mult)

    def down_stage(fb, j, A):
        for c in range(8):
            b = cm.next_bank((4, 5, 6, 7))
            for q in range(4):
                k.mm(cm.bank(b), wdn[fb % 2][:, q, c * 128:(c + 1) * 128], A[:, q, :], start=(q == 0), stop=(q == 3))
            k.tt("dve", X[:, c, j * TB:(j + 1) * TB], X[:, c, j * TB:(j + 1) * TB], cm.bank(b), ALU.add)
        if fb == 7:
            if final_gcol is not None:
                rstd = cm.rms_stats([X[:, c, j * TB:(j + 1) * TB] for c in range(8)], TB, 1.0 / D)
                for c in range(8):
                    k.stt("dve", X[:, c, j * TB:(j + 1) * TB], X[:, c, j * TB:(j + 1) * TB], final_gcol[:, c:c + 1], rstd,
                          ALU.mult, ALU.mult)
            cm.store_xT_block(X, j * TB, 4, out[j * TB:(j + 1) * TB, :], ostage, "os")

    wload(0)
    wload(1)
    seq = [(fb, j) for fb in range(8) for j in range(nb)]
    for i in range(len(seq) + 1):
        if i < len(seq):
            fb, j = seq[i]
            if fb == 0:
                cm.load_xT_block(x_src[j * TB:(j + 1) * TB, :], X, j * TB, 4, stage, "xs")
                cm.norm_block(X, j * TB, TB, gcol, H, j * TB)
            up_stage(fb, j, aT[i % 2])
        if i >= 1:
            fbp, jp = seq[i - 1]
            down_stage(fbp, jp, aT[(i - 1) % 2])
            if jp == nb - 1 and fbp + 2 < 8:
                wload(fbp + 2)
    k.pop_scope()


def emit_swa(ctx, x_src, nb, bnd_src, bnd_bias, out, gcol, MEMN, wi_in, wm_in, wo_in, sk, bmh_in, bml_in):
    k, cm, at = ctx.k, ctx.cm, ctx.at
    U = ctx.names()
    k.push_scope()
    Wi = k.sb(U("wi"), [128, 8, 1536], BF16)
    Wm = k.sb(U("wm"), [128, 8, 512], BF16)
    Wo = k.sb(U("wo"), [128, 8, D], BF16)
    BMh = k.sb(U("bmh"), [128, 12, 256], BF16)
    BMl = k.sb(U("bml"), [128, 12, 256], BF16)
    k.dma("pool", Wm[:], wm_in.rearrange("(kc p) f -> p kc f", p=128), "w0")
    k.dma("pool", Wi[:], wi_in.rearrange("(kc p) f -> p kc f", p=128), "w1")
    k.dma("pool", BMh[:], bmh_in, "w2")
    k.dma("pool", BMl[:], bml_in, "w3")
    k.dma("pool", Wo[:], wo_in.rearrange("(kc p) f -> p kc f", p=128), "w4")
    stage = [k.sb(U(f"stg{i}"), [128, D], F32) for i in range(2)]
    ostage = [k.sb(U("ostg"), [128, D], F32)]
    Kmem, Vmem = mem_kv(ctx, U, MEMN, Wm)
    nt = nb * TB
    KS = k.sb(U("ks"), [128, 2, nt + 128], BF16)
    VS = k.sb(U("vs"), [128, 4 * nb + 1, 4, 192], BF16)
    QS = k.sb(U("qs"), [128, 12, TB], BF16)
    k.memset("pool", QS[:], 0.0)
    QC = k.sb(U("qc"), [128, 4, TB], BF16)
    k.memset("pool", QC[:], 0.0)
    AO = k.sb(U("ao"), [128, 8, TB], BF16)
    X = k.sb(U("xblk"), [128, 8, TB], F32)
    HT = k.sb(U("ht"), [128, 8, TB], BF16)
    XB = k.sb(U("xb"), [128, 8, 128], F32)
    k.memset("pool", VS[:], 1.0)
    if bnd_src is not None:
        cm.load_xT_block(bnd_src, XB, 0, 1, stage, "xs")
        cm.norm_block(XB, 0, 128, gcol, HT, 0)
        for c in range(2):
            proj_chunk(cm, Wi, 768 + c * 128, 128, HT, 0, 128, KS[:, c, 0:128], "act")
        b = cm.next_bank(PROJ_BANKS)
        for kc in range(8):
            k.mm(cm.bank(b, 0, 256), HT[:, kc, 0:128], Wi[:, kc, 1024:1280], start=(kc == 0), stop=(kc == 7))
        k.copy("dve", VS[:, 0, :, 64:128], cm.bank(b, 0, 256).rearrange("p (h d) -> p h d", h=4))
    else:
        k.memset("pool", KS[:, :, 0:128], 0.0)
    for j in range(nb):
        cm.load_xT_block(x_src[j * TB:(j + 1) * TB, :], X, 0, 4, stage, "xs")
        cm.norm_block(X, 0, TB, gcol, HT, 0)
        for c in range(6):
            b = proj_chunk(cm, Wi, c * 128, 128, HT, 0, TB, None, "dve")
            k.copy("act", QS[0:64, 2 * c, :], cm.bank(b, 0, TB, 0, 64))
            k.copy("dve", QS[64:128, 2 * c + 1, :], cm.bank(b, 0, TB, 64, 128))
        for c in range(2):
            proj_chunk(cm, Wi, 768 + c * 128, 128, HT, 0, TB, KS[:, c, 128 + j * TB:128 + (j + 1) * TB], "act")
        for c in range(2):
            b = proj_chunk(cm, Wi, 1280 + c * 128, 128, HT, 0, TB, None, "dve")
            k.copy("dve", QC[0:64, 2 * c, :], cm.bank(b, 0, TB, 0, 64))
            k.copy("dve", QC[64:128, 2 * c + 1, :], cm.bank(b, 0, TB, 64, 128))
        for t in range(4):
            b = cm.next_bank(PROJ_BANKS)
            for kc in range(8):
                k.mm(cm.bank(b, 0, 256), HT[:, kc, t * 128:(t + 1) * 128], Wi[:, kc, 1024:1280], start=(kc == 0), stop=(kc == 7))
            k.copy("dve" if t % 2 else "act", VS[:, 1 + 4 * j + t, :, 64:128],
                   cm.bank(b, 0, 256).rearrange("p (h d) -> p h d", h=4))
        cross_units(cm, at, Kmem, Vmem, QC, 0, AO, 0)
        for p in range(12):
            kh = SWA_POS[p] // 3
            par = p % 2
            r0 = par * 64
            tiles = []
            for i in range(5):
                s = 4 * j + i
                c0 = max(0, (i - 1) * 128)
                c1 = min(TB, (i + 1) * 128)
                m0 = 128 if i == 0 else 0
                V = VS[:, s, kh, 64:192] if par == 0 else VS[:, s, kh, 0:128]
                tiles.append(dict(K=KS[:, kh // 2, s * 128:(s + 1) * 128], V=V, c0=c0, c1=c1,
                                  bias=(bnd_bias if s == 0 else cm.zcol[:, 0:1]),
                                  masks=[(c0, c1, BMh[:, p, m0:m0 + (c1 - c0)]), (c0, c1, BMl[:, p, m0:m0 + (c1 - c0)])]))
            sr = 64 if par == 0 else 0
            at.unit(tiles, lambda c0, c1, p=p: QS[:, p, c0:c1], 0.125, par,
                    AO[r0:r0 + 64, p // 2, :], extra_sum=sk[sr:sr + 1, p:p + 1])
        out_proj_block(cm, Wo, AO, 0, X, 0)
        cm.store_xT_block(X, 0, 4, out[j * TB:(j + 1) * TB, :], ostage, "os")
    k.pop_scope()


def emit_mla(ctx, x_src, nb, prev_src, npb, pos_own, pos_prev, prev_bias, out, g_attn, g_q, g_kv, MEMN,
             rc, tri, wi_in, wuq_in, wukv_in, wm_in, wo_in):
    k, cm, at = ctx.k, ctx.cm, ctx.at
    U = ctx.names()
    nt = nb * TB
    npv = npb * TB
    nkey = nt + npv
    nkt = nkey // 128
    k.push_scope()
    Wi = k.sb(U("wi"), [128, 8, 1088], BF16)
    Wukv = k.sb(U("wukv"), [128, 2, 1536], BF16)
    CKVN = k.sb(U("ckvn"), [128, 2, nkey], BF16)
    KTs = [k.sb(U("kt0"), [128, nkey], BF16), k.sb(U("kt1"), [128, nkey], BF16)]
    CQN = k.sb(U("cqn"), [128, 3, nt], BF16)
    CC = k.sb(U("cc"), [128, nt], F32)
    SS = k.sb(U("ss"), [128, nt], F32)
    AO = k.sb(U("ao"), [128, 8, nt], BF16)
    rt1 = k.sb(U("rt1"), [128, TB], F32)
    rt2 = k.sb(U("rt2"), [128, TB], F32)
    k.dma("pool", Wi[:], wi_in.rearrange("(kc p) f -> p kc f", p=128), "w1")
    k.dma("pool", Wukv[:], wukv_in.rearrange("(kc p) f -> p kc f", p=128), "w2")

    def rope(dsts, A, B, cc, ss):
        k.tt("dve", rt1[64:96, :], A, cc, ALU.mult)
        k.tt("dve", rt2[64:96, :], B, ss, ALU.mult)
        for dst in dsts:
            k.tt("pool", dst, rt1[64:96, :], rt2[64:96, :], ALU.add)

    k.push_scope()
    X = k.sb(U("xblk"), [128, 8, TB], F32)
    HT = k.sb(U("ht"), [128, 8, TB], BF16)
    stage = [k.sb(U(f"stg{i}"), [128, D], F32) for i in range(2)]
    QC = k.sb(U("qc"), [128, 4, TB], BF16)
    k.memset("pool", QC[:], 0.0)
    Kmem = k.sb(U("kmem"), [128, 2, N_MEM], BF16)
    Vmem = k.sb(U("vmem"), [128, 2, 4, 192], BF16)
    k.push_scope()
    Wm = k.sb(U("wm"), [128, 8, 512], BF16)
    k.dma("pool", Wm[:], wm_in.rearrange("(kc p) f -> p kc f", p=128), "w0")
    mem_kv(ctx, U, MEMN, Wm, Kmem, Vmem)
    k.pop_scope()
    posi = k.sb(U("posi"), [128, TB], I32)
    yv = k.sb(U("yv"), [128, TB], F32)
    yi = k.sb(U("yi"), [128, TB], I32)
    yf = k.sb(U("yf"), [128, TB], F32)
    fr = k.sb(U("fr"), [128, TB], F32)
    cmp_ = k.sb(U("cmp"), [128, TB], F32)
    CCt = k.sb(U("cct"), [128, TB], F32)
    SSt = k.sb(U("sst"), [128, TB], F32)

    def frac_sin(dst, add, scale_ap):
        src = yv
        if add != 0.0:
            k.ts("dve", yf[:], yv[:], add, ALU.add)
            src = yf
            k.copy("dve", yi[:], yf[:])
        else:
            k.copy("dve", yi[:], yv[:])
        k.copy("dve", fr[:], yi[:])
        k.tt("dve", fr[:], src[:], fr[:], ALU.subtract)
        k.ts("dve", cmp_[:], fr[:], 0.5, ALU.is_gt)
        k.tt("dve", fr[:], fr[:], cmp_[:], ALU.subtract)
        k.ts("dve", cmp_[:], fr[:], -0.5, ALU.is_lt)
        k.tt("dve", fr[:], fr[:], cmp_[:], ALU.add)
        k.act(dst, fr[:], AF.Sin, scale=scale_ap)

    def rope_tables(pos_slice, cc, ss):
        k.dma("sp", posi[:], pos_slice.partition_broadcast(128), "posd")
        k.copy("dve", yv[:], posi[:])
        k.ts("dve", yv[:], yv[:], rc[:, 0:1], ALU.mult)
        frac_sin(ss, 0.0, rc[:, 1:2])
        frac_sin(cc, 0.25, rc[:, 2:3])

    def latents_block(xsrc, t0, koff, pos_src, own):
        cm.load_xT_block(xsrc[t0:t0 + TB, :], X, 0, 4, stage, "xs")
        cm.norm_block(X, 0, TB, g_attn, HT, 0)
        if own:
            cc, ss = CC[:, t0:t0 + TB], SS[:, t0:t0 + TB]
        else:
            cc, ss = CCt[:, :], SSt[:, :]
        rope_tables(pos_src[t0:t0 + TB], cc, ss)
        if own:
            for c in range(3):
                for kc in range(8):
                    k.mm(cm.bank(c), Wi[:, kc, c * 128:(c + 1) * 128], HT[:, kc, :], start=(kc == 0), stop=(kc == 7))
            rstd = cm.rms_stats([cm.bank(c) for c in range(3)], TB, 1.0 / 384, banks=(5,), sq_eng="act")
            for c in range(3):
                k.stt("dve", CQN[:, c, t0:t0 + TB], cm.bank(c), g_q[:, c:c + 1], rstd, ALU.mult, ALU.mult)
        for c in range(2):
            for kc in range(8):
                k.mm(cm.bank(3 + c), Wi[:, kc, 384 + c * 128:384 + (c + 1) * 128], HT[:, kc, :], start=(kc == 0), stop=(kc == 7))
        rstd = cm.rms_stats([cm.bank(3 + c) for c in range(2)], TB, 1.0 / 256, banks=(5,), sq_eng="act")
        for c in range(2):
            k.stt("dve", CKVN[:, c, koff + t0:koff + t0 + TB], cm.bank(3 + c), g_kv[:, c:c + 1], rstd, ALU.mult, ALU.mult)
        for i, b in enumerate((6, 7)):
            for kc in range(8):
                k.mm(cm.bank(b, 0, TB, 0, 96), Wi[:, kc, 896 + i * 96:896 + (i + 1) * 96], HT[:, kc, :], start=(kc == 0), stop=(kc == 7))
        rope([KT_[64:96, koff + t0:koff + t0 + TB] for KT_ in KTs], cm.bank(6, 0, TB, 64, 96), cm.bank(7, 0, TB, 64, 96),
             cc[64:96, :], ss[64:96, :])
        if own:
            for c in range(2):
                b = proj_chunk(cm, Wi, 640 + c * 128, 128, HT, 0, TB, None, "dve")
                k.copy("dve", QC[0:64, 2 * c, :], cm.bank(b, 0, TB, 0, 64))
                k.copy("dve", QC[64:128, 2 * c + 1, :], cm.bank(b, 0, TB, 64, 128))
            cross_units(cm, at, Kmem, Vmem, QC, 0, AO, t0)

    for j in range(npb):
        latents_block(prev_src, j * TB, 0, pos_prev, False)
    for j in range(nb):
        latents_block(x_src, j * TB, npv, pos_own, True)
    k.pop_scope()

    k.push_scope()
    Wo = Wi
    k.dma("pool", Wo[:, :, 0:D], wo_in.rearrange("(kc p) f -> p kc f", p=128), "w3")
    VV = [k.sb(U("ve"), [128, nkt, 128], BF16), k.sb(U("vo"), [128, nkt, 128], BF16)]
    QT = [k.sb(U(f"qt{i}"), [128, nt], BF16) for i in range(2)]
    Wq = [k.sb(U(f"wq{i}"), [128, 3, 192], BF16) for i in range(2)]
    k.memset("pool", VV[0][:], 1.0)
    k.memset("pool", VV[1][:], 1.0)
    wuqv = wuq_in.rearrange("(kc p) f -> p kc f", p=128)
    P2 = (7, 6)
    zb = cm.zcol[:, 0:1]

    def prologue(h):
        par = h % 2
        wq, qt, Vh, KT = Wq[par], QT[par], VV[par], KTs[par]
        voff = 0 if par == 0 else 64
        k.dma("pool", wq[:], wuqv[:, :, h * 192:(h + 1) * 192], f"wq{par}")
        for j in range(nb):
            t0 = j * TB
            bA = cm.next_bank(P2)
            for kc in range(3):
                k.mm(cm.bank(bA, 0, TB, 0, 96), wq[:, kc, 0:96], CQN[:, kc, t0:t0 + TB], start=(kc == 0), stop=(kc == 2))
            bB = cm.next_bank(P2)
            for kc in range(3):
                k.mm(cm.bank(bB, 0, TB, 0, 96), wq[:, kc, 96:192], CQN[:, kc, t0:t0 + TB], start=(kc == 0), stop=(kc == 2))
            k.copy("dve", qt[0:64, t0:t0 + TB], cm.bank(bA, 0, TB, 0, 64))
            rope([qt[64:96, t0:t0 + TB]], cm.bank(bA, 0, TB, 64, 96), cm.bank(bB, 0, TB, 64, 96), CC[64:96, t0:t0 + TB], SS[64:96, t0:t0 + TB])
        for kb in range(nkey // TB):
            b = cm.next_bank(P2)
            for kc in range(2):
                k.mm(cm.bank(b), Wukv[:, kc, 128 * h:128 * h + 128], CKVN[:, kc, kb * TB:(kb + 1) * TB], start=(kc == 0), stop=(kc == 1))
            k.copy("dve", KT[0:64, kb * TB:(kb + 1) * TB], cm.bank(b, 0, TB, 0, 64))
        for g in range(nkt // 8):
            b = cm.next_bank(P2)
            for tl in range(8):
                kt_ = g * 8 + tl
                for kc in range(2):
                    k.mm(cm.bank(b, tl * 64, (tl + 1) * 64), CKVN[:, kc, kt_ * 128:(kt_ + 1) * 128],
                         Wukv[:, kc, 128 * h + 64:128 * h + 128], start=(kc == 0), stop=(kc == 1))
            k.copy("dve", Vh[:, g * 8:(g + 1) * 8, voff:voff + 64], cm.bank(b).rearrange("p (t d) -> p t d", t=8))

    def attention(h):
        par = h % 2
        r0 = par * 64
        qt, Vh, KT = QT[par], VV[par], KTs[par]
        for j in range(nb):
            t0 = j * TB
            tiles = []
            for kt_ in range(npb * 4):
                tiles.append(dict(K=KT[0:96, kt_ * 128:(kt_ + 1) * 128], V=Vh[:, kt_, :], c0=0, c1=TB, bias=prev_bias))
            for kt_ in range(npb * 4, npb * 4 + 4 * j):
                tiles.append(dict(K=KT[0:96, kt_ * 128:(kt_ + 1) * 128], V=Vh[:, kt_, :], c0=0, c1=TB, bias=zb))
            for t in range(4):
                kt_ = npb * 4 + 4 * j + t
                tiles.append(dict(K=KT[0:96, kt_ * 128:(kt_ + 1) * 128], V=Vh[:, kt_, :], c0=128 * t, c1=TB,
                                  bias=zb, masks=[(128 * t, 128 * t + 128, tri[:, :])]))
            at.unit(tiles, lambda c0, c1, qt=qt, t0=t0: qt[0:96, t0 + c0:t0 + c1], MLA_SCALE, par,
                    AO[r0:r0 + 64, h // 2, t0:t0 + TB])

    prologue(0)
    for h in range(12):
        if h + 1 < 12:
            prologue(h + 1)
        attention(h)
    k.pop_scope()

    k.push_scope()
    X = k.sb(U("xblk3"), [128, 8, TB], F32)
    stage = [k.sb(U(f"stg3{i}"), [128, D], F32) for i in range(2)]
    ostage = [k.sb(U("ostg3"), [128, D], F32)]
    for j in range(nb):
        cm.load_xT_block(x_src[j * TB:(j + 1) * TB, :], X, 0, 4, stage, "xs3")
        out_proj_block(cm, Wo, AO, j * TB, X, 0)
        cm.store_xT_block(X, 0, 4, out[j * TB:(j + 1) * TB, :], ostage, "os")
    k.pop_scope()
    k.pop_scope()


GCOLS = 96


def build_fused(only=None):
    k = KB()
    nc = k.nc
    x_in = k.dram_in("x", [NT, D])
    xp_in = k.dram_in("xprev", [NT, D])
    mem_in = k.dram_in("mem", [N_MEM, D])
    g_in = k.dram_in("g", [128, GCOLS])
    pos_in = k.dram_in("pos", [NT], I32)
    posp_in = k.dram_in("posprev", [NT], I32)
    rc_in = k.dram_in("ropec", [128, 4])
    tri_in = k.dram_in("trimask", [128, 128])
    pb_in = k.dram_in("prevbias", [128, 1])
    ident_in = k.dram_in("ident", [128, 128])
    sk_in = k.dram_in("sinks", [128, 24])
    bmh_in = k.dram_in("bmhi", [128, 12, 256])
    bml_in = k.dram_in("bmlo", [128, 12, 256])
    W = {}
    for l in range(4):
        W[f"wm{l}"] = k.dram_in(f"w_mem{l}", [D, 512])
        W[f"wo{l}"] = k.dram_in(f"w_o{l}", [D, D])
        W[f"up{l}"] = k.dram_in(f"w_up{l}", [D, DFF])
        W[f"dn{l}"] = k.dram_in(f"w_dn{l}", [DFF, D])
        if l % 2 == 0:
            W[f"wi{l}"] = k.dram_in(f"w_in{l}", [D, 1088])
            W[f"wuq{l}"] = k.dram_in(f"w_uq{l}", [384, 12 * 192])
            W[f"wukv{l}"] = k.dram_in(f"w_ukv{l}", [256, 1536])
        else:
            W[f"wi{l}"] = k.dram_in(f"w_in{l}", [D, 1536])
    y_out = k.dram_out("y", [NT, D])
    scr = {n: nc.dram_tensor("scr_" + n, [NT, D], F32).ap() for n in ("p1a", "p1b", "p1c", "p1d", "p1e", "p2a", "p2b")}
    cm = Common(k, {"ident": ident_in}, nbuf=1)
    at = Attn(cm)
    ctx = Ctx(k, cm, at)
    G = k.sb("G", [128, GCOLS], F32)
    rc = k.sb("rc", [128, 4], F32)
    pb = k.sb("pb", [128, 1], F32)
    negc = k.sb("negc", [128, 1], F32)
    sk = k.sb("sk", [128, 24], F32)
    tri = k.sb("tri", [128, 128], BF16)
    MEMN = k.sb("memn", [128, 8, N_MEM], BF16)
    k.dma("sp", G[:], g_in, "const")
    k.dma("sp", rc[:], rc_in, "const")
    k.dma("sp", pb[:], pb_in, "const")
    k.dma("sp", sk[:], sk_in, "const")
    k.dma("pool", tri[:], tri_in, "constp")
    k.memset("dve", negc[:], NEGB)
    k.act(sk[:], sk[:], AF.Exp)
    k.push_scope()
    mT = k.sb("memT", [128, 8, N_MEM], F32)
    mstage = [k.sb(f"mstg{i}", [128, D], F32) for i in range(2)]
    cm.load_xT_block(mem_in, mT, 0, 2, mstage, "xs")
    cm.norm_block(mT, 0, N_MEM, G[:, 64:72], MEMN, 0)
    k.pop_scope()

    def ga(l):
        return G[:, l * 16:l * 16 + 8]

    def gm(l):
        return G[:, l * 16 + 8:l * 16 + 16]

    def mla(l, x_src, nb, prev_src, npb, pos_own, pos_prev, prev_bias, out):
        j = l // 2
        emit_mla(ctx, x_src, nb, prev_src, npb, pos_own, pos_prev, prev_bias, out, ga(l),
                 G[:, 80 + 5 * j:83 + 5 * j], G[:, 83 + 5 * j:85 + 5 * j], MEMN, rc, tri,
                 W[f"wi{l}"], W[f"wuq{l}"], W[f"wukv{l}"], W[f"wm{l}"], W[f"wo{l}"])

    def swa(l, x_src, nb, bnd_src, bnd_bias, out):
        j = l // 2
        emit_swa(ctx, x_src, nb, bnd_src, bnd_bias, out, ga(l), MEMN, W[f"wi{l}"], W[f"wm{l}"], W[f"wo{l}"],
                 sk[:, 12 * j:12 * j + 12], bmh_in, bml_in)

    def mlp(l, x_src, nb, out, final=False):
        emit_mlp(ctx, x_src, nb, out, gm(l), W[f"up{l}"], W[f"dn{l}"], G[:, 72:80] if final else None)

    z = cm.zcol[:, 0:1]
    if only is not None:
        if only == "mla":
            mla(0, x_in, NB, xp_in, NB, pos_in, posp_in, pb[:, 0:1], y_out)
        elif only == "mlalite":
            mla(0, xp_in, NB, None, 0, posp_in, None, z, y_out)
        elif only == "swa":
            swa(1, x_in, NB, xp_in[NT - 128:NT, :], pb[:, 0:1], y_out)
        elif only == "mlp":
            mlp(0, x_in, NB, y_out)
        k.finish()
        return k
    mla(0, xp_in, NB, None, 0, posp_in, None, z, scr["p1a"])
    mlp(0, scr["p1a"], NB, scr["p1b"])
    swa(1, scr["p1b"], NB, None, negc[:, 0:1], scr["p1a"])
    mlp(1, scr["p1a"], NB, scr["p1c"])
    mla(2, scr["p1c"][3 * TB:4 * TB, :], 1, scr["p1c"][0:3 * TB, :], 3, posp_in[3 * TB:4 * TB], posp_in[0:3 * TB], z,
        scr["p1d"][0:TB, :])
    mlp(2, scr["p1d"][0:TB, :], 1, scr["p1e"][0:TB, :])
    mla(0, x_in, NB, xp_in, NB, pos_in, posp_in, pb[:, 0:1], scr["p2a"])
    mlp(0, scr["p2a"], NB, scr["p2b"])
    swa(1, scr["p2b"], NB, scr["p1b"][NT - 128:NT, :], pb[:, 0:1], scr["p2a"])
    mlp(1, scr["p2a"], NB, scr["p2b"])
    mla(2, scr["p2b"], NB, scr["p1c"], NB, pos_in, posp_in, pb[:, 0:1], scr["p2a"])
    mlp(2, scr["p2a"], NB, scr["p2b"])
    swa(3, scr["p2b"], NB, scr["p1e"][TB - 128:TB, :], pb[:, 0:1], scr["p2a"])
    mlp(3, scr["p2a"], NB, y_out, final=True)
    k.finish()
    return k


N_CORES = 8


def _f(a):
    return np.ascontiguousarray(a, dtype=np.float32)


def kernel(x, mem, positions, attn_norm_g, mlp_norm_g, mem_norm_g, final_norm_g,
           mla_w_in, mla_q_norm_g, mla_kv_norm_g, mla_w_uq, mla_w_ukv,
           swa_w_in, swa_sinks, w_mem_kv, w_o, mlp_w_up, mlp_w_down):
    x = np.asarray(x, dtype=np.float32)
    mem = np.asarray(mem, dtype=np.float32)
    positions = np.asarray(positions)
    B, S, _ = x.shape
    shared = {}
    g = np.zeros((128, GCOLS), np.float32)
    for l in range(4):
        g[:, l * 16:l * 16 + 8] = np.asarray(attn_norm_g[l]).reshape(8, 128).T
        g[:, l * 16 + 8:l * 16 + 16] = np.asarray(mlp_norm_g[l]).reshape(8, 128).T
    g[:, 64:72] = np.asarray(mem_norm_g).reshape(8, 128).T
    g[:, 72:80] = np.asarray(final_norm_g).reshape(8, 128).T
    for j in range(2):
        g[:, 80 + 5 * j:83 + 5 * j] = np.asarray(mla_q_norm_g[j]).reshape(3, 128).T
        g[:, 83 + 5 * j:85 + 5 * j] = np.asarray(mla_kv_norm_g[j]).reshape(2, 128).T
    shared["g"] = g
    r = np.arange(128)
    inv = 10000.0 ** (-(np.arange(16, dtype=np.float64) * 2.0) / 32)
    rcv = np.zeros((128, 4), np.float64)
    rcv[:, 0] = inv[r % 16] / (2 * np.pi)
    rcv[:, 1] = np.where((r % 32) < 16, -1.0, 1.0) * 2 * np.pi
    rcv[:, 2] = 2 * np.pi
    shared["ropec"] = _f(rcv)
    kk = np.arange(128)[:, None]
    cc = np.arange(128)[None, :]
    shared["trimask"] = _f(np.where(kk <= cc, 0.0, 8.0 * NEGB))
    shared["ident"] = np.eye(128, dtype=np.float32)
    sk = np.concatenate([np.asarray(swa_sinks[j])[SWA_POS] for j in range(2)])
    shared["sinks"] = _f(np.broadcast_to(sk[None, :], (128, 24)))
    hi, lo = alibi_tables()
    shared["bmhi"], shared["bmlo"] = _f(hi), _f(lo)
    for l in range(4):
        j = l // 2
        shared[f"w_mem{l}"] = _f(w_mem_kv[l])
        shared[f"w_up{l}"] = _f(mlp_w_up[l])
        shared[f"w_dn{l}"] = _f(mlp_w_down[l])
        if l % 2 == 0:
            w_in = np.asarray(mla_w_in[j])
            kr = w_in[:, 640:672]
            krp = np.concatenate([kr[:, 16:32], kr[:, 0:16]], axis=1)
            pad = w_in[:, 576:640]
            shared[f"w_in{l}"] = _f(np.concatenate([w_in[:, 0:640], w_in[:, 672:928], pad, kr, pad, krp], axis=1))
            w_uq = np.asarray(mla_w_uq[j])
            hq = []
            for h in range(12):
                wh = w_uq[:, h * 96:(h + 1) * 96]
                hq += [wh, wh[:, 0:64], wh[:, 80:96], wh[:, 64:80]]
            shared[f"w_uq{l}"] = _f(np.concatenate(hq, axis=1))
            shared[f"w_ukv{l}"] = _f(mla_w_ukv[j])
            shared[f"w_o{l}"] = _f(w_o[l])
        else:
            w_in = np.asarray(swa_w_in[j])
            wq = np.concatenate([w_in[:, h * 64:(h + 1) * 64] for h in SWA_POS], axis=1)
            shared[f"w_in{l}"] = _f(np.concatenate([wq, w_in[:, 768:]], axis=1))
            wo = np.asarray(w_o[l])
            shared[f"w_o{l}"] = _f(np.concatenate([wo[h * 64:(h + 1) * 64] for h in SWA_POS] + [wo[768:]], axis=0))
    in_maps = []
    for c in range(N_CORES):
        b, half = c // 2, c % 2
        m = dict(shared)
        m["x"] = _f(x[b, half * NT:(half + 1) * NT])
        m["xprev"] = _f(x[b, 0:NT])
        m["mem"] = _f(mem[b])
        m["pos"] = np.ascontiguousarray(positions[b, half * NT:(half + 1) * NT], dtype=np.int32)
        m["posprev"] = np.ascontiguousarray(positions[b, 0:NT], dtype=np.int32)
        m["prevbias"] = np.full((128, 1), 0.0 if half else NEGB, np.float32)
        in_maps.append(m)
    kb = build_fused()
    res = run_bass_kernel_spmd(kb.nc, in_maps, core_ids=list(range(N_CORES)))
    out = np.empty((B, S, D), np.float32)
    for c in range(N_CORES):
        out[c // 2, (c % 2) * NT:(c % 2 + 1) * NT] = np.asarray(res.results[c]["y"])
    return out
```

```python
import numpy as np
from contextlib import ExitStack
import concourse.bass as bass
import concourse.mybir as mybir
from concourse.bass_utils import run_bass_kernel_spmd

F32, BF16, I32 = mybir.dt.float32, mybir.dt.bfloat16, mybir.dt.int32
ALU = mybir.AluOpType
AF = mybir.ActivationFunctionType

D = 1024
NT = 2048
TB = 512
NB = NT // TB
DFF = 4096
EPS = 1e-6
NEGB = -30000.0
N_MEM = 256


class KB:
    def __init__(self):
        self.nc = bass.Bass("TRN2", target_bir_lowering=False)
        nc = self.nc
        self.E = dict(pe=nc.tensor, act=nc.scalar, dve=nc.vector, pool=nc.gpsimd, sp=nc.sync)
        self.csem = {e: nc.alloc_semaphore("s_" + e) for e in ("pe", "act", "dve", "pool")}
        self.cnt = {e: 0 for e in self.csem}
        self.waited = {e: {} for e in self.E}
        self.trk = {}
        self.W = {}
        self.dsem = {}
        self.dcnt = {}
        self.semh = {}
        for e, h in self.csem.items():
            self.semh["c_" + e] = h
        self.es = ExitStack()
        self.n_ins = 0

    def sb(self, name, shape, dtype):
        t = self.es.enter_context(self.nc.sbuf_tensor(name, list(shape), dtype))
        self.W[name] = int(np.prod(shape[1:]))
        self.trk[name] = []
        return t

    def ps(self, name, shape, dtype=F32):
        t = self.es.enter_context(self.nc.psum_tensor(name, list(shape), dtype))
        self.W[name] = int(np.prod(shape[1:]))
        self.trk[name] = []
        return t

    def push_scope(self):
        self._scopes = getattr(self, "_scopes", [])
        self._scopes.append(self.es)
        self.es = ExitStack()

    def pop_scope(self):
        self.barrier()
        self.es.close()
        self.es = self._scopes.pop()

    def barrier(self):
        toks = [("c_" + e, c) for e, c in self.cnt.items() if c > 0]
        toks += [("d_" + s, c * 16) for s, c in self.dcnt.items() if c > 0]
        for e in self.E:
            self._wait(e, toks)

    def dram_in(self, name, shape, dtype=F32):
        return self.nc.dram_tensor(name, list(shape), dtype, kind="ExternalInput").ap()

    def dram_out(self, name, shape, dtype=F32):
        return self.nc.dram_tensor(name, list(shape), dtype, kind="ExternalOutput").ap()

    def dma_sem(self, name):
        if name not in self.dsem:
            self.dsem[name] = self.nc.alloc_semaphore("d_" + name)
            self.dcnt[name] = 0
            self.semh["d_" + name] = self.dsem[name]
        return name

    def _region(self, ap):
        name = ap.tensor.name
        if name not in self.W:
            return None
        W = self.W[name]
        off = int(ap.offset)
        a = ap.ap
        p0 = off // W
        f0 = off % W
        p1 = p0 + a[0][1]
        hi = f0 + sum((c - 1) * s for s, c in a[1:]) + 1
        if name.startswith("ps"):
            return name, 0, 128, (f0 // 512) * 512, ((hi + 511) // 512) * 512
        return name, p0, p1, f0, hi

    def _deps(self, reads, writes):
        toks = set()
        for ap in reads:
            r = self._region(ap)
            if r is None:
                continue
            name, p0, p1, lo, hi = r
            for e in self.trk[name]:
                if e[4] and e[0] < p1 and p0 < e[1] and e[2] < hi and lo < e[3]:
                    toks.add(e[5])
        for ap in writes:
            r = self._region(ap)
            if r is None:
                continue
            name, p0, p1, lo, hi = r
            for e in self.trk[name]:
                if e[0] < p1 and p0 < e[1] and e[2] < hi and lo < e[3]:
                    toks.add(e[5])
        return toks

    def _record(self, reads, writes, tok):
        for ap in writes:
            r = self._region(ap)
            if r is None:
                continue
            name, p0, p1, lo, hi = r
            lst = self.trk[name]
            lst[:] = [e for e in lst if not (p0 <= e[0] and e[1] <= p1 and lo <= e[2] and e[3] <= hi)]
            lst.append([p0, p1, lo, hi, True, tok])
        for ap in reads:
            r = self._region(ap)
            if r is None:
                continue
            name, p0, p1, lo, hi = r
            lst = self.trk[name]
            for e in lst:
                if (not e[4]) and e[0] == p0 and e[1] == p1 and e[2] == lo and e[3] == hi and e[5][0] == tok[0]:
                    e[5] = tok
                    break
            else:
                lst.append([p0, p1, lo, hi, False, tok])

    def _wait(self, eng, toks):
        best = {}
        for s, v in toks:
            if s.startswith("d_"):
                v = max(v, self.dcnt[s[2:]] * 16)
            if v > best.get(s, 0):
                best[s] = v
        for s, v in best.items():
            if eng == "pe" and s == "c_pe":
                continue
            if self.waited[eng].get(s, 0) >= v:
                continue
            self.E[eng].wait_ge(self.semh[s], v)
            self.waited[eng][s] = v
            self.n_ins += 1

    def op(self, eng, fn, reads=(), writes=()):
        toks = self._deps(reads, writes)
        self._wait(eng, toks)
        ins = fn()
        self.cnt[eng] += 1
        tok = ("c_" + eng, self.cnt[eng])
        ins.then_inc(self.csem[eng], 1)
        self._record(reads, writes, tok)
        self.n_ins += 1
        return tok

    def dma(self, q, out, in_, sem):
        self.dma_sem(sem)
        toks = self._deps([in_], [out])
        self._wait(q, toks)
        ins = self.E[q].dma_start(out=out, in_=in_)
        self.dcnt[sem] += 1
        tok = ("d_" + sem, self.dcnt[sem] * 16)
        ins.then_inc(self.dsem[sem], 16)
        self._record([in_], [out], tok)
        self.n_ins += 1
        return tok

    def wait_all_dma(self, eng):
        toks = [("d_" + s, c * 16) for s, c in self.dcnt.items() if c > 0]
        self._wait(eng, toks)

    def mm(self, out, lhsT, rhs, start=True, stop=True, extra_reads=()):
        return self.op("pe", lambda: self.nc.tensor.matmul(out, lhsT, rhs, start=start, stop=stop,
                                                          skip_group_check=True),
                       reads=[lhsT, rhs, *extra_reads], writes=[out])

    def tr(self, out, in_, ident):
        return self.op("pe", lambda: self.nc.tensor.transpose(out, in_, ident),
                       reads=[in_, ident], writes=[out])

    def act(self, out, in_, func, bias=None, scale=1.0, eng="act"):
        reads = [in_]
        kw = {}
        if bias is not None:
            kw["bias"] = bias
            if not isinstance(bias, (int, float)):
                reads.append(bias)
        if not isinstance(scale, (int, float)):
            reads.append(scale)
        return self.op("act", lambda: self.nc.scalar.activation(out=out, in_=in_, func=func, scale=scale, **kw),
                       reads=reads, writes=[out])

    def copy(self, eng, out, in_):
        if eng == "act":
            return self.op("act", lambda: self.nc.scalar.copy(out=out, in_=in_), reads=[in_], writes=[out])
        return self.op(eng, lambda: self.E[eng].tensor_copy(out=out, in_=in_), reads=[in_], writes=[out])

    def tt(self, eng, out, in0, in1, op):
        return self.op(eng, lambda: self.E[eng].tensor_tensor(out=out, in0=in0, in1=in1, op=op),
                       reads=[in0, in1], writes=[out])

    def ts(self, eng, out, in0, s1, op0, s2=None, op1=None):
        reads = [in0] + [s for s in (s1, s2) if s is not None and not isinstance(s, (int, float))]
        if op1 is None:
            return self.op(eng, lambda: self.E[eng].tensor_scalar(out=out, in0=in0, scalar1=s1, scalar2=None, op0=op0),
                           reads=reads, writes=[out])
        return self.op(eng, lambda: self.E[eng].tensor_scalar(out=out, in0=in0, scalar1=s1, scalar2=s2, op0=op0, op1=op1),
                       reads=reads, writes=[out])

    def stt(self, eng, out, in0, scalar, in1, op0, op1):
        reads = [in0, in1] + ([] if isinstance(scalar, (int, float)) else [scalar])
        return self.op(eng, lambda: self.E[eng].scalar_tensor_tensor(out=out, in0=in0, scalar=scalar, in1=in1,
                                                                    op0=op0, op1=op1),
                       reads=reads, writes=[out])

    def memset(self, eng, out, val):
        return self.op(eng, lambda: self.E[eng].memset(out, val), reads=[], writes=[out])

    def finish(self):
        self.wait_all_dma("sp")
        self.es.close()


class Common:
    def __init__(self, kb, consts_dram, nbuf=2):
        self.kb = kb
        k = kb
        self.PS = [k.ps(f"ps{i}", [128, 1024]) for i in range(4)]
        self.rr = 0
        self.identf = k.sb("identf", [128, 128], F32)
        self.identb = k.sb("identb", [128, 128], BF16)
        self.onesb = k.sb("onesb", [128, 128], BF16)
        self.onesf = k.sb("onesf", [128, 128], F32)
        self.zcol = k.sb("zcol", [128, 1], F32)
        self.epscol = k.sb("epscol", [128, 1], F32)
        k.dma("sp", self.identf[:], consts_dram["ident"], "const")
        k.dma("pool", self.identb[:], consts_dram["ident"], "constp")
        k.memset("dve", self.onesb[:], 1.0)
        k.memset("dve", self.onesf[:], 1.0)
        k.memset("dve", self.zcol[:], 0.0)
        k.memset("dve", self.epscol[:], EPS)
        self.nbuf = nbuf
        self.sq = [k.sb(f"sq{i}", [128, 8, TB], BF16) for i in range(nbuf)]
        self.lnv = [k.sb(f"lnv{i}", [128, TB], F32) for i in range(nbuf)]
        self.rstd = [k.sb(f"rstd{i}", [128, TB], F32) for i in range(nbuf)]
        self.nrm_i = 0

    def bank(self, b, lo=0, hi=512, p0=0, p1=128):
        return self.PS[b // 2][p0:p1, (b % 2) * 512 + lo:(b % 2) * 512 + hi]

    def next_bank(self, banks=(0, 1, 2, 3, 4, 5, 6, 7)):
        b = banks[self.rr % len(banks)]
        self.rr += 1
        return b

    def rms_stats(self, chunks, w, inv_n, banks=(0, 1, 2, 3, 4, 5, 6, 7), sq_eng="pool"):
        if sq_eng == "act":
            return self._rms_stats_act(chunks, w, inv_n, banks)
        return self._rms_stats(chunks, w, inv_n, banks, sq_eng)

    def _rms_stats_act(self, chunks, w, inv_n, banks):
        k = self.kb
        i = self.nrm_i % self.nbuf
        self.nrm_i += 1
        sq, lnv, rstd = self.sq[i], self.lnv[i], self.rstd[i]
        b = self.next_bank(banks)
        n = len(chunks)
        for c, ap in enumerate(chunks):
            k.act(sq[:, c, 0:w], ap, AF.Square)
        for c in range(n):
            k.mm(self.bank(b, 0, w), self.onesb[:, :], sq[:, c, 0:w], start=(c == 0), stop=(c == n - 1))
        k.act(lnv[:, 0:w], self.bank(b, 0, w), AF.Ln, bias=self.epscol[:, 0:1], scale=inv_n)
        k.act(rstd[:, 0:w], lnv[:, 0:w], AF.Exp, scale=-0.5)
        return rstd[:, 0:w]

    def _rms_stats(self, chunks, w, inv_n, banks=(0, 1, 2, 3, 4, 5, 6, 7), sq_eng="pool"):
        k = self.kb
        i = self.nrm_i % self.nbuf
        self.nrm_i += 1
        sq, lnv, rstd = self.sq[i], self.lnv[i], self.rstd[i]
        b = self.next_bank(banks)
        n = len(chunks)
        for c, ap in enumerate(chunks):
            k.tt(sq_eng, sq[:, c, 0:w], ap, ap, ALU.mult)
        for c in range(n):
            k.mm(self.bank(b, 0, w), self.onesb[:, :], sq[:, c, 0:w], start=(c == 0), stop=(c == n - 1))
        k.act(lnv[:, 0:w], self.bank(b, 0, w), AF.Ln, bias=self.epscol[:, 0:1], scale=inv_n)
        k.act(rstd[:, 0:w], lnv[:, 0:w], AF.Exp, scale=-0.5)
        return rstd[:, 0:w]

    def load_xT_block(self, x_rows, dstT, t0, ntiles, stage, sem_prefix, dst_chunks=8):
        k = self.kb
        for i in range(ntiles):
            st = stage[i % len(stage)]
            k.dma("sp", st[:], x_rows[i * 128:(i + 1) * 128, :], f"{sem_prefix}{i % len(stage)}")
            for h in range(2):
                b = self.next_bank()
                for cc in range(4):
                    c = h * 4 + cc
                    k.tr(self.bank(b, cc * 128, cc * 128 + 128), st[:, c * 128:(c + 1) * 128], self.identf[:])
                src = self.bank(b).rearrange("p (c t) -> p c t", c=4)
                dst = dstT[:, h * 4:h * 4 + 4, t0 + i * 128:t0 + (i + 1) * 128]
                k.copy("act" if (i + h) % 2 == 0 else "dve", dst, src)

    def store_xT_block(self, srcT, t0, ntiles, out_rows, stage, sem_prefix):
        k = self.kb
        for i in range(ntiles):
            st = stage[i % len(stage)]
            for h in range(2):
                b = self.next_bank()
                for cc in range(4):
                    c = h * 4 + cc
                    k.tr(self.bank(b, cc * 128, cc * 128 + 128), srcT[:, c, t0 + i * 128:t0 + (i + 1) * 128],
                         self.identf[:])
                k.copy("act" if (i + h) % 2 == 0 else "dve", st[:, h * 512:(h + 1) * 512], self.bank(b))
            k.dma("sp", out_rows[i * 128:(i + 1) * 128, :], st[:], f"{sem_prefix}{i % len(stage)}")

    def norm_block(self, xT, t0, w, gcols, outT, o0, out_dtype_bf16=True):
        k = self.kb
        rstd = self.rms_stats([xT[:, c, t0:t0 + w] for c in range(8)], w, 1.0 / D)
        for c in range(8):
            k.stt("dve", outT[:, c, o0:o0 + w], xT[:, c, t0:t0 + w], gcols[:, c:c + 1], rstd, ALU.mult, ALU.mult)


class Attn:
    def __init__(self, cm, npt=4):
        k = cm.kb
        self.cm = cm
        self.PT = [k.sb(f"pt{i}", [128, 2 * TB], BF16) for i in range(3)]
        self.rsum2 = [k.sb("rsumE", [128, TB], F32), k.sb("rsumO", [128, TB], F32)]
        k.memset("dve", self.rsum2[0][:], 0.0)
        k.memset("dve", self.rsum2[1][:], 0.0)
        self.bc = k.sb("bcs", [128, TB], F32)
        self.pi = 0
        self.si = 0
        self.oi = 0

    def unit(self, tiles, qfn, scale, par, dst, extra_sum=None):
        cm, k = self.cm, self.cm.kb
        nc = k.nc
        ob = 4 + (self.oi % 2)
        self.oi += 1
        n = len(tiles)
        LOOK = 2
        pts = {}

        def full(t):
            return t["c0"] == 0 and t["c1"] == TB and not t.get("masks")

        def same(a, b):
            return a.tensor.name == b.tensor.name and int(a.offset) == int(b.offset)

        steps = []
        i = 0
        while i < n:
            t = tiles[i]
            if i + 1 < n and full(t) and full(tiles[i + 1]) and same(t["bias"], tiles[i + 1]["bias"]):
                steps.append([i, i + 1])
                i += 2
            else:
                steps.append([i])
                i += 1

        def s_stage(si_):
            step = steps[si_]
            pi_ = self.si % 2
            self.si += 1
            for idx, ti in enumerate(step):
                t = tiles[ti]
                c0, c1 = t["c0"], t["c1"]
                sb_ = 2 * pi_ + idx
                masks = t.get("masks", ())
                k.mm(cm.bank(sb_, c0, c1), t["K"], qfn(c0, c1), start=True, stop=(len(masks) == 0))
                for mi, (m0, m1, mrhs) in enumerate(masks):
                    k.mm(cm.bank(sb_, m0, m1), cm.identb[:, :], mrhs, start=False, stop=(mi == len(masks) - 1))
            pt = self.PT[self.pi % len(self.PT)]
            self.pi += 1
            t0_ = tiles[step[0]]
            if len(step) == 2:
                k.act(pt[:, 0:2 * TB], cm.PS[pi_][:, 0:2 * TB], AF.Exp, bias=t0_["bias"], scale=scale)
            else:
                c0, c1 = t0_["c0"], t0_["c1"]
                k.act(pt[:, c0:c1], cm.bank(2 * pi_, c0, c1), AF.Exp, bias=t0_["bias"], scale=scale)
            pts[si_] = pt

        def pv_stage(si_):
            step = steps[si_]
            for idx, ti in enumerate(step):
                t = tiles[ti]
                c0, c1 = t["c0"], t["c1"]
                k.mm(cm.bank(ob, c0, c1), t["V"], pts[si_][:, idx * TB + c0:idx * TB + c1], start=(ti == 0), stop=(ti == n - 1))

        ns = len(steps)
        for i in range(ns + LOOK):
            if i < ns:
                s_stage(i)
            if i - LOOK >= 0:
                pv_stage(i - LOOK)
        sr = 64 if par == 0 else 0
        r0 = 0 if par == 0 else 64
        rsb = self.rsum2[par]
        rs = rsb[sr:sr + 1, :]
        src = cm.bank(ob, 0, TB, sr, sr + 1)
        k.act(rs, src, AF.Ln, bias=(extra_sum if extra_sum is not None else cm.zcol[sr:sr + 1, 0:1]), scale=1.0)
        k.act(rs, rs, AF.Exp, scale=-1.0)
        k.mm(cm.bank(6), cm.onesf[:, :], rsb[:, :], start=True, stop=True)
        k.copy("dve", self.bc[r0:r0 + 64, :], cm.bank(6, 0, TB, r0, r0 + 64))
        k.tt("dve", dst, cm.bank(ob, 0, TB, r0, r0 + 64), self.bc[r0:r0 + 64, :], ALU.mult)


PROJ_BANKS = (0, 1, 2, 3, 7)


def proj_chunk(cm, W, c0, M, HT, h0, w, dst, eng, nk=8):
    k = cm.kb
    b = cm.next_bank(PROJ_BANKS)
    for kc in range(nk):
        k.mm(cm.bank(b, 0, w, 0, M), W[:, kc, c0:c0 + M], HT[:, kc, h0:h0 + w], start=(kc == 0), stop=(kc == nk - 1))
    if dst is not None:
        k.copy(eng, dst, cm.bank(b, 0, w, 0, M))
    return b


def mem_setup(cm, mem_in, gt_mem, Wm, stage):
    k = cm.kb
    memT = k.sb("memT", [128, 8, N_MEM], F32)
    MEMN = k.sb("memn", [128, 8, N_MEM], BF16)
    Kmem = k.sb("kmem", [128, 2, N_MEM], BF16)
    Vmem = k.sb("vmem", [128, 2, 4, 192], BF16)
    cm.load_xT_block(mem_in, memT, 0, 2, stage, "xs")
    cm.norm_block(memT, 0, N_MEM, gt_mem, MEMN, 0)
    k.memset("pool", Vmem[:], 1.0)
    for pr in range(2):
        proj_chunk(cm, Wm, pr * 128, 128, MEMN, 0, N_MEM, Kmem[:, pr, :], "act")
    for t in range(2):
        b = cm.next_bank(PROJ_BANKS)
        for kc in range(8):
            k.mm(cm.bank(b, 0, 256), MEMN[:, kc, t * 128:(t + 1) * 128], Wm[:, kc, 256:512], start=(kc == 0), stop=(kc == 7))
        k.copy("dve", Vmem[:, t, :, 64:128], cm.bank(b, 0, 256).rearrange("p (h d) -> p h d", h=4))
    return Kmem, Vmem


def cross_units(cm, at, Kmem, Vmem, QC, q0, AO, a0):
    for ch in range(4):
        par = ch % 2
        r0 = par * 64
        tiles = []
        for t in range(2):
            V = Vmem[:, t, ch, 64:192] if par == 0 else Vmem[:, t, ch, 0:128]
            tiles.append(dict(K=Kmem[:, ch // 2, t * 128:(t + 1) * 128], V=V, c0=0, c1=TB,
                              bias=cm.zcol[:, 0:1]))
        at.unit(tiles, lambda c0, c1, ch=ch: QC[:, ch, q0 + c0:q0 + c1], 0.125, par,
                AO[r0:r0 + 64, 6 + ch // 2, a0:a0 + TB])


def out_proj_block(cm, Wo, AO, a0, X, x0):
    k = cm.kb
    for co in range(8):
        b = cm.next_bank(PROJ_BANKS)
        for kc in range(8):
            k.mm(cm.bank(b), Wo[:, kc, co * 128:(co + 1) * 128], AO[:, kc, a0:a0 + TB], start=(kc == 0), stop=(kc == 7))
        k.tt("dve", X[:, co, x0:x0 + TB], X[:, co, x0:x0 + TB], cm.bank(b), ALU.add)


SWA_POS = [0, 3, 1, 4, 2, 5, 6, 9, 7, 10, 8, 11]


def alibi_tables():
    slopes = 2.0 ** (-8.0 * (np.arange(12, dtype=np.float64) + 1.0) / 12)
    kk = np.arange(128)[:, None]
    c = np.arange(256)[None, :]
    d = (c - kk).astype(np.float64)
    valid = (d >= 0) & (d < 128)
    out = np.zeros((128, 12, 256), np.float64)
    for p in range(12):
        out[:, p, :] = np.where(valid, -8.0 * slopes[SWA_POS[p]] * d, 8.0 * NEGB)
    import ml_dtypes
    hi = out.astype(np.float32).astype(ml_dtypes.bfloat16).astype(np.float32)
    lo = (out - hi).astype(np.float32).astype(ml_dtypes.bfloat16).astype(np.float32)
    return hi, lo


MLA_SCALE = 96.0 ** -0.5


class Ctx:
    def __init__(self, k, cm, at):
        self.k, self.cm, self.at = k, cm, at
        self.uid = 0

    def names(self):
        self.uid += 1
        u = self.uid
        return lambda n: f"{n}_{u}"


def mem_kv(ctx, U, MEMN, Wm, Kmem=None, Vmem=None):
    k, cm = ctx.k, ctx.cm
    if Kmem is None:
        Kmem = k.sb(U("kmem"), [128, 2, N_MEM], BF16)
        Vmem = k.sb(U("vmem"), [128, 2, 4, 192], BF16)
    k.memset("pool", Vmem[:], 1.0)
    for pr in range(2):
        proj_chunk(cm, Wm, pr * 128, 128, MEMN, 0, N_MEM, Kmem[:, pr, :], "act")
    for t in range(2):
        b = cm.next_bank(PROJ_BANKS)
        for kc in range(8):
            k.mm(cm.bank(b, 0, 256), MEMN[:, kc, t * 128:(t + 1) * 128], Wm[:, kc, 256:512], start=(kc == 0), stop=(kc == 7))
        k.copy("dve", Vmem[:, t, :, 64:128], cm.bank(b, 0, 256).rearrange("p (h d) -> p h d", h=4))
    return Kmem, Vmem


def emit_mlp(ctx, x_src, nb, out, gcol, up_in, dn_in, final_gcol=None):
    k, cm = ctx.k, ctx.cm
    U = ctx.names()
    k.push_scope()
    upv = up_in.rearrange("(kc p) f -> p kc f", p=128)
    dnv = dn_in.rearrange("(fc p) d -> p fc d", p=128)
    wup = [k.sb(U(f"wup{i}"), [128, 8, 512], BF16) for i in range(2)]
    wdn = [k.sb(U(f"wdn{i}"), [128, 4, D], BF16) for i in range(2)]
    X = k.sb(U("xT"), [128, 8, nb * TB], F32)
    H = k.sb(U("hT"), [128, 8, nb * TB], BF16)
    stage = [k.sb(U(f"stg{i}"), [128, D], F32) for i in range(2)]
    ostage = [k.sb(U("ostg"), [128, D], F32)]
    rl = [k.sb(U(f"rl{i}"), [128, TB], F32) for i in range(2)]
    aT = [k.sb(U(f"aT{i}"), [128, 4, TB], BF16) for i in range(2)]
    ri = 0

    def wload(fb):
        k.dma("pool", wup[fb % 2][:], upv[:, :, fb * 512:(fb + 1) * 512], f"wup{fb % 2}")
        k.dma("pool", wdn[fb % 2][:], dnv[:, fb * 4:(fb + 1) * 4, :], f"wdn{fb % 2}")

    def up_stage(fb, j, A):
        nonlocal ri
        for q in range(4):
            b = cm.next_bank((0, 1, 2, 3))
            for kc in range(8):
                k.mm(cm.bank(b), wup[fb % 2][:, kc, q * 128:(q + 1) * 128], H[:, kc, j * TB:(j + 1) * TB],
                     start=(kc == 0), stop=(kc == 7))
            r = rl[ri % 2]
            ri += 1
            k.act(r[:], cm.bank(b), AF.Relu)
            k.tt("pool", A[:, q, :], r[:], r[:], ALU.mult)

    def down_stage(fb, j, A):
        for c in range(8):
            b = cm.next_bank((4, 5, 6, 7))
            for q in range(4):
                k.mm(cm.bank(b), wdn[fb % 2][:, q, c * 128:(c + 1) * 128], A[:, q, :], start=(q == 0), stop=(q == 3))
            k.tt("dve", X[:, c, j * TB:(j + 1) * TB], X[:, c, j * TB:(j + 1) * TB], cm.bank(b), ALU.add)
        if fb == 7:
            if final_gcol is not None:
                rstd = cm.rms_stats([X[:, c, j * TB:(j + 1) * TB] for c in range(8)], TB, 1.0 / D)
                for c in range(8):
                    k.stt("dve", X[:, c, j * TB:(j + 1) * TB], X[:, c, j * TB:(j + 1) * TB], final_gcol[:, c:c + 1], rstd,
                          ALU.mult, ALU.mult)
            cm.store_xT_block(X, j * TB, 4, out[j * TB:(j + 1) * TB, :], ostage, "os")

    wload(0)
    wload(1)
    seq = [(fb, j) for fb in range(8) for j in range(nb)]
    for i in range(len(seq) + 1):
        if i < len(seq):
            fb, j = seq[i]
            if fb == 0:
                cm.load_xT_block(x_src[j * TB:(j + 1) * TB, :], X, j * TB, 4, stage, "xs")
                cm.norm_block(X, j * TB, TB, gcol, H, j * TB)
            up_stage(fb, j, aT[i % 2])
        if i >= 1:
            fbp, jp = seq[i - 1]
            down_stage(fbp, jp, aT[(i - 1) % 2])
            if jp == nb - 1 and fbp + 2 < 8:
                wload(fbp + 2)
    k.pop_scope()


def emit_swa(ctx, x_src, nb, bnd_src, bnd_bias, out, gcol, MEMN, wi_in, wm_in, wo_in, sk, bmh_in, bml_in):
    k, cm, at = ctx.k, ctx.cm, ctx.at
    U = ctx.names()
    k.push_scope()
    Wi = k.sb(U("wi"), [128, 8, 1536], BF16)
    Wm = k.sb(U("wm"), [128, 8, 512], BF16)
    Wo = k.sb(U("wo"), [128, 8, D], BF16)
    BMh = k.sb(U("bmh"), [128, 12, 256], BF16)
    BMl = k.sb(U("bml"), [128, 12, 256], BF16)
    k.dma("pool", Wm[:], wm_in.rearrange("(kc p) f -> p kc f", p=128), "w0")
    k.dma("pool", Wi[:], wi_in.rearrange("(kc p) f -> p kc f", p=128), "w1")
    k.dma("pool", BMh[:], bmh_in, "w2")
    k.dma("pool", BMl[:], bml_in, "w3")
    k.dma("pool", Wo[:], wo_in.rearrange("(kc p) f -> p kc f", p=128), "w4")
    stage = [k.sb(U(f"stg{i}"), [128, D], F32) for i in range(2)]
    ostage = [k.sb(U("ostg"), [128, D], F32)]
    Kmem, Vmem = mem_kv(ctx, U, MEMN, Wm)
    nt = nb * TB
    KS = k.sb(U("ks"), [128, 2, nt + 128], BF16)
    VS = k.sb(U("vs"), [128, 4 * nb + 1, 4, 192], BF16)
    QS = k.sb(U("qs"), [128, 12, TB], BF16)
    k.memset("pool", QS[:], 0.0)
    QC = k.sb(U("qc"), [128, 4, TB], BF16)
    k.memset("pool", QC[:], 0.0)
    AO = k.sb(U("ao"), [128, 8, TB], BF16)
    Xs = [k.sb(U("xblk0"), [128, 8, TB], F32), k.sb(U("xblk1"), [128, 8, TB], F32)]
    HT = k.sb(U("ht"), [128, 8, TB], BF16)
    XB = Xs[1]
    k.memset("pool", VS[:], 1.0)
    if bnd_src is not None:
        cm.load_xT_block(bnd_src, XB, 0, 1, stage, "xs")
        cm.norm_block(XB, 0, 128, gcol, HT, 0)
        for c in range(2):
            proj_chunk(cm, Wi, 768 + c * 128, 128, HT, 0, 128, KS[:, c, 0:128], "act")
        b = cm.next_bank(PROJ_BANKS)
        for kc in range(8):
            k.mm(cm.bank(b, 0, 256), HT[:, kc, 0:128], Wi[:, kc, 1024:1280], start=(kc == 0), stop=(kc == 7))
        k.copy("dve", VS[:, 0, :, 64:128], cm.bank(b, 0, 256).rearrange("p (h d) -> p h d", h=4))
    else:
        k.memset("pool", KS[:, :, 0:128], 0.0)
    def load_norm(j):
        cm.load_xT_block(x_src[j * TB:(j + 1) * TB, :], Xs[j % 2], 0, 4, stage, "xs")
        cm.norm_block(Xs[j % 2], 0, TB, gcol, HT, 0)

    load_norm(0)
    for j in range(nb):
        X = Xs[j % 2]
        for c in range(6):
            b = proj_chunk(cm, Wi, c * 128, 128, HT, 0, TB, None, "dve")
            k.copy("act", QS[0:64, 2 * c, :], cm.bank(b, 0, TB, 0, 64))
            k.copy("dve", QS[64:128, 2 * c + 1, :], cm.bank(b, 0, TB, 64, 128))
        for c in range(2):
            proj_chunk(cm, Wi, 768 + c * 128, 128, HT, 0, TB, KS[:, c, 128 + j * TB:128 + (j + 1) * TB], "act")
        for c in range(2):
            b = proj_chunk(cm, Wi, 1280 + c * 128, 128, HT, 0, TB, None, "dve")
            k.copy("dve", QC[0:64, 2 * c, :], cm.bank(b, 0, TB, 0, 64))
            k.copy("dve", QC[64:128, 2 * c + 1, :], cm.bank(b, 0, TB, 64, 128))
        for t in range(4):
            b = cm.next_bank(PROJ_BANKS)
            for kc in range(8):
                k.mm(cm.bank(b, 0, 256), HT[:, kc, t * 128:(t + 1) * 128], Wi[:, kc, 1024:1280], start=(kc == 0), stop=(kc == 7))
            k.copy("dve" if t % 2 else "act", VS[:, 1 + 4 * j + t, :, 64:128],
                   cm.bank(b, 0, 256).rearrange("p (h d) -> p h d", h=4))
        if j + 1 < nb:
            load_norm(j + 1)
        cross_units(cm, at, Kmem, Vmem, QC, 0, AO, 0)
        for p in range(12):
            kh = SWA_POS[p] // 3
            par = p % 2
            r0 = par * 64
            tiles = []
            for i in range(5):
                s = 4 * j + i
                c0 = max(0, (i - 1) * 128)
                c1 = min(TB, (i + 1) * 128)
                m0 = 128 if i == 0 else 0
                V = VS[:, s, kh, 64:192] if par == 0 else VS[:, s, kh, 0:128]
                tiles.append(dict(K=KS[:, kh // 2, s * 128:(s + 1) * 128], V=V, c0=c0, c1=c1,
                                  bias=(bnd_bias if s == 0 else cm.zcol[:, 0:1]),
                                  masks=[(c0, c1, BMh[:, p, m0:m0 + (c1 - c0)]), (c0, c1, BMl[:, p, m0:m0 + (c1 - c0)])]))
            sr = 64 if par == 0 else 0
            at.unit(tiles, lambda c0, c1, p=p: QS[:, p, c0:c1], 0.125, par,
                    AO[r0:r0 + 64, p // 2, :], extra_sum=sk[sr:sr + 1, p:p + 1])
        out_proj_block(cm, Wo, AO, 0, X, 0)
        cm.store_xT_block(X, 0, 4, out[j * TB:(j + 1) * TB, :], ostage, "os")
    k.pop_scope()


def emit_mla(ctx, x_src, nb, prev_src, npb, pos_own, pos_prev, prev_bias, out, g_attn, g_q, g_kv, MEMN,
             rc, tri, wi_in, wuq_in, wukv_in, wm_in, wo_in):
    k, cm, at = ctx.k, ctx.cm, ctx.at
    U = ctx.names()
    nt = nb * TB
    npv = npb * TB
    nkey = nt + npv
    nkt = nkey // 128
    k.push_scope()
    Wi = k.sb(U("wi"), [128, 8, 1088], BF16)
    Wukv = k.sb(U("wukv"), [128, 2, 1536], BF16)
    CKVN = k.sb(U("ckvn"), [128, 2, nkey], BF16)
    KTs = [k.sb(U("kt0"), [128, nkey], BF16), k.sb(U("kt1"), [128, nkey], BF16)]
    CQN = k.sb(U("cqn"), [128, 3, nt], BF16)
    CC = k.sb(U("cc"), [128, nt], F32)
    SS = k.sb(U("ss"), [128, nt], F32)
    AO = k.sb(U("ao"), [128, 8, nt], BF16)
    rt1 = k.sb(U("rt1"), [128, TB], F32)
    rt2 = k.sb(U("rt2"), [128, TB], F32)
    k.dma("pool", Wi[:], wi_in.rearrange("(kc p) f -> p kc f", p=128), "w1")
    k.dma("pool", Wukv[:], wukv_in.rearrange("(kc p) f -> p kc f", p=128), "w2")

    def rope(dsts, A, B, cc, ss):
        k.tt("dve", rt1[64:96, :], A, cc, ALU.mult)
        k.tt("dve", rt2[64:96, :], B, ss, ALU.mult)
        for dst in dsts:
            k.tt("pool", dst, rt1[64:96, :], rt2[64:96, :], ALU.add)

    k.push_scope()
    X = k.sb(U("xblk"), [128, 8, TB], F32)
    HT = k.sb(U("ht"), [128, 8, TB], BF16)
    stage = [k.sb(U(f"stg{i}"), [128, D], F32) for i in range(2)]
    QC = k.sb(U("qc"), [128, 4, TB], BF16)
    k.memset("pool", QC[:], 0.0)
    Kmem = k.sb(U("kmem"), [128, 2, N_MEM], BF16)
    Vmem = k.sb(U("vmem"), [128, 2, 4, 192], BF16)
    k.push_scope()
    Wm = k.sb(U("wm"), [128, 8, 512], BF16)
    k.dma("pool", Wm[:], wm_in.rearrange("(kc p) f -> p kc f", p=128), "w0")
    mem_kv(ctx, U, MEMN, Wm, Kmem, Vmem)
    k.pop_scope()
    posi = k.sb(U("posi"), [128, TB], I32)
    yv = k.sb(U("yv"), [128, TB], F32)
    yi = k.sb(U("yi"), [128, TB], I32)
    yf = k.sb(U("yf"), [128, TB], F32)
    fr = k.sb(U("fr"), [128, TB], F32)
    cmp_ = k.sb(U("cmp"), [128, TB], F32)
    CCt = k.sb(U("cct"), [128, TB], F32)
    SSt = k.sb(U("sst"), [128, TB], F32)

    def frac_sin(dst, add, scale_ap):
        src = yv
        if add != 0.0:
            k.ts("dve", yf[:], yv[:], add, ALU.add)
            src = yf
            k.copy("dve", yi[:], yf[:])
        else:
            k.copy("dve", yi[:], yv[:])
        k.copy("dve", fr[:], yi[:])
        k.tt("dve", fr[:], src[:], fr[:], ALU.subtract)
        k.ts("dve", cmp_[:], fr[:], 0.5, ALU.is_gt)
        k.tt("dve", fr[:], fr[:], cmp_[:], ALU.subtract)
        k.ts("dve", cmp_[:], fr[:], -0.5, ALU.is_lt)
        k.tt("dve", fr[:], fr[:], cmp_[:], ALU.add)
        k.act(dst, fr[:], AF.Sin, scale=scale_ap)

    def rope_tables(pos_slice, cc, ss):
        k.dma("sp", posi[:], pos_slice.partition_broadcast(128), "posd")
        k.copy("dve", yv[:], posi[:])
        k.ts("dve", yv[:], yv[:], rc[:, 0:1], ALU.mult)
        frac_sin(ss, 0.0, rc[:, 1:2])
        frac_sin(cc, 0.25, rc[:, 2:3])

    def latents_block(xsrc, t0, koff, pos_src, own):
        cm.load_xT_block(xsrc[t0:t0 + TB, :], X, 0, 4, stage, "xs")
        cm.norm_block(X, 0, TB, g_attn, HT, 0)
        if own:
            cc, ss = CC[:, t0:t0 + TB], SS[:, t0:t0 + TB]
        else:
            cc, ss = CCt[:, :], SSt[:, :]
        rope_tables(pos_src[t0:t0 + TB], cc, ss)
        if own:
            for c in range(3):
                for kc in range(8):
                    k.mm(cm.bank(c), Wi[:, kc, c * 128:(c + 1) * 128], HT[:, kc, :], start=(kc == 0), stop=(kc == 7))
            rstd = cm.rms_stats([cm.bank(c) for c in range(3)], TB, 1.0 / 384, banks=(5,), sq_eng="act")
            for c in range(3):
                k.stt("dve", CQN[:, c, t0:t0 + TB], cm.bank(c), g_q[:, c:c + 1], rstd, ALU.mult, ALU.mult)
        for c in range(2):
            for kc in range(8):
                k.mm(cm.bank(3 + c), Wi[:, kc, 384 + c * 128:384 + (c + 1) * 128], HT[:, kc, :], start=(kc == 0), stop=(kc == 7))
        rstd = cm.rms_stats([cm.bank(3 + c) for c in range(2)], TB, 1.0 / 256, banks=(5,), sq_eng="act")
        for c in range(2):
            k.stt("dve", CKVN[:, c, koff + t0:koff + t0 + TB], cm.bank(3 + c), g_kv[:, c:c + 1], rstd, ALU.mult, ALU.mult)
        for i, b in enumerate((6, 7)):
            for kc in range(8):
                k.mm(cm.bank(b, 0, TB, 0, 96), Wi[:, kc, 896 + i * 96:896 + (i + 1) * 96], HT[:, kc, :], start=(kc == 0), stop=(kc == 7))
        rope([KT_[64:96, koff + t0:koff + t0 + TB] for KT_ in KTs], cm.bank(6, 0, TB, 64, 96), cm.bank(7, 0, TB, 64, 96),
             cc[64:96, :], ss[64:96, :])
        if own:
            for c in range(2):
                b = proj_chunk(cm, Wi, 640 + c * 128, 128, HT, 0, TB, None, "dve")
                k.copy("dve", QC[0:64, 2 * c, :], cm.bank(b, 0, TB, 0, 64))
                k.copy("dve", QC[64:128, 2 * c + 1, :], cm.bank(b, 0, TB, 64, 128))
            cross_units(cm, at, Kmem, Vmem, QC, 0, AO, t0)

    for j in range(npb):
        latents_block(prev_src, j * TB, 0, pos_prev, False)
    for j in range(nb):
        latents_block(x_src, j * TB, npv, pos_own, True)
    k.pop_scope()

    k.push_scope()
    Wo = Wi
    k.dma("pool", Wo[:, :, 0:D], wo_in.rearrange("(kc p) f -> p kc f", p=128), "w3")
    VV = [k.sb(U("ve"), [128, nkt, 128], BF16), k.sb(U("vo"), [128, nkt, 128], BF16)]
    QT = [k.sb(U(f"qt{i}"), [128, nt], BF16) for i in range(2)]
    Wq = [k.sb(U(f"wq{i}"), [128, 3, 192], BF16) for i in range(2)]
    k.memset("pool", VV[0][:], 1.0)
    k.memset("pool", VV[1][:], 1.0)
    wuqv = wuq_in.rearrange("(kc p) f -> p kc f", p=128)
    P2 = (7, 6)
    zb = cm.zcol[:, 0:1]

    def prologue(h):
        par = h % 2
        wq, qt, Vh, KT = Wq[par], QT[par], VV[par], KTs[par]
        voff = 0 if par == 0 else 64
        k.dma("pool", wq[:], wuqv[:, :, h * 192:(h + 1) * 192], f"wq{par}")
        for j in range(nb):
            t0 = j * TB
            bA = cm.next_bank(P2)
            for kc in range(3):
                k.mm(cm.bank(bA, 0, TB, 0, 96), wq[:, kc, 0:96], CQN[:, kc, t0:t0 + TB], start=(kc == 0), stop=(kc == 2))
            bB = cm.next_bank(P2)
            for kc in range(3):
                k.mm(cm.bank(bB, 0, TB, 0, 96), wq[:, kc, 96:192], CQN[:, kc, t0:t0 + TB], start=(kc == 0), stop=(kc == 2))
            k.copy("dve", qt[0:64, t0:t0 + TB], cm.bank(bA, 0, TB, 0, 64))
            rope([qt[64:96, t0:t0 + TB]], cm.bank(bA, 0, TB, 64, 96), cm.bank(bB, 0, TB, 64, 96), CC[64:96, t0:t0 + TB], SS[64:96, t0:t0 + TB])
        for kb in range(nkey // TB):
            b = cm.next_bank(P2)
            for kc in range(2):
                k.mm(cm.bank(b), Wukv[:, kc, 128 * h:128 * h + 128], CKVN[:, kc, kb * TB:(kb + 1) * TB], start=(kc == 0), stop=(kc == 1))
            k.copy("dve", KT[0:64, kb * TB:(kb + 1) * TB], cm.bank(b, 0, TB, 0, 64))
        for g in range(nkt // 8):
            b = cm.next_bank(P2)
            for tl in range(8):
                kt_ = g * 8 + tl
                for kc in range(2):
                    k.mm(cm.bank(b, tl * 64, (tl + 1) * 64), CKVN[:, kc, kt_ * 128:(kt_ + 1) * 128],
                         Wukv[:, kc, 128 * h + 64:128 * h + 128], start=(kc == 0), stop=(kc == 1))
            k.copy("dve", Vh[:, g * 8:(g + 1) * 8, voff:voff + 64], cm.bank(b).rearrange("p (t d) -> p t d", t=8))

    def attention(h):
        par = h % 2
        r0 = par * 64
        qt, Vh, KT = QT[par], VV[par], KTs[par]
        for j in range(nb):
            t0 = j * TB
            tiles = []
            for kt_ in range(npb * 4):
                tiles.append(dict(K=KT[0:96, kt_ * 128:(kt_ + 1) * 128], V=Vh[:, kt_, :], c0=0, c1=TB, bias=prev_bias))
            for kt_ in range(npb * 4, npb * 4 + 4 * j):
                tiles.append(dict(K=KT[0:96, kt_ * 128:(kt_ + 1) * 128], V=Vh[:, kt_, :], c0=0, c1=TB, bias=zb))
            for t in range(4):
                kt_ = npb * 4 + 4 * j + t
                tiles.append(dict(K=KT[0:96, kt_ * 128:(kt_ + 1) * 128], V=Vh[:, kt_, :], c0=128 * t, c1=TB,
                                  bias=zb, masks=[(128 * t, 128 * t + 128, tri[:, :])]))
            at.unit(tiles, lambda c0, c1, qt=qt, t0=t0: qt[0:96, t0 + c0:t0 + c1], MLA_SCALE, par,
                    AO[r0:r0 + 64, h // 2, t0:t0 + TB])

    prologue(0)
    for h in range(12):
        if h + 1 < 12:
            prologue(h + 1)
        attention(h)
    k.pop_scope()

    k.push_scope()
    X = k.sb(U("xblk3"), [128, 8, TB], F32)
    stage = [k.sb(U(f"stg3{i}"), [128, D], F32) for i in range(2)]
    ostage = [k.sb(U("ostg3"), [128, D], F32)]
    for j in range(nb):
        cm.load_xT_block(x_src[j * TB:(j + 1) * TB, :], X, 0, 4, stage, "xs3")
        out_proj_block(cm, Wo, AO, j * TB, X, 0)
        cm.store_xT_block(X, 0, 4, out[j * TB:(j + 1) * TB, :], ostage, "os")
    k.pop_scope()
    k.pop_scope()


GCOLS = 96


def build_fused(only=None):
    k = KB()
    nc = k.nc
    x_in = k.dram_in("x", [NT, D])
    xp_in = k.dram_in("xprev", [NT, D])
    mem_in = k.dram_in("mem", [N_MEM, D])
    g_in = k.dram_in("g", [128, GCOLS])
    pos_in = k.dram_in("pos", [NT], I32)
    posp_in = k.dram_in("posprev", [NT], I32)
    rc_in = k.dram_in("ropec", [128, 4])
    tri_in = k.dram_in("trimask", [128, 128])
    pb_in = k.dram_in("prevbias", [128, 1])
    ident_in = k.dram_in("ident", [128, 128])
    sk_in = k.dram_in("sinks", [128, 24])
    bmh_in = k.dram_in("bmhi", [128, 12, 256])
    bml_in = k.dram_in("bmlo", [128, 12, 256])
    W = {}
    for l in range(4):
        W[f"wm{l}"] = k.dram_in(f"w_mem{l}", [D, 512])
        W[f"wo{l}"] = k.dram_in(f"w_o{l}", [D, D])
        W[f"up{l}"] = k.dram_in(f"w_up{l}", [D, DFF])
        W[f"dn{l}"] = k.dram_in(f"w_dn{l}", [DFF, D])
        if l % 2 == 0:
            W[f"wi{l}"] = k.dram_in(f"w_in{l}", [D, 1088])
            W[f"wuq{l}"] = k.dram_in(f"w_uq{l}", [384, 12 * 192])
            W[f"wukv{l}"] = k.dram_in(f"w_ukv{l}", [256, 1536])
        else:
            W[f"wi{l}"] = k.dram_in(f"w_in{l}", [D, 1536])
    y_out = k.dram_out("y", [NT, D])
    scr = {n: nc.dram_tensor("scr_" + n, [NT, D], F32).ap() for n in ("p1a", "p1b", "p1c", "p1d", "p1e", "p2a", "p2b")}
    cm = Common(k, {"ident": ident_in}, nbuf=1)
    at = Attn(cm)
    ctx = Ctx(k, cm, at)
    G = k.sb("G", [128, GCOLS], F32)
    rc = k.sb("rc", [128, 4], F32)
    pb = k.sb("pb", [128, 1], F32)
    negc = k.sb("negc", [128, 1], F32)
    sk = k.sb("sk", [128, 24], F32)
    tri = k.sb("tri", [128, 128], BF16)
    MEMN = k.sb("memn", [128, 8, N_MEM], BF16)
    k.dma("sp", G[:], g_in, "const")
    k.dma("sp", rc[:], rc_in, "const")
    k.dma("sp", pb[:], pb_in, "const")
    k.dma("sp", sk[:], sk_in, "const")
    k.dma("pool", tri[:], tri_in, "constp")
    k.memset("dve", negc[:], NEGB)
    k.act(sk[:], sk[:], AF.Exp)
    k.push_scope()
    mT = k.sb("memT", [128, 8, N_MEM], F32)
    mstage = [k.sb(f"mstg{i}", [128, D], F32) for i in range(2)]
    cm.load_xT_block(mem_in, mT, 0, 2, mstage, "xs")
    cm.norm_block(mT, 0, N_MEM, G[:, 64:72], MEMN, 0)
    k.pop_scope()

    def ga(l):
        return G[:, l * 16:l * 16 + 8]

    def gm(l):
        return G[:, l * 16 + 8:l * 16 + 16]

    def mla(l, x_src, nb, prev_src, npb, pos_own, pos_prev, prev_bias, out):
        j = l // 2
        emit_mla(ctx, x_src, nb, prev_src, npb, pos_own, pos_prev, prev_bias, out, ga(l),
                 G[:, 80 + 5 * j:83 + 5 * j], G[:, 83 + 5 * j:85 + 5 * j], MEMN, rc, tri,
                 W[f"wi{l}"], W[f"wuq{l}"], W[f"wukv{l}"], W[f"wm{l}"], W[f"wo{l}"])

    def swa(l, x_src, nb, bnd_src, bnd_bias, out):
        j = l // 2
        emit_swa(ctx, x_src, nb, bnd_src, bnd_bias, out, ga(l), MEMN, W[f"wi{l}"], W[f"wm{l}"], W[f"wo{l}"],
                 sk[:, 12 * j:12 * j + 12], bmh_in, bml_in)

    def mlp(l, x_src, nb, out, final=False):
        emit_mlp(ctx, x_src, nb, out, gm(l), W[f"up{l}"], W[f"dn{l}"], G[:, 72:80] if final else None)

    z = cm.zcol[:, 0:1]
    if only is not None:
        if only == "mla":
            mla(0, x_in, NB, xp_in, NB, pos_in, posp_in, pb[:, 0:1], y_out)
        elif only == "mlalite":
            mla(0, xp_in, NB, None, 0, posp_in, None, z, y_out)
        elif only == "swa":
            swa(1, x_in, NB, xp_in[NT - 128:NT, :], pb[:, 0:1], y_out)
        elif only == "mlp":
            mlp(0, x_in, NB, y_out)
        k.finish()
        return k
    mla(0, xp_in, NB, None, 0, posp_in, None, z, scr["p1a"])
    mlp(0, scr["p1a"], NB, scr["p1b"])
    swa(1, scr["p1b"], NB, None, negc[:, 0:1], scr["p1a"])
    mlp(1, scr["p1a"], NB, scr["p1c"])
    mla(2, scr["p1c"][3 * TB:4 * TB, :], 1, scr["p1c"][0:3 * TB, :], 3, posp_in[3 * TB:4 * TB], posp_in[0:3 * TB], z,
        scr["p1d"][0:TB, :])
    mlp(2, scr["p1d"][0:TB, :], 1, scr["p1e"][0:TB, :])
    mla(0, x_in, NB, xp_in, NB, pos_in, posp_in, pb[:, 0:1], scr["p2a"])
    mlp(0, scr["p2a"], NB, scr["p2b"])
    swa(1, scr["p2b"], NB, scr["p1b"][NT - 128:NT, :], pb[:, 0:1], scr["p2a"])
    mlp(1, scr["p2a"], NB, scr["p2b"])
    mla(2, scr["p2b"], NB, scr["p1c"], NB, pos_in, posp_in, pb[:, 0:1], scr["p2a"])
    mlp(2, scr["p2a"], NB, scr["p2b"])
    swa(3, scr["p2b"], NB, scr["p1e"][TB - 128:TB, :], pb[:, 0:1], scr["p2a"])
    mlp(3, scr["p2a"], NB, y_out, final=True)
    k.finish()
    return k


N_CORES = 8


def _f(a):
    return np.ascontiguousarray(a, dtype=np.float32)


def kernel(x, mem, positions, attn_norm_g, mlp_norm_g, mem_norm_g, final_norm_g,
           mla_w_in, mla_q_norm_g, mla_kv_norm_g, mla_w_uq, mla_w_ukv,
           swa_w_in, swa_sinks, w_mem_kv, w_o, mlp_w_up, mlp_w_down):
    x = np.asarray(x, dtype=np.float32)
    mem = np.asarray(mem, dtype=np.float32)
    positions = np.asarray(positions)
    B, S, _ = x.shape
    shared = {}
    g = np.zeros((128, GCOLS), np.float32)
    for l in range(4):
        g[:, l * 16:l * 16 + 8] = np.asarray(attn_norm_g[l]).reshape(8, 128).T
        g[:, l * 16 + 8:l * 16 + 16] = np.asarray(mlp_norm_g[l]).reshape(8, 128).T
    g[:, 64:72] = np.asarray(mem_norm_g).reshape(8, 128).T
    g[:, 72:80] = np.asarray(final_norm_g).reshape(8, 128).T
    for j in range(2):
        g[:, 80 + 5 * j:83 + 5 * j] = np.asarray(mla_q_norm_g[j]).reshape(3, 128).T
        g[:, 83 + 5 * j:85 + 5 * j] = np.asarray(mla_kv_norm_g[j]).reshape(2, 128).T
    shared["g"] = g
    r = np.arange(128)
    inv = 10000.0 ** (-(np.arange(16, dtype=np.float64) * 2.0) / 32)
    rcv = np.zeros((128, 4), np.float64)
    rcv[:, 0] = inv[r % 16] / (2 * np.pi)
    rcv[:, 1] = np.where((r % 32) < 16, -1.0, 1.0) * 2 * np.pi
    rcv[:, 2] = 2 * np.pi
    shared["ropec"] = _f(rcv)
    kk = np.arange(128)[:, None]
    cc = np.arange(128)[None, :]
    shared["trimask"] = _f(np.where(kk <= cc, 0.0, 8.0 * NEGB))
    shared["ident"] = np.eye(128, dtype=np.float32)
    sk = np.concatenate([np.asarray(swa_sinks[j])[SWA_POS] for j in range(2)])
    shared["sinks"] = _f(np.broadcast_to(sk[None, :], (128, 24)))
    hi, lo = alibi_tables()
    shared["bmhi"], shared["bmlo"] = _f(hi), _f(lo)
    for l in range(4):
        j = l // 2
        shared[f"w_mem{l}"] = _f(w_mem_kv[l])
        shared[f"w_up{l}"] = _f(mlp_w_up[l])
        shared[f"w_dn{l}"] = _f(mlp_w_down[l])
        if l % 2 == 0:
            w_in = np.asarray(mla_w_in[j])
            kr = w_in[:, 640:672]
            krp = np.concatenate([kr[:, 16:32], kr[:, 0:16]], axis=1)
            pad = w_in[:, 576:640]
            shared[f"w_in{l}"] = _f(np.concatenate([w_in[:, 0:640], w_in[:, 672:928], pad, kr, pad, krp], axis=1))
            w_uq = np.asarray(mla_w_uq[j])
            hq = []
            for h in range(12):
                wh = w_uq[:, h * 96:(h + 1) * 96]
                hq += [wh, wh[:, 0:64], wh[:, 80:96], wh[:, 64:80]]
            shared[f"w_uq{l}"] = _f(np.concatenate(hq, axis=1))
            shared[f"w_ukv{l}"] = _f(mla_w_ukv[j])
            shared[f"w_o{l}"] = _f(w_o[l])
        else:
            w_in = np.asarray(swa_w_in[j])
            wq = np.concatenate([w_in[:, h * 64:(h + 1) * 64] for h in SWA_POS], axis=1)
            shared[f"w_in{l}"] = _f(np.concatenate([wq, w_in[:, 768:]], axis=1))
            wo = np.asarray(w_o[l])
            shared[f"w_o{l}"] = _f(np.concatenate([wo[h * 64:(h + 1) * 64] for h in SWA_POS] + [wo[768:]], axis=0))
    in_maps = []
    for c in range(N_CORES):
        b, half = c // 2, c % 2
        m = dict(shared)
        m["x"] = _f(x[b, half * NT:(half + 1) * NT])
        m["xprev"] = _f(x[b, 0:NT])
        m["mem"] = _f(mem[b])
        m["pos"] = np.ascontiguousarray(positions[b, half * NT:(half + 1) * NT], dtype=np.int32)
        m["posprev"] = np.ascontiguousarray(positions[b, 0:NT], dtype=np.int32)
        m["prevbias"] = np.full((128, 1), 0.0 if half else NEGB, np.float32)
        in_maps.append(m)
    kb = build_fused()
    res = run_bass_kernel_spmd(kb.nc, in_maps, core_ids=list(range(N_CORES)))
    out = np.empty((B, S, D), np.float32)
    for c in range(N_CORES):
        out[c // 2, (c % 2) * NT:(c % 2 + 1) * NT] = np.asarray(res.results[c]["y"])
    return out
```
